# Optimizing a Trainium2 kernel written in Bass

```python
import math
import jax
import jax.numpy as jnp
from jax import lax
import numpy as np

D_MODEL = 1024
BATCH = 8
SEQ = 8192
DEPTH = 2

MIX_A = D_MODEL // 2
RET_HEADS = 4
RET_HEAD_DIM = (D_MODEL - MIX_A) // RET_HEADS
RET_WIDTH = RET_HEADS * RET_HEAD_DIM
IN_COLS = 2 * MIX_A + 4 * RET_WIDTH
CONV_WIDTH = 31
RET_CHUNK = 128
ROPE_BASE = 10000.0
S5_GROUP = 16
S5_GROUPS = D_MODEL // S5_GROUP
S5_STATE = 64
D_FF = 4 * D_MODEL
N_EVEN = (DEPTH + 1) // 2
N_ODD = DEPTH // 2
EPS = 1e-6

kernel_name = "hybrid_conv_retention_s5_adaln_encoder"


def _rmsnorm(x, g):
    xf = x.astype(jnp.float32)
    y = xf * lax.rsqrt(jnp.mean(xf * xf, axis=-1, keepdims=True) + EPS)
    return (y * g.astype(jnp.float32)).astype(x.dtype)


def _head_rmsnorm(x):
    xf = x.astype(jnp.float32)
    y = xf * lax.rsqrt(jnp.mean(xf * xf, axis=-1, keepdims=True) + EPS)
    return y.astype(x.dtype)


def _layernorm(x, g, b):
    xf = x.astype(jnp.float32)
    mu = jnp.mean(xf, axis=-1, keepdims=True)
    xc = xf - mu
    var = jnp.mean(xc * xc, axis=-1, keepdims=True)
    y = xc * lax.rsqrt(var + EPS) * g.astype(jnp.float32) + b.astype(jnp.float32)
    return y.astype(x.dtype)


def _rotary(t, cos, sin):
    half = t.shape[-1] // 2
    t1, t2 = t[..., :half], t[..., half:]
    cos = cos.astype(t.dtype)
    sin = sin.astype(t.dtype)
    return jnp.concatenate([t1 * cos - t2 * sin, t1 * sin + t2 * cos], axis=-1)


def _retention_bidir(q, k, v):
    bsz, seq, nh, dh = q.shape
    L = RET_CHUNK
    n = seq // L
    dt = q.dtype
    q = q.reshape(bsz, n, L, nh, dh)
    k = k.reshape(bsz, n, L, nh, dh)
    v = v.reshape(bsz, n, L, nh, dh)
    log_g = jnp.log1p(-jnp.exp2(-5.0 - jnp.arange(nh, dtype=jnp.float32)))
    idx = jnp.arange(L, dtype=jnp.float32)
    dist = jnp.abs(idx[:, None] - idx[None, :])
    dmat = jnp.exp(log_g[:, None, None] * dist).astype(dt)
    scores = jnp.einsum('bnihd,bnjhd->bnhij', q, k) * dmat[None, None]
    intra = jnp.einsum('bnhij,bnjhe->bnihe', scores, v)
    w_kf = jnp.exp(log_g[None, :] * (L - 1.0 - idx)[:, None]).astype(dt)
    w_kb = jnp.exp(log_g[None, :] * idx[:, None]).astype(dt)
    kv_f = jnp.einsum('bnjhd,jh,bnjhe->nbhde', k, w_kf, v)
    kv_b = jnp.einsum('bnjhd,jh,bnjhe->nbhde', k, w_kb, v)
    g_L = jnp.exp(log_g * L).astype(dt)[None, :, None, None]

    def step(state, kv):
        return g_L * state + kv, state

    init = jnp.zeros((bsz, nh, dh, dh), dtype=kv_f.dtype)
    _, s_f = lax.scan(step, init, kv_f)
    _, s_b = lax.scan(step, init, kv_b, reverse=True)
    w_qf = jnp.exp(log_g[None, :] * (idx + 1.0)[:, None]).astype(dt)
    w_qb = jnp.exp(log_g[None, :] * (L - idx)[:, None]).astype(dt)
    inter = (jnp.einsum('bnihd,ih,nbhde->bnihe', q, w_qf, s_f)
             + jnp.einsum('bnihd,ih,nbhde->bnihe', q, w_qb, s_b))
    return (intra + inter).reshape(bsz, seq, nh, dh)


def _mixer_conv_retention(h, w_in, conv_w, conv_b, cln_g, cln_b, w_out):
    bsz, seq, _ = h.shape
    proj = h @ w_in
    splits = [MIX_A, 2 * MIX_A, 2 * MIX_A + RET_WIDTH,
              2 * MIX_A + 2 * RET_WIDTH, 2 * MIX_A + 3 * RET_WIDTH]
    a_val, a_gate, q, k, v, g = jnp.split(proj, splits, axis=-1)
    a = a_val * jax.nn.sigmoid(a_gate)
    a = lax.conv_general_dilated(
        a, conv_w[:, None, :].astype(a.dtype), window_strides=(1,), padding='SAME',
        dimension_numbers=('NWC', 'WIO', 'NWC'), feature_group_count=MIX_A)
    a = jax.nn.silu(_layernorm(a + conv_b, cln_g, cln_b))
    q = q.reshape(bsz, seq, RET_HEADS, RET_HEAD_DIM)
    k = k.reshape(bsz, seq, RET_HEADS, RET_HEAD_DIM)
    v = v.reshape(bsz, seq, RET_HEADS, RET_HEAD_DIM)
    inv = ROPE_BASE ** (-jnp.arange(0, RET_HEAD_DIM, 2, dtype=jnp.float32) / RET_HEAD_DIM)
    ang = jnp.arange(seq, dtype=jnp.float32)[:, None] * inv[None, :]
    cos, sin = jnp.cos(ang)[:, None, :], jnp.sin(ang)[:, None, :]
    q = _rotary(q, cos, sin) * (RET_HEAD_DIM ** -0.5)
    k = _rotary(k, cos, sin)
    r = _head_rmsnorm(_retention_bidir(q, k, v)).reshape(bsz, seq, RET_WIDTH)
    r = jax.nn.silu(g) * r
    return jnp.concatenate([a, r], axis=-1) @ w_out


def _s5_combine(e1, e2):
    a1r, a1i, b1r, b1i = e1
    a2r, a2i, b2r, b2i = e2
    ar = a2r * a1r - a2i * a1i
    ai = a2r * a1i + a2i * a1r
    br = a2r * b1r - a2i * b1i + b2r
    bi = a2r * b1i + a2i * b1r + b2i
    return ar, ai, br, bi


def _mixer_s5(h, lam_re, lam_im, log_step, b_re, b_im, c_re, c_im, d_skip, w_glu_a, w_glu_b):
    bsz, seq, dm = h.shape
    f32 = jnp.float32
    u = h.astype(f32)
    ug = u.reshape(bsz, seq, S5_GROUPS, S5_GROUP)
    lr = jnp.minimum(lam_re.astype(f32), -1e-4)
    li = lam_im.astype(f32)
    dt = jnp.exp(log_step.astype(f32))[..., None]
    mag = jnp.exp(lr * dt)
    ab_re, ab_im = mag * jnp.cos(li * dt), mag * jnp.sin(li * dt)
    den = lr * lr + li * li
    nr, ni = ab_re - 1.0, ab_im
    f_re = (nr * lr + ni * li) / den
    f_im = (ni * lr - nr * li) / den
    br_, bi_ = b_re.astype(f32), b_im.astype(f32)
    bb_re = f_re[..., None] * br_ - f_im[..., None] * bi_
    bb_im = f_re[..., None] * bi_ + f_im[..., None] * br_
    cr, ci = c_re.astype(f32), c_im.astype(f32)

    def one_sequence(us):
        y = jnp.zeros_like(us)
        for direction in (0, 1):
            bu_re = jnp.einsum('sgc,gpc->sgp', us, bb_re[direction])
            bu_im = jnp.einsum('sgc,gpc->sgp', us, bb_im[direction])
            a_re = jnp.broadcast_to(ab_re[direction], bu_re.shape)
            a_im = jnp.broadcast_to(ab_im[direction], bu_re.shape)
            _, _, h_re, h_im = lax.associative_scan(
                _s5_combine, (a_re, a_im, bu_re, bu_im), reverse=(direction == 1), axis=0)
            y = y + (jnp.einsum('sgp,gcp->sgc', h_re, cr[direction])
                     - jnp.einsum('sgp,gcp->sgc', h_im, ci[direction]))
        return y

    y = lax.map(one_sequence, ug).reshape(bsz, seq, dm)
    y = (y + d_skip.astype(f32) * u).astype(h.dtype)
    z = jax.nn.gelu(y)
    return (z @ w_glu_a) * jax.nn.sigmoid(z @ w_glu_b)


def setup_inputs(seed: int = 0) -> dict:
    key = jax.random.key(seed)
    ks = jax.random.split(key, 32)
    f32 = jnp.float32

    def nrm(k, shape, scale):
        return jax.random.normal(k, shape, f32) * scale

    P, G, Cg = S5_STATE, S5_GROUPS, S5_GROUP
    return {
        "x": nrm(ks[0], (BATCH, SEQ, D_MODEL), 1.0),
        "c": nrm(ks[1], (BATCH, D_MODEL), 1.0),
        "norm_g": 1.0 + nrm(ks[2], (DEPTH, 2, D_MODEL), 0.05),
        "ada_w": nrm(ks[3], (DEPTH, D_MODEL, 6 * D_MODEL), 0.5 * D_MODEL ** -0.5),
        "ada_b": nrm(ks[4], (DEPTH, 6 * D_MODEL), 0.02),
        "w_in": nrm(ks[5], (N_EVEN, D_MODEL, IN_COLS), D_MODEL ** -0.5),
        "conv_w": nrm(ks[6], (N_EVEN, CONV_WIDTH, MIX_A), CONV_WIDTH ** -0.5),
        "conv_b": nrm(ks[7], (N_EVEN, MIX_A), 0.02),
        "cln_g": 1.0 + nrm(ks[8], (N_EVEN, MIX_A), 0.05),
        "cln_b": nrm(ks[9], (N_EVEN, MIX_A), 0.02),
        "w_out": nrm(ks[10], (N_EVEN, MIX_A + RET_WIDTH, D_MODEL), (MIX_A + RET_WIDTH) ** -0.5),
        "s5_lam_re": -0.5 + nrm(ks[11], (N_ODD, 2, G, P), 0.01),
        "s5_lam_im": math.pi * jnp.arange(P, dtype=f32) + nrm(ks[12], (N_ODD, 2, G, P), 0.01),
        "s5_log_step": jax.random.uniform(ks[13], (N_ODD, 2, G), f32,
                                          math.log(1e-3), math.log(1e-1)),
        "s5_b_re": nrm(ks[14], (N_ODD, 2, G, P, Cg), (2 * Cg) ** -0.5),
        "s5_b_im": nrm(ks[15], (N_ODD, 2, G, P, Cg), (2 * Cg) ** -0.5),
        "s5_c_re": nrm(ks[16], (N_ODD, 2, G, Cg, P), 0.5),
        "s5_c_im": nrm(ks[17], (N_ODD, 2, G, Cg, P), 0.5),
        "s5_d": nrm(ks[18], (N_ODD, D_MODEL), 1.0),
        "w_glu_a": nrm(ks[19], (N_ODD, D_MODEL, D_MODEL), D_MODEL ** -0.5),
        "w_glu_b": nrm(ks[20], (N_ODD, D_MODEL, D_MODEL), D_MODEL ** -0.5),
        "w_fc1": nrm(ks[21], (DEPTH, D_MODEL, D_FF), D_MODEL ** -0.5),
        "w_fc2": nrm(ks[22], (DEPTH, D_FF, D_MODEL), D_FF ** -0.5),
        "norm_f": 1.0 + nrm(ks[23], (D_MODEL,), 0.05),
    }


def reference(x, c, norm_g, ada_w, ada_b, w_in, conv_w, conv_b, cln_g, cln_b, w_out,
              s5_lam_re, s5_lam_im, s5_log_step, s5_b_re, s5_b_im, s5_c_re, s5_c_im, s5_d,
              w_glu_a, w_glu_b, w_fc1, w_fc2, norm_f):
    c_act = jax.nn.silu(c)
    for layer in range(DEPTH):
        mod = (c_act @ ada_w[layer] + ada_b[layer])[:, None, :]
        shift_m, scale_m, gate_m, shift_f, scale_f, gate_f = jnp.split(mod, 6, axis=-1)
        h = _rmsnorm(x, norm_g[layer, 0]) * (1.0 + scale_m) + shift_m
        if layer % 2 == 0:
            i = layer // 2
            mix = _mixer_conv_retention(h, w_in[i], conv_w[i], conv_b[i], cln_g[i],
                                        cln_b[i], w_out[i])
        else:
            i = layer // 2
            mix = _mixer_s5(h, s5_lam_re[i], s5_lam_im[i], s5_log_step[i], s5_b_re[i],
                            s5_b_im[i], s5_c_re[i], s5_c_im[i], s5_d[i],
                            w_glu_a[i], w_glu_b[i])
        x = x + gate_m * mix
        h = _rmsnorm(x, norm_g[layer, 1]) * (1.0 + scale_f) + shift_f
        x = x + gate_f * (jnp.square(jax.nn.relu(h @ w_fc1[layer])) @ w_fc2[layer])
    return _rmsnorm(x, norm_f)
```

```python
import math
from contextlib import ExitStack

import numpy as np
import ml_dtypes
import concourse.bass as bass
import concourse.mybir as mybir
from concourse.bass_utils import run_bass_kernel_spmd

F32 = mybir.dt.float32
BF16 = mybir.dt.bfloat16
I32 = mybir.dt.int32
ALU = mybir.AluOpType
AF = mybir.ActivationFunctionType

ENGS = ["tensor", "vector", "scalar", "gpsimd", "sync"]
S = 8192
D = 1024
DFF = 4096
EPS = 1e-6
NCH = 64
PADC = 16


class Buf:
    __slots__ = ("name", "w", "r", "dsem", "dcnt")

    def __init__(self, name=""):
        self.name = name
        self.w = None
        self.r = []
        self.dsem = None
        self.dcnt = 0


class Prog:
    def __init__(self, nc, es):
        self.nc = nc
        self.es = es
        self.ops = {e: [] for e in ENGS}
        self.cnt = {e: 0 for e in ENGS}
        self.sem = {e: es.enter_context(nc.semaphore("se_" + e)) for e in ENGS}
        self.seen = {e: {} for e in ENGS}
        self.dsems = []
        self.free_dsems = []
        self.arena = None
        self.aoff = 0

    def init_arena(self, nbytes):
        self.arena = self.es.enter_context(self.nc.sbuf_tensor("arena", [128, nbytes // 2], BF16))
        self.asize = nbytes
        self.aoff = 0

    def reset_arena(self, keep=0):
        self.aoff = keep
        self.atop = self.asize

    def tile_top(self, shape, dt):
        n = 1
        for s_ in shape[1:]:
            n *= s_
        nb = (n * 4 + 63) // 64 * 64
        self.atop -= nb
        assert self.atop >= self.aoff
        ap = self.arena[0:shape[0], self.atop // 2:(self.atop + n * 4) // 2].bitcast(dt)
        if len(shape) == 3:
            ap = ap.rearrange("p (a b) -> p a b", a=shape[1])
        elif len(shape) == 4:
            ap = ap.rearrange("p (a b c) -> p a b c", a=shape[1], b=shape[2])
        return ap

    def tile(self, shape, dt):
        esz = 4 if dt in (F32, I32) else 2
        n = 1
        for s_ in shape[1:]:
            n *= s_
        nb = (n * esz + 63) // 64 * 64
        assert self.aoff + nb <= getattr(self, "atop", self.asize), ("SBUF arena overflow", self.aoff, nb, self.asize)
        ap = self.arena[0:shape[0], self.aoff // 2:(self.aoff + n * esz) // 2]
        self.aoff += nb
        if esz == 4:
            ap = ap.bitcast(dt)
        if len(shape) == 3:
            ap = ap.rearrange("p (a b) -> p a b", a=shape[1])
        elif len(shape) == 4:
            ap = ap.rearrange("p (a b c) -> p a b c", a=shape[1], b=shape[2])
        return ap

    def _dsem(self, b):
        if b.dsem is None:
            if self.free_dsems:
                b.dsem, b.dcnt = self.free_dsems.pop()
            else:
                s_ = self.es.enter_context(self.nc.semaphore("sd%d" % len(self.dsems)))
                self.dsems.append(s_)
                b.dsem, b.dcnt = s_, 0
        return b.dsem

    def release(self, bufs):
        for b in bufs:
            if b.dsem is not None:
                self.free_dsems.append((b.dsem, b.dcnt))
                b.dsem = None

    def _waits(self, e, reads, writes):
        evs = []
        for b in reads:
            if b.w is not None:
                evs.append(b.w)
        for b in writes:
            if b.w is not None:
                evs.append(b.w)
            evs.extend(b.r)
        out = {}
        for (s_, v) in evs:
            if e == "tensor" and s_ is self.sem["tensor"]:
                continue
            if self.seen[e].get(s_.name, -1) >= v:
                continue
            if out.get(s_.name, (None, -1))[1] < v:
                out[s_.name] = (s_, v)
        for (s_, v) in out.values():
            self.seen[e][s_.name] = v
        return list(out.values())

    def op(self, e, fn, reads=(), writes=()):
        waits = self._waits(e, reads, writes)
        self.cnt[e] += 1
        ev = (self.sem[e], self.cnt[e])
        for b in reads:
            b.r.append(ev)
        for b in writes:
            b.w = ev
            b.r = []
        self.ops[e].append((waits, fn, ev[0], 1))
        return ev

    def dma(self, e, out, in_, sbuf_buf, reads=(), writes=(), **kw):
        waits = self._waits(e, reads, writes)
        s_ = self._dsem(sbuf_buf)
        sbuf_buf.dcnt += 16
        ev = (s_, sbuf_buf.dcnt)
        for b in reads:
            b.r.append(ev)
        for b in writes:
            b.w = ev
            b.r = []
        self.ops[e].append((waits, (lambda eng: eng.dma_start(out=out, in_=in_, **kw)), s_, 16))
        return ev

    def barrier(self, all_bufs=(), skip=()):
        evs = [(self.sem[f], self.cnt[f]) for f in ENGS if self.cnt[f] > 0]
        live = {}
        for b in all_bufs:
            if b.dsem is not None:
                live[b.dsem.name] = (b.dsem, b.dcnt)
        for (s_, c_) in self.free_dsems:
            live.setdefault(s_.name, (s_, c_))
        evs += [v for v in live.values() if v[1] > 0]
        for e in ENGS:
            if e in skip:
                continue
            w = []
            for (s_, v) in evs:
                if self.seen[e].get(s_.name, -1) >= v:
                    continue
                self.seen[e][s_.name] = v
                w.append((s_, v))
            if w:
                self.ops[e].append((w, None, None, 0))

    def emit(self):
        with self.nc.Block() as block:
            def mk(e):
                def body(eng):
                    for (waits, fn, s_, inc) in self.ops[e]:
                        for (ws, wv) in waits:
                            eng.wait_ge(ws, wv)
                        if fn is not None:
                            fn(eng).then_inc(s_, inc)
                return body
            block.tensor(mk("tensor"))
            block.vector(mk("vector"))
            block.scalar(mk("scalar"))
            block.gpsimd(mk("gpsimd"))
            block.sync(mk("sync"))

    def act(self, out, in_, func, r, w, **kw):
        return self.op("scalar", lambda a: a.activation(out=out, in_=in_, func=func, **kw), r, w)

    def tt(self, out, in0, in1, op, r, w, e="vector"):
        return self.op(e, lambda v: v.tensor_tensor(out=out, in0=in0, in1=in1, op=op), r, w)

    def ts(self, out, in0, s1, s2, op0, op1, r, w, e="vector"):
        if s2 is None:
            return self.op(e, lambda v: v.tensor_scalar(out=out, in0=in0, scalar1=s1, scalar2=None,
                                                        op0=op0), r, w)
        return self.op(e, lambda v: v.tensor_scalar(out=out, in0=in0, scalar1=s1, scalar2=s2,
                                                    op0=op0, op1=op1), r, w)

    def stt(self, out, in0, sc, in1, op0, op1, r, w):
        return self.op("vector", lambda v: v.scalar_tensor_tensor(out=out, in0=in0, scalar=sc, in1=in1,
                                                                  op0=op0, op1=op1), r, w)

    def cp(self, e, out, in_, r, w):
        if e == "scalar":
            return self.act(out, in_, AF.Copy, r, w)
        return self.op(e, lambda v: v.tensor_copy(out=out, in_=in_), r, w)

    def mm(self, out, lhsT, rhs, start, stop, r, w):
        return self.op("tensor", lambda t: t.matmul(out, lhsT=lhsT, rhs=rhs, start=start, stop=stop), r, w)

    def tr(self, out, in_, ident, r, w):
        return self.op("tensor", lambda t: t.transpose(out=out, in_=in_, identity=ident), r, w)

    def memset(self, e, ap, val, w):
        return self.op(e, lambda g: g.memset(ap, val), (), w)

    def recip(self, out, in_, r, w):
        return self.op("vector", lambda v: v.reciprocal(out=out, in_=in_), r, w)


class K:
    pass


def bcast_mid(ap2, n):
    return ap2.unsqueeze(1).to_broadcast([ap2.shape[0], n, ap2.shape[1]])


def bcast_last(ap2, n):
    return ap2.unsqueeze(2).to_broadcast([ap2.shape[0], ap2.shape[1], n])


def setup_consts(P, k):
    k.identf = P.tile([128, 128], F32)
    k.ident = P.tile([128, 128], BF16)
    k.jswap = P.tile([128, 128], F32)
    k.bconst = Buf("const")
    b = k.bconst
    P.memset("gpsimd", k.identf, 1.0, [b])
    P.op("gpsimd", lambda g: g.affine_select(out=k.identf, in_=k.identf, pattern=[[-1, 128]],
                                             compare_op=ALU.is_equal, fill=0.0, base=0,
                                             channel_multiplier=1), [b], [b])
    P.cp("gpsimd", k.ident, k.identf, [b], [b])
    P.memset("gpsimd", k.jswap, 0.0, [b])
    P.cp("gpsimd", k.jswap[0:64, 64:128], k.identf[0:64, 0:64], [b], [b])
    P.cp("gpsimd", k.jswap[64:128, 0:64], k.identf[64:128, 64:128], [b], [b])
    k.psbig = P.es.enter_context(P.nc.psum_tensor("psbig", [128, 4096], F32))
    k.ps = [k.psbig[:, i * 512:(i + 1) * 512] for i in range(8)]
    k.pb = [Buf("psb%d" % i) for i in range(8)]
    k.keep = P.aoff


def rms_rstd(P, xin, ss, bss, junk, bjunk, rx, col=0, n=D):
    P.act(junk, xin, AF.Square, rx, [bjunk, bss], accum_out=ss[:, col:col + 1])
    P.act(ss[:, col + 1:col + 2], ss[:, col:col + 1], AF.Sqrt, [bss], [bss], scale=1.0 / n, bias=EPS)
    P.recip(ss[:, col + 2:col + 3], ss[:, col + 1:col + 2], [bss], [bss])


_EPS_AP = {}


def k_eps(P):
    return _EPS_AP["ap"]


def load_w_bf16(P, dst, bdst_list, w, kts, ncols, stg, bstg, si, col0=0, scale_cols=None):
    cw = min(ncols, stg[0].shape[1])
    for kt in range(kts):
        for c0 in range(0, ncols, cw):
            s_ = si[0] % 2
            si[0] += 1
            P.dma("sync", stg[s_][:, 0:cw], w[kt * 128:(kt + 1) * 128, col0 + c0:col0 + c0 + cw], bstg[s_],
                  writes=[bstg[s_]])
            P.cp(("gpsimd", "vector", "scalar")[si[0] % 3], dst[:, kt, c0:c0 + cw], stg[s_][:, 0:cw], [bstg[s_]],
                 [bdst_list[kt]])


def phase_mod(P, k, T):
    P.reset_arena(k.keep)
    bufs = []
    cT = P.tile([128, 8], F32)
    bc = Buf("cT"); bufs.append(bc)
    P.dma("sync", cT, T["c"][0, :].rearrange("(kt p) -> p kt", p=128), bc, writes=[bc],
          allow_slow_non_contiguous=True)
    P.act(cT, cT, AF.Silu, [bc], [bc])
    adab = P.tile([1, 2 * 6144], F32)
    ng = P.tile([1, 4 * 1024], F32)
    bsm = Buf("small"); bufs.append(bsm)
    P.dma("sync", adab, T["ada_b"].rearrange("l n -> (l n)").unsqueeze(0), bsm, writes=[bsm])
    P.dma("sync", ng, T["norm_g"].rearrange("l t n -> (l t n)").unsqueeze(0), bsm, writes=[bsm])
    MR = P.tile([1, 2 * 6144], F32)
    bmr = Buf("MR"); bufs.append(bmr)
    NB = 2
    aw = [P.tile([128, 8, 512], F32) for _ in range(NB)]
    baw = [Buf("aw%d" % i) for i in range(NB)]
    bufs += baw
    it = 0
    for l in range(2):
        for cb in range(12):
            s_ = it % NB
            P.dma("sync", aw[s_], T["ada_w"][l, :, cb * 512:(cb + 1) * 512].rearrange("(kt p) n -> p kt n", p=128),
                  baw[s_], writes=[baw[s_]])
            bank = it % 2
            for kt in range(8):
                P.mm(k.ps[bank][0:1, :], cT[:, kt:kt + 1], aw[s_][:, kt, :], kt == 0, kt == 7,
                     [bc, baw[s_]], [k.pb[bank]])
            o = l * 6144 + cb * 512
            P.tt(MR[:, o:o + 512], k.ps[bank][0:1, :], adab[:, o:o + 512], ALU.add, [k.pb[bank], bsm], [bmr])
            it += 1
    for l in range(2):
        for t in range(2):
            o = l * 6144 + (1 + 3 * t) * 1024
            P.stt(MR[:, o:o + 1024], MR[:, o:o + 1024], 1.0, ng[:, (l * 2 + t) * 1024:(l * 2 + t + 1) * 1024],
                  ALU.add, ALU.mult, [bmr, bsm], [bmr])
    P.dma("sync", T["modrows"].rearrange("l t n -> (l t n)").unsqueeze(0), MR, bmr, reads=[bmr],
          writes=[k.b_modrows])
    P.barrier(bufs)
    P.release(bufs)


def load_mods(P, k, T, l, which, bufs):
    m = P.tile([128, 3, 1024], F32)
    bm = Buf("mods"); bufs.append(bm)
    for i in range(3):
        P.dma("sync", m[:, i, :], T["modrows"][l, which * 3 + i, :].partition_broadcast(128), bm,
              reads=[k.b_modrows], writes=[bm])
    return m, bm


class NormW(dict):
    def __init__(self, sets):
        super().__init__()
        self.sets = sets
        self.i = 0

    def __getitem__(self, key):
        return self.sets[self.i][key]

    def flip(self):
        self.i ^= 1


def norm_mod_tile(P, k, X, bX, m, bm, W):
    W.flip()
    rms_rstd(P, X, W["ss"], W["bss"], W["junk"], Buf(), [bX])
    P.stt(W["tn"], X, W["ss"][:, 2:3], m[:, 1, :], ALU.mult, ALU.mult, [bX, W["bss"], bm], [W["btn"]])
    P.tt(W["hb"], W["tn"], m[:, 0, :], ALU.add, [W["btn"], bm], [W["bhb"]])


def norm_work(P, bufs, with_f32=False):
    sets = []
    junk = P.tile([128, D], BF16)
    for _ in range(2):
        Wd = {}
        Wd["ss"] = P.tile([128, 8], F32); Wd["bss"] = Buf("ss")
        Wd["junk"] = junk; Wd["bjunk"] = Buf("junk")
        Wd["tn"] = P.tile([128, D], F32); Wd["btn"] = Buf("tn")
        Wd["hb"] = P.tile([128, D], BF16); Wd["bhb"] = Buf("hb")
        sets.append(Wd)
    return NormW(sets)


def transpose8(P, k, src, bsrc, dst3, bdst, bank, n=8, evac="scalar"):
    pT = k.ps[bank].bitcast(BF16)
    for kt in range(n):
        P.tr(pT[:, kt * 128:(kt + 1) * 128], src[:, kt * 128:(kt + 1) * 128], k.ident,
             [bsrc, k.bconst], [k.pb[bank]])
    P.cp(evac, dst3, pT[:, 0:n * 128].rearrange("p (k t) -> p k t", k=n), [k.pb[bank]], [bdst])


def phase_mlp(P, k, T, l, x_in, b_in, x_out, b_out, final, h1_out=False):
    P.reset_arena(k.keep)
    bufs = []
    TB = 256
    NS = 2
    W1 = P.tile([128, 8, DFF], BF16)
    W2 = P.tile([128, 32, D], BF16)
    bW1 = [Buf() for _ in range(8)]
    bW2 = [Buf() for _ in range(32)]
    stg = [P.tile([128, 1024], F32) for _ in range(2)]
    bstg = [Buf("stg%d" % i) for i in range(2)]
    bufs += bstg
    si = [0]
    m, bm = load_mods(P, k, T, l, 1, bufs)
    fg = None
    if final:
        fg = P.tile([128, D], F32)
        bfg = Buf("fg"); bufs.append(bfg)
        P.dma("sync", fg, T["norm_f"].partition_broadcast(128), bfg, writes=[bfg])
    load_w_bf16(P, W1, bW1, T["w_fc1"][l], 8, DFF, stg, bstg, si)
    w2v = T["w_fc2"][l]
    for kt in range(32):
        s_ = si[0] % 2
        si[0] += 1
        P.dma("sync", stg[s_], w2v[kt * 128:(kt + 1) * 128, :], bstg[s_], writes=[bstg[s_]])
        P.tt(W2[:, kt, :], stg[s_], m[:, 2, :], ALU.mult, [bstg[s_], bm], [bW2[kt]],
             e=("gpsimd", "vector")[kt % 2])
    NX = 2
    xt = [P.tile([128, NS, D], F32) for _ in range(NX)]
    bxt = [Buf("xt%d" % i) for i in range(NX)]
    bufs += bxt
    W = norm_work(P, bufs)
    if h1_out:
        bufs += [W.sets[0]["bhb"], W.sets[1]["bhb"]]
        m1 = P.tile([128, 2, 1024], F32)
        bm1 = Buf("m1"); bufs.append(bm1)
        for i_ in range(2):
            P.dma("sync", m1[:, i_, :], T["modrows"][1, i_, :].partition_broadcast(128), bm1,
                  reads=[k.b_modrows], writes=[bm1])
    hT = [P.tile([128, 8, TB], BF16) for _ in range(2)]
    bhT = [Buf("hT0"), Buf("hT1")]
    NA = 4
    actT = [P.tile([128, TB], BF16) for _ in range(NA)]
    bact = [Buf() for _ in range(NA)]
    rl = [P.tile([128, TB], F32) for _ in range(2)]
    brl = [Buf() for _ in range(2)]
    ps, pb = k.ps, k.pb
    nblk = S // TB

    def load_x(blk):
        xb = blk % NX
        P.dma("sync", xt[xb], x_in[blk * TB:(blk + 1) * TB, :].rearrange("(s p) f -> p s f", p=128),
              bxt[xb], reads=[b_in], writes=[bxt[xb]])

    def front(blk):
        X, bX = xt[blk % NX], bxt[blk % NX]
        for s_ in range(NS):
            norm_mod_tile(P, k, X[:, s_, :], bX, m, bm, W)
            transpose8(P, k, W["hb"], W["bhb"], hT[blk % 2][:, :, s_ * 128:(s_ + 1) * 128], bhT[blk % 2], 0)

    def fc1(blk, j):
        bank = 1 + (j % 2)
        for kt in range(8):
            P.mm(ps[bank][:, 0:TB], W1[:, kt, j * 128:(j + 1) * 128], hT[blk % 2][:, kt, :], kt == 0, kt == 7,
                 [bW1[kt], bhT[blk % 2]], [pb[bank]])
        a = j % NA
        r = j % 2
        P.act(rl[r], ps[bank][:, 0:TB], AF.Relu, [pb[bank]], [brl[r]])
        P.tt(actT[a], rl[r], rl[r], ALU.mult, [brl[r]], [bact[a]])

    def fc2(blk, j):
        a = j % NA
        for s_ in range(NS):
            for h in range(2):
                bank = 4 + s_ * 2 + h
                P.mm(ps[bank], actT[a][:, s_ * 128:(s_ + 1) * 128], W2[:, j, h * 512:(h + 1) * 512],
                     j == 0, j == 31, [bact[a], bW2[j]], [pb[bank]])

    tails = []

    def tail(blk):
        X, bX = xt[blk % NX], bxt[blk % NX]
        if final:
            for s_ in range(NS):
                W.flip()
                rms_rstd(P, X[:, s_, :], W["ss"], W["bss"], W["junk"], Buf(), [bX], col=4)
                P.stt(X[:, s_, :], X[:, s_, :], W["ss"][:, 6:7], fg, ALU.mult, ALU.mult,
                      [bX, W["bss"], bfg], [bX])
        P.dma("gpsimd", x_out[blk * TB:(blk + 1) * TB, :].rearrange("(s p) f -> p s f", p=128), X, bX,
              reads=[bX], writes=[b_out])
        if h1_out:
            for s_ in range(NS):
                W.flip()
                rms_rstd(P, X[:, s_, :], W["ss"], W["bss"], W["junk"], Buf(), [bX], col=4)
                P.stt(W["tn"], X[:, s_, :], W["ss"][:, 6:7], m1[:, 1, :], ALU.mult, ALU.mult,
                      [bX, W["bss"], bm1], [W["btn"]])
                P.tt(W["hb"], W["tn"], m1[:, 0, :], ALU.add, [W["btn"], bm1], [W["bhb"]])
                r0 = blk * TB + s_ * 128
                P.dma("gpsimd", T["h1"][r0:r0 + 128, :], W["hb"], W["bhb"], reads=[W["bhb"]], writes=[k.b_h1])
        if blk + 2 < nblk:
            load_x(blk + 2)

    load_x(0)
    load_x(1)
    front(0)
    for blk in range(nblk):
        X, bX = xt[blk % NX], bxt[blk % NX]
        fc1(blk, 0)
        for j in range(32):
            if j + 1 < 32:
                fc1(blk, j + 1)
            fc2(blk, j)
            if j == 4 and tails:
                tail(tails.pop(0))
            if j == 14 and blk + 1 < nblk:
                front(blk + 1)
        for s_ in range(NS):
            for h in range(2):
                bank = 4 + s_ * 2 + h
                P.tt(X[:, s_, h * 512:(h + 1) * 512], ps[bank], X[:, s_, h * 512:(h + 1) * 512], ALU.add,
                     [pb[bank], bX], [bX])
        tails.append(blk)
    while tails:
        tail(tails.pop(0))
    P.barrier(bufs)
    P.release(bufs)


def phase_l0(P, k, T):
    P.reset_arena(k.keep)
    bufs = []
    ps, pb = k.ps, k.pb
    Win = P.tile([128, 8, 3072], BF16)
    bWin = [Buf() for _ in range(8)]
    Wout = P.tile([128, 8, 1024], BF16)
    bWout = [Buf() for _ in range(8)]
    stg = [P.tile([128, 1536], F32) for _ in range(2)]
    bstg = [Buf("stg%d" % i) for i in range(2)]
    bufs += bstg
    si = [0]
    m, bm = load_mods(P, k, T, 0, 0, bufs)
    for kt in range(8):
        for c0 in (0, 1536):
            s_ = si[0] % 2
            si[0] += 1
            P.dma("sync", stg[s_][:, 0:1536], T["w_in"][0, kt * 128:(kt + 1) * 128, c0:c0 + 1536], bstg[s_],
                  writes=[bstg[s_]])
            P.cp(("gpsimd", "vector", "scalar")[si[0] % 3], Win[:, kt, c0:c0 + 1536], stg[s_][:, 0:1536],
                 [bstg[s_]], [bWin[kt]])
    load_w_bf16(P, Wout, bWout, T["w_out"][0], 8, 1024, stg, bstg, si)
    rt = P.tile([128, 5, 512], F32)
    brt = Buf("rt"); bufs.append(brt)
    P.dma("sync", rt, T["ret_tab"].rearrange("t p n -> p t n"), brt, writes=[brt])
    DMT, WQF, WQB, GL = rt[:, 0, :], rt[:, 1, :], rt[:, 2, :], rt[:, 3, :]
    wkf, wkb = rt[:, 4, 0:4], rt[:, 4, 4:8]
    cw = P.tile([128, 4, 31], F32)
    cpar = P.tile([128, 3, 4], F32)
    bcp = Buf("convp"); bufs.append(bcp)
    for j in range(4):
        P.dma("sync", cw[:, j, :], T["conv_w"][0][:, j * 128:(j + 1) * 128].rearrange("w c -> c w"), bcp,
              writes=[bcp], allow_slow_non_contiguous=True)
    for i, nm in enumerate(["conv_b", "cln_g", "cln_b"]):
        P.dma("sync", cpar[:, i, :], T[nm][0].rearrange("(j c) -> c j", c=128), bcp, writes=[bcp],
              allow_slow_non_contiguous=True)
    DG = P.tile([128, 4 * 31, 128], BF16)
    bDG = Buf("DG")
    for j in range(4):
        for w in range(31):
            if (j * 31 + w) % 2 == 0:
                P.ts(DG[:, j * 31 + w, :], k.identf, cw[:, j, w:w + 1], None, ALU.mult, None, [bcp, k.bconst], [bDG])
            else:
                P.act(DG[:, j * 31 + w, :], k.identf, AF.Copy, [bcp, k.bconst], [bDG], scale=cw[:, j, w:w + 1])
    onesf = P.tile([128, 128], F32)
    bones = Buf("ones")
    P.memset("gpsimd", onesf, 1.0 / 512.0, [bones])

    W = norm_work(P, bufs)
    cs = [P.tile([128, 2, 64], F32) for _ in range(2)]
    bcs = [Buf("cs%d" % i) for i in range(2)]
    bufs += bcs
    tA = P.tile([128, 8, 64], F32); btA = Buf()
    tB = P.tile([128, 8, 64], F32); btB = Buf()
    qk = P.tile([128, 2, 512], BF16); bqk = Buf("qk")
    vbf = P.tile([128, 2, 512], BF16); bvbf = Buf("vbf")
    markA = P.aoff
    xt = [P.tile([128, D], F32) for _ in range(2)]
    bxt = [Buf("xt%d" % i) for i in range(2)]
    bufs += bxt
    hTc = [P.tile([128, 8, 128], BF16) for _ in range(2)]
    bhTc = [Buf("hTc%d" % i) for i in range(2)]
    bufs += bhTc
    R = P.tile([128, 512], F32); bR = Buf("R")
    Rb = [P.tile([128, 512], BF16) for _ in range(2)]
    bRb = [Buf("Rb%d" % i) for i in range(2)]
    bufs += bRb
    zero_bf = P.tile([128, 8, PADC], BF16); bz = Buf("zero"); bufs.append(bz)
    P.memset("gpsimd", zero_bf, 0.0, [bz])
    hT0 = T["hT0"]
    P.dma("sync", hT0[:, :, 0:PADC], zero_bf, bz, reads=[bz], writes=[k.b_hT0])
    P.dma("sync", hT0[:, :, PADC + S:PADC + S + PADC], zero_bf, bz, reads=[bz], writes=[k.b_hT0])

    def rotary(src_ps, dst, nh, bsrc, bdst, cst, bcst):
        sv = src_ps.rearrange("p (h t d) -> p h t d", h=nh, t=2)
        dv = dst.rearrange("p (h t d) -> p h t d", h=nh, t=2)
        cosb = bcast_mid(cst[:, 0, :], nh)
        sinb = bcast_mid(cst[:, 1, :], nh)
        a_, b_ = tA[:, 0:nh, :], tB[:, 0:nh, :]
        P.tt(a_, sv[:, :, 0, :], cosb, ALU.mult, [bsrc, bcst], [btA])
        P.tt(b_, sv[:, :, 1, :], sinb, ALU.mult, [bsrc, bcst], [btB])
        P.tt(dv[:, :, 0, :], a_, b_, ALU.subtract, [btA, btB], [bdst])
        P.tt(a_, sv[:, :, 0, :], sinb, ALU.mult, [bsrc, bcst], [btA])
        P.tt(b_, sv[:, :, 1, :], cosb, ALU.mult, [bsrc, bcst], [btB])
        P.tt(dv[:, :, 1, :], a_, b_, ALU.add, [btA, btB], [bdst])

    P.memset("vector", R, 0.0, [bR])
    P.memset("vector", Rb[0], 0.0, [bRb[0]])
    P.memset("vector", Rb[1], 0.0, [bRb[1]])
    order = list(range(NCH - 1, -1, -1))

    def loadA(i):
        c = order[i]
        P.dma("sync", xt[i % 2], T["x"][c * 128:(c + 1) * 128, :], bxt[i % 2], writes=[bxt[i % 2]])
        P.dma("sync", cs[i % 2], T["rot_tab"][c], bcs[i % 2], writes=[bcs[i % 2]])
    def frontA(i):
        c = order[i]
        X, bX = xt[i % 2], bxt[i % 2]
        H, bH = hTc[i % 2], bhTc[i % 2]
        norm_mod_tile(P, k, X, bX, m, bm, W)
        transpose8(P, k, W["hb"], W["bhb"], H, bH, 0)
        P.dma("gpsimd", hT0[:, :, PADC + c * 128:PADC + (c + 1) * 128], H, bH, reads=[bH], writes=[k.b_hT0])

    def backA(i):
        c = order[i]
        H, bH = hTc[i % 2], bhTc[i % 2]
        kb_, vb_ = (1, 2) if i % 2 == 0 else (4, 5)
        for kt in range(8):
            P.mm(ps[kb_], H[:, kt, :], Win[:, kt, 1536:2048], kt == 0, kt == 7, [bH, bWin[kt]], [pb[kb_]])
        for kt in range(8):
            P.mm(ps[vb_], H[:, kt, :], Win[:, kt, 2048:2560], kt == 0, kt == 7, [bH, bWin[kt]], [pb[vb_]])
        rotary(ps[kb_], qk[:, 1, :], 4, pb[kb_], bqk, cs[i % 2], bcs[i % 2])
        P.tt(vbf[:, 1, :].rearrange("p (h e) -> p h e", h=4), ps[vb_].rearrange("p (h e) -> p h e", h=4),
             bcast_last(wkb, 128), ALU.mult, [pb[vb_], brt], [bvbf])
        for h in range(4):
            P.mm(ps[3][:, h * 128:(h + 1) * 128], qk[:, 1, h * 128:(h + 1) * 128], vbf[:, 1, h * 128:(h + 1) * 128],
                 True, True, [bqk, bvbf], [pb[3]])
        rb, brb = Rb[i % 2], bRb[i % 2]
        P.cp("scalar", rb, R, [bR], [brb])
        P.dma("gpsimd", T["sb_all"][c], rb, brb, reads=[brb], writes=[k.b_sb])
        P.tt(R, R, GL, ALU.mult, [bR, brt], [bR])
        P.tt(R, R, ps[3], ALU.add, [bR, pb[3]], [bR])

    loadA(0)
    loadA(1)
    frontA(0)
    for i in range(NCH):
        if i + 1 < NCH:
            frontA(i + 1)
        backA(i)
        if i + 2 < NCH:
            loadA(i + 2)

    P.barrier(bufs)
    P.aoff = markA
    hTh = [P.tile([128, 8, 160], BF16) for _ in range(2)]
    bhTh = [Buf("hTh%d" % i) for i in range(2)]
    bufs += bhTh
    sbt = [P.tile([128, 512], BF16) for _ in range(2)]
    bsbt = [Buf("sbt%d" % i) for i in range(2)]
    bufs += bsbt
    sig = P.tile([128, 4, 158], F32); bsig = Buf()
    abf = P.tile([128, 4, 158], BF16); babf = Buf()
    conv = P.tile([128, 4, 128], F32); bconv = Buf()
    csq = P.tile([128, 4, 128], F32); bcsq = Buf()
    st = P.tile([128, 4, 128], F32); bst = Buf()
    aT = P.tile([128, 8, 128], BF16); baT = Buf("aT")
    sg = P.tile([128, 512], F32); bsg = Buf()
    qT = P.tile([128, 4, 512], BF16); bqT = Buf("qT")
    sT = P.tile([128, 512], BF16); bsT = Buf()
    rr = P.tile([128, 512], BF16); brr = Buf()
    hs = P.tile([128, 16], F32); bhs = Buf()
    Sf = P.tile([128, 512], F32); bSf = Buf("Sf")
    Sfb = P.tile([128, 512], BF16); bSfb = Buf("Sfb")
    xo = [P.tile([128, D], F32) for _ in range(2)]
    bxo = [Buf("xo%d" % i) for i in range(2)]
    bufs += bxo
    tmpo = P.tile([128, 512], F32); btmpo = Buf()
    P.memset("vector", Sf, 0.0, [bSf])
    P.memset("vector", Sfb, 0.0, [bSfb])

    def loadB(c):
        i = c % 2
        P.dma("sync", hTh[i][:, :, 0:158], hT0[:, :, PADC + c * 128 - 15:PADC + c * 128 + 143], bhTh[i],
              reads=[k.b_hT0], writes=[bhTh[i]])
        P.dma("sync", sbt[i], T["sb_all"][c], bsbt[i], reads=[k.b_sb], writes=[bsbt[i]])
        P.dma("sync", cs[i], T["rot_tab"][c], bcs[i], writes=[bcs[i]])
        P.dma("sync", xo[i], T["x"][c * 128:(c + 1) * 128, :], bxo[i], writes=[bxo[i]])
    def stA1(c):
        H, bH = hTh[c % 2], bhTh[c % 2]
        for half in range(2):
            for jj in range(2):
                j = half * 2 + jj
                for kt in range(8):
                    P.mm(ps[half][:, jj * 160:jj * 160 + 158], Win[:, kt, j * 128:(j + 1) * 128], H[:, kt, 0:158],
                         kt == 0, kt == 7, [bWin[kt], bH], [pb[half]])
                for kt in range(8):
                    P.mm(ps[2 + half][:, jj * 160:jj * 160 + 158], Win[:, kt, 512 + j * 128:512 + (j + 1) * 128],
                         H[:, kt, 0:158], kt == 0, kt == 7, [bWin[kt], bH], [pb[2 + half]])

    def stA2(c):
        for half in range(2):
            gv = ps[2 + half][:, 0:320].rearrange("p (j t) -> p j t", j=2)[:, :, 0:158]
            vv = ps[half][:, 0:320].rearrange("p (j t) -> p j t", j=2)[:, :, 0:158]
            P.act(sig[:, half * 2:half * 2 + 2, :], gv, AF.Sigmoid, [pb[2 + half]], [bsig])
            P.tt(abf[:, half * 2:half * 2 + 2, :], vv, sig[:, half * 2:half * 2 + 2, :], ALU.mult,
                 [pb[half], bsig], [babf])

    def stA3(c):
        for j in range(4):
            for w in range(31):
                P.mm(ps[0][:, j * 128:(j + 1) * 128], DG[:, j * 31 + w, :], abf[:, j, w:w + 128], w == 0, w == 30,
                     [bDG, babf], [pb[0]])

    def stA4(c):
        for j in range(4):
            P.act(conv[:, j, :], ps[0][:, j * 128:(j + 1) * 128], AF.Identity, [pb[0], bcp], [bconv],
                  bias=cpar[:, 0, j:j + 1])
        P.act(csq, conv, AF.Square, [bconv], [bcsq])

    def stA5(c):
        for j in range(4):
            P.mm(ps[1][:, 0:128], onesf, conv[:, j, :], j == 0, j == 3, [bones, bconv], [pb[1]])
        for j in range(4):
            P.mm(ps[1][:, 128:256], onesf, csq[:, j, :], j == 0, j == 3, [bones, bcsq], [pb[1]])

    def stA6(c):
        P.cp("vector", st[:, 0, :], ps[1][:, 0:128], [pb[1]], [bst])
        P.tt(st[:, 1, :], st[:, 0, :], st[:, 0, :], ALU.mult, [bst], [bst])
        P.tt(st[:, 1, :], ps[1][:, 128:256], st[:, 1, :], ALU.subtract, [pb[1], bst], [bst])
        P.act(st[:, 2, :], st[:, 1, :], AF.Sqrt, [bst], [bst], bias=EPS)
        P.recip(st[:, 3, :], st[:, 2, :], [bst], [bst])
        P.tt(conv, conv, bcast_mid(st[:, 0, :], 4), ALU.subtract, [bconv, bst], [bconv])
        P.tt(conv, conv, bcast_mid(st[:, 3, :], 4), ALU.mult, [bconv, bst], [bconv])

    def stA7(c):
        for j in range(4):
            P.act(aT[:, j, :], conv[:, j, :], AF.Silu, [bconv, bcp], [baT], scale=cpar[:, 1, j:j + 1],
                  bias=cpar[:, 2, j:j + 1])

    def stB1(c):
        H, bH = hTh[c % 2], bhTh[c % 2]
        for blk_, col0 in enumerate((1024, 1536, 2048, 2560)):
            bank = 4 + blk_
            for kt in range(8):
                P.mm(ps[bank], H[:, kt, 15:143], Win[:, kt, col0:col0 + 512], kt == 0, kt == 7, [bH, bWin[kt]],
                     [pb[bank]])

    def stB2(c):
        i = c % 2
        rotary(ps[4], qk[:, 0, :], 4, pb[4], bqk, cs[i], bcs[i])
        rotary(ps[5], qk[:, 1, :], 4, pb[5], bqk, cs[i], bcs[i])
        P.cp("scalar", vbf[:, 0, :], ps[6], [pb[6]], [bvbf])
        P.tt(vbf[:, 1, :].rearrange("p (h e) -> p h e", h=4), ps[6].rearrange("p (h e) -> p h e", h=4),
             bcast_last(wkf, 128), ALU.mult, [pb[6], brt], [bvbf])
        P.act(sg, ps[7], AF.Silu, [pb[7]], [bsg])

    def stB3(c):
        pTq = ps[4].bitcast(BF16)
        pTk = ps[5].bitcast(BF16)
        for h in range(4):
            P.tr(pTq[:, h * 128:(h + 1) * 128], qk[:, 0, h * 128:(h + 1) * 128], k.ident, [bqk, k.bconst], [pb[4]])
        for h in range(4):
            P.tr(pTk[:, h * 128:(h + 1) * 128], qk[:, 1, h * 128:(h + 1) * 128], k.ident, [bqk, k.bconst], [pb[5]])

    def stB4(c):
        pTq = ps[4].bitcast(BF16)
        pTk = ps[5].bitcast(BF16)
        P.cp("scalar", qT[:, 0, :], pTq[:, 0:512], [pb[4]], [bqT])
        P.tt(qT[:, 1, :], pTq[:, 0:512], WQF, ALU.mult, [pb[4], brt], [bqT])
        P.tt(qT[:, 2, :], pTq[:, 0:512], WQB, ALU.mult, [pb[4], brt], [bqT])
        P.cp("scalar", qT[:, 3, :], pTk[:, 0:512], [pb[5]], [bqT])

    def stB5(c):
        for h in range(4):
            P.mm(ps[2][:, h * 128:(h + 1) * 128], qT[:, 3, h * 128:(h + 1) * 128], qT[:, 0, h * 128:(h + 1) * 128],
                 True, True, [bqT], [pb[2]])

    def stB6(c):
        P.tt(sT, ps[2], DMT, ALU.mult, [pb[2], brt], [bsT])

    def stB7(c):
        i = c % 2
        for h in range(4):
            hs_ = slice(h * 128, (h + 1) * 128)
            P.mm(ps[7][:, hs_], sT[:, hs_], vbf[:, 0, hs_], True, False, [bsT, bvbf], [pb[7]])
            P.mm(ps[7][:, hs_], qT[:, 1, hs_], Sfb[:, hs_], False, False, [bqT, bSfb], [pb[7]])
            P.mm(ps[7][:, hs_], qT[:, 2, hs_], sbt[i][:, hs_], False, True, [bqT, bsbt[i]], [pb[7]])
        for h in range(4):
            hs_ = slice(h * 128, (h + 1) * 128)
            P.mm(ps[6][:, hs_], qk[:, 1, hs_], vbf[:, 1, hs_], True, True, [bqk, bvbf], [pb[6]])

    def stB8(c):
        P.tt(Sf, Sf, GL, ALU.mult, [bSf, brt], [bSf])
        P.tt(Sf, Sf, ps[6], ALU.add, [bSf, pb[6]], [bSf])
        P.cp("scalar", Sfb, Sf, [bSf], [bSfb])
        for h in range(4):
            P.act(W["junk"][:, 0:128], ps[7][:, h * 128:(h + 1) * 128], AF.Square, [pb[7]], [Buf(), bhs],
                  accum_out=hs[:, h:h + 1])
        P.act(hs[:, 4:8], hs[:, 0:4], AF.Sqrt, [bhs], [bhs], scale=1.0 / 128.0, bias=EPS)
        P.recip(hs[:, 8:12], hs[:, 4:8], [bhs], [bhs])
        for h in range(4):
            hs_ = slice(h * 128, (h + 1) * 128)
            P.stt(rr[:, hs_], ps[7][:, hs_], hs[:, 8 + h:9 + h], sg[:, hs_], ALU.mult, ALU.mult,
                  [pb[7], bhs, bsg], [brr])

    def stB9(c):
        transpose8(P, k, rr, brr, aT[:, 4:8, :], baT, 6, n=4)

    def stC(c):
        i = c % 2
        for hf in range(2):
            for kt in range(8):
                P.mm(ps[4 + hf], aT[:, kt, :], Wout[:, kt, hf * 512:(hf + 1) * 512], kt == 0, kt == 7,
                     [baT, bWout[kt]], [pb[4 + hf]])
        for hf in range(2):
            P.tt(tmpo, ps[4 + hf], m[:, 2, hf * 512:(hf + 1) * 512], ALU.mult, [pb[4 + hf], bm], [btmpo])
            P.tt(xo[i][:, hf * 512:(hf + 1) * 512], tmpo, xo[i][:, hf * 512:(hf + 1) * 512], ALU.add,
                 [btmpo, bxo[i]], [bxo[i]])
        P.dma("gpsimd", T["x1"][c * 128:(c + 1) * 128, :], xo[i], bxo[i], reads=[bxo[i]], writes=[k.b_x1])

    loadB(0)
    loadB(1)
    stA1(0)
    for c in range(NCH):
        for stg_ in (stB1, stA2, stB2, stA3, stB3, stB4, stA4, stB5, stA5, stB6, stA6, stB7):
            stg_(c)
        if c + 1 < NCH:
            stA1(c + 1)
        for stg_ in (stB8, stA7, stB9, stC):
            stg_(c)
        if c + 2 < NCH:
            loadB(c + 2)
    P.barrier(bufs)
    P.release(bufs)


def phase_s5(P, k, T, do_pre=True):
    ps, pb = k.ps, k.pb
    if do_pre:
        P.reset_arena(k.keep)
        bufs = []
        m, bm = load_mods(P, k, T, 1, 0, bufs)
        W = norm_work(P, bufs)
        xt = [P.tile([128, D], F32) for _ in range(2)]
        bxt = [Buf("xt%d" % i) for i in range(2)]
        bufs += bxt
        hbs = [P.tile([128, D], BF16) for _ in range(2)]
        bhbs = [Buf("hbs%d" % i) for i in range(2)]
        bufs += bhbs

        def ldx(c):
            P.dma("sync", xt[c % 2], T["x2"][c * 128:(c + 1) * 128, :], bxt[c % 2], reads=[k.b_x2], writes=[bxt[c % 2]])
        ldx(0)
        for c in range(NCH):
            if c + 1 < NCH:
                ldx(c + 1)
            X, bX = xt[c % 2], bxt[c % 2]
            W.flip()
            rms_rstd(P, X, W["ss"], W["bss"], W["junk"], Buf(), [bX])
            P.stt(W["tn"], X, W["ss"][:, 2:3], m[:, 1, :], ALU.mult, ALU.mult, [bX, W["bss"], bm], [W["btn"]])
            P.tt(hbs[c % 2], W["tn"], m[:, 0, :], ALU.add, [W["btn"], bm], [bhbs[c % 2]])
            P.dma("gpsimd", T["h1"][c * 128:(c + 1) * 128, :], hbs[c % 2], bhbs[c % 2], reads=[bhbs[c % 2]],
                  writes=[k.b_h1])
        P.barrier(bufs)
        P.release(bufs)

    P.reset_arena(k.keep)
    bufs = []
    NG = 128
    bp = Buf("s5prep")
    bld = [Buf("s5ld%d" % i) for i in range(6)]
    bufs += bld
    sg = P.tile([128, 2], F32)
    P.memset("gpsimd", sg[0:64, 0:1], -1.0, [bp])
    P.memset("gpsimd", sg[64:128, 0:1], 1.0, [bp])
    P.memset("gpsimd", sg[0:64, 1:2], 1.0, [bp])
    P.memset("gpsimd", sg[64:128, 1:2], -1.0, [bp])
    lr = P.tile([128, NG], F32)
    li = P.tile([128, NG], F32)
    dtv = P.tile([128, NG], F32)
    for hlf in range(2):
        P.dma("sync", lr[hlf * 64:(hlf + 1) * 64, :], T["s5_lam_re"][0].rearrange("d g p -> p (d g)"), bld[0],
              writes=[bld[0]], allow_slow_non_contiguous=True)
        P.dma("sync", li[hlf * 64:(hlf + 1) * 64, :], T["s5_lam_im"][0].rearrange("d g p -> p (d g)"), bld[1],
              writes=[bld[1]], allow_slow_non_contiguous=True)
    P.dma("sync", dtv, T["s5_log_step"][0].rearrange("d g -> (d g)").partition_broadcast(128), bld[2],
          writes=[bld[2]])
    X1b = P.tile_top([128, NG, 16], F32)
    X2b = P.tile_top([128, NG, 16], F32)
    X1c = P.tile([128, NG, 16], F32)
    X2c = P.tile([128, NG, 16], F32)
    bre = T["s5_b_re"][0].rearrange("d g p c -> p (d g) c")
    bim = T["s5_b_im"][0].rearrange("d g p c -> p (d g) c")
    P.dma("sync", X1b[0:64], bre, bld[3], writes=[bld[3]], allow_slow_non_contiguous=True)
    P.dma("sync", X1b[64:128], bim, bld[3], writes=[bld[3]], allow_slow_non_contiguous=True)
    P.dma("sync", X2b[0:64], bim, bld[3], writes=[bld[3]], allow_slow_non_contiguous=True)
    P.dma("sync", X2b[64:128], bre, bld[3], writes=[bld[3]], allow_slow_non_contiguous=True)
    CRI = P.tile_top([128, 16, 2, 64], F32)
    cre = T["s5_c_re"][0].rearrange("d g c p -> (d g c) p").rearrange("(t r) p -> r t p", r=128)
    cim = T["s5_c_im"][0].rearrange("d g c p -> (d g c) p").rearrange("(t r) p -> r t p", r=128)
    P.dma("sync", CRI[:, :, 0, :], cre, bld[4], writes=[bld[4]])
    P.dma("sync", CRI[:, :, 1, :], cim, bld[4], writes=[bld[4]])
    CIR = P.tile_top([128, 16, 2, 64], F32)
    P.dma("sync", CIR[:, :, 0, :], cim, bld[5], writes=[bld[5]])
    P.dma("sync", CIR[:, :, 1, :], cre, bld[5], writes=[bld[5]])
    for t in range(16):
        P.tr(ps[6][:, 0:128], CRI[:, t, :, :], k.identf, [bld[4], k.bconst], [pb[6]])
        P.cp("vector", X1c[:, t * 8:(t + 1) * 8, :], ps[6][:, 0:128].rearrange("p (g c) -> p g c", g=8),
             [pb[6]], [bp])
        P.tr(ps[7][:, 0:128], CIR[:, t, :, :], k.identf, [bld[5], k.bconst], [pb[7]])
        P.cp("vector", X2c[:, t * 8:(t + 1) * 8, :], ps[7][:, 0:128].rearrange("p (g c) -> p g c", g=8),
             [pb[7]], [bp])
    ldall = list(bld)
    t1 = P.tile([128, NG], F32)
    t2 = P.tile([128, NG], F32)
    t3 = P.tile([128, NG], F32)
    ti = P.tile([128, NG], I32)
    cosv = P.tile([128, NG], F32)
    sinv = P.tile([128, NG], F32)
    mag = P.tile([128, NG], F32)
    R_ = [bp] + ldall
    Wp = [bp]
    P.ts(lr, lr, -1e-4, None, ALU.min, None, R_, Wp)
    P.act(dtv, dtv, AF.Exp, R_, Wp)
    P.tt(t1, lr, dtv, ALU.mult, R_, Wp)
    P.act(mag, t1, AF.Exp, R_, Wp)
    P.tt(t1, li, dtv, ALU.mult, R_, Wp)
    P.ts(t1, t1, 1.0 / (2.0 * math.pi), None, ALU.mult, None, R_, Wp)

    def sin_turns(dst, off):
        P.ts(t2, t1, off, None, ALU.add, None, R_, Wp)
        P.cp("vector", ti, t2, R_, Wp)
        P.cp("vector", t3, ti, R_, Wp)
        P.tt(t2, t2, t3, ALU.subtract, R_, Wp)
        P.ts(t3, t2, 0.5, None, ALU.is_gt, None, R_, Wp)
        P.tt(t2, t2, t3, ALU.subtract, R_, Wp)
        P.ts(t3, t2, -0.5, None, ALU.is_lt, None, R_, Wp)
        P.tt(t2, t2, t3, ALU.add, R_, Wp)
        P.ts(t2, t2, 0.4999995, -0.4999995, ALU.min, ALU.max, R_, Wp)
        P.act(dst, t2, AF.Sin, R_, Wp, scale=2.0 * math.pi)
    sin_turns(sinv, 0.0)
    sin_turns(cosv, 0.25)
    NP_ = 23
    PWr = P.tile([128, NP_, NG], F32)
    PWi = P.tile_top([128, NP_, NG], F32)
    ar, ai = PWr[:, 8, :], PWi[:, 8, :]
    P.tt(ar, mag, cosv, ALU.mult, R_, Wp)
    P.tt(ai, mag, sinv, ALU.mult, R_, Wp)
    P.memset("vector", PWr[:, 7, :], 1.0, Wp)
    P.memset("vector", PWi[:, 7, :], 0.0, Wp)
    fr = P.tile([128, NG], F32)
    fi = P.tile([128, NG], F32)
    P.tt(t1, lr, lr, ALU.mult, R_, Wp)
    P.tt(t2, li, li, ALU.mult, R_, Wp)
    P.tt(t1, t1, t2, ALU.add, R_, Wp)
    P.recip(t1, t1, R_, Wp)
    P.ts(t2, ar, -1.0, None, ALU.add, None, R_, Wp)
    P.tt(t3, t2, lr, ALU.mult, R_, Wp)
    P.tt(fr, ai, li, ALU.mult, R_, Wp)
    P.tt(fr, fr, t3, ALU.add, R_, Wp)
    P.tt(fr, fr, t1, ALU.mult, R_, Wp)
    P.tt(t3, ai, lr, ALU.mult, R_, Wp)
    P.tt(fi, t2, li, ALU.mult, R_, Wp)
    P.tt(fi, t3, fi, ALU.subtract, R_, Wp)
    P.tt(fi, fi, t1, ALU.mult, R_, Wp)
    P.tt(t1, mag, mag, ALU.mult, R_, Wp)
    P.recip(t1, t1, R_, Wp)
    P.tt(PWr[:, 6, :], ar, t1, ALU.mult, R_, Wp)
    P.tt(t2, ai, t1, ALU.mult, R_, Wp)
    P.ts(PWi[:, 6, :], t2, -1.0, None, ALU.mult, None, R_, Wp)

    def cmul(or_, oi_, xr, xi, yr, yi):
        P.tt(t1, xr, yr, ALU.mult, R_, Wp)
        P.tt(t2, xi, yi, ALU.mult, R_, Wp)
        P.tt(t3, xr, yi, ALU.mult, R_, Wp)
        P.tt(cosv, xi, yr, ALU.mult, R_, Wp)
        P.tt(or_, t1, t2, ALU.subtract, R_, Wp)
        P.tt(oi_, t3, cosv, ALU.add, R_, Wp)
    for n in range(2, 16):
        cmul(PWr[:, 7 + n, :], PWi[:, 7 + n, :], PWr[:, 6 + n, :], PWi[:, 6 + n, :], ar, ai)
    for n in range(2, 8):
        cmul(PWr[:, 7 - n, :], PWi[:, 7 - n, :], PWr[:, 8 - n, :], PWi[:, 8 - n, :], PWr[:, 6, :], PWi[:, 6, :])
    KSr = P.tile([128, 10, NG], F32)
    KSi = P.tile([128, 10, NG], F32)
    KS2 = P.tile([128, 10, NG], F32)
    P.cp("vector", KSr[:, 0, :], PWr[:, 15, :], R_, Wp)
    P.cp("vector", KSi[:, 0, :], PWi[:, 15, :], R_, Wp)
    for kk in range(1, 10):
        cmul(KSr[:, kk, :], KSi[:, kk, :], KSr[:, kk - 1, :], KSi[:, kk - 1, :], KSr[:, kk - 1, :], KSi[:, kk - 1, :])
    P.ts(KS2, KSi, sg[:, 1:2], None, ALU.mult, None, R_, Wp)
    C2q = P.tile([128, 8, NG], F32)
    C1p = P.tile([128, 16, NG], F32)
    C2p = P.tile([128, 16, NG], F32)
    P.ts(C2q, PWi[:, 0:8, :], sg[:, 0:1], None, ALU.mult, None, R_, Wp)
    P.ts(C1p, PWr[:, 7:23, :], sg[:, 1:2], None, ALU.mult, None, R_, Wp)
    P.ts(C2p, PWi[:, 7:23, :], -1.0, None, ALU.mult, None, R_, Wp)
    X1B = P.tile([128, NG, 16], F32)
    X2B = P.tile([128, NG, 16], F32)
    tb1 = P.tile_top([128, NG, 16], F32)
    P.ts(t1, fi, sg[:, 0:1], None, ALU.mult, None, R_, Wp)
    P.ts(t2, fi, sg[:, 1:2], None, ALU.mult, None, R_, Wp)
    P.tt(X1B, X1b, bcast_last(fr, 16), ALU.mult, R_, Wp)
    P.tt(tb1, X2b, bcast_last(t1, 16), ALU.mult, R_, Wp)
    P.tt(X1B, X1B, tb1, ALU.add, R_, Wp)
    P.tt(X2B, X2b, bcast_last(fr, 16), ALU.mult, R_, Wp)
    P.tt(tb1, X1b, bcast_last(t2, 16), ALU.mult, R_, Wp)
    P.tt(X2B, X2B, tb1, ALU.add, R_, Wp)
    msk = P.tile([128, 2, 128], F32)
    bmsk = Buf("msk"); bufs.append(bmsk)
    P.dma("sync", msk, T["s5_mask"].rearrange("d p n -> p d n"), bmsk, writes=[bmsk])

    P.barrier(bufs)
    P.atop = P.asize
    Qm = [P.tile([128, 2, 8, 128], BF16) for _ in range(2)]
    Pm = [P.tile([128, 2, 8, 128], BF16) for _ in range(2)]
    Po = [P.tile([128, 2, 8, 128], BF16) for _ in range(2)]
    bgen = [Buf("gen0"), Buf("gen1")]
    g1 = P.tile([128, 8, 16], F32); g2 = P.tile([128, 8, 16], F32)
    bg12 = Buf("g12")
    hcm = P.tile([128, 8, 8, 128], BF16)
    bhcm = Buf("hcm"); bufs.append(bhcm)
    hcg = P.tile([128, 8, 8, 128], BF16)
    bhcg = Buf("hcg")
    ycm = P.tile([128, 8, 8, 128], F32)
    bycm = Buf("ycm"); bufs.append(bycm)
    U = [P.tile([128, 1024], BF16) for _ in range(2)]; bU = [Buf("U0"), Buf("U1")]
    TT = [[P.tile([128, 128], BF16) for _ in range(2)] for _ in range(2)]
    bTT = [[Buf() for _ in range(2)] for _ in range(2)]
    WT = [[P.tile([128, 128], BF16) for _ in range(2)] for _ in range(2)]
    bWT = [[Buf() for _ in range(2)] for _ in range(2)]
    Mk = [[P.tile([128, 10, 128], BF16) for _ in range(2)] for _ in range(2)]
    bMk = [[Buf() for _ in range(2)] for _ in range(2)]
    NTJ = 4
    tJ = [P.tile([128, 128], F32) for _ in range(NTJ)]; btJ = [Buf() for _ in range(NTJ)]
    Ib = [P.tile([128, 1026], BF16) for _ in range(2)]; bIb = [Buf("Ib0"), Buf("Ib1")]
    for d in range(2):
        P.memset("gpsimd", Ib[d], 0.0, [bIb[d]])
    chain_ps = [k.psbig[:, (2 + 2 * d) * 512:(4 + 2 * d) * 512] for d in range(2)]
    chain_pb = [[pb[2 + 2 * d], pb[3 + 2 * d]] for d in range(2)]
    cp_eng = ["scalar", "vector"]
    tjc = [0]

    def batch_thunks(gb):
        q = gb % 2
        th = []

        def ld():
            for blk in range(8):
                P.dma("sync", hcm[:, blk, :, :],
                      T["h1"][blk * 1024:(blk + 1) * 1024, gb * 128:(gb + 1) * 128].rearrange("(c i) f -> c i f", i=8),
                      bhcm, reads=[k.b_h1], writes=[bhcm])
        th.append(ld)
        for blk in range(8):
            th.append(lambda blk=blk: P.cp("gpsimd", hcg[:, blk, :, :].rearrange("p g (i c) -> p i g c", i=8),
                                           hcm[:, blk, :, :].rearrange("p i (g c) -> p i g c", g=8), [bhcm], [bhcg]))
        for d in range(2):
            gsl = slice(d * 64 + gb * 8, d * 64 + gb * 8 + 8)
            for jj in range(8):
                slot = jj if d == 0 else 7 - jj
                ssl = slice(slot * 16, slot * 16 + 16)

                def gq(d=d, gsl=gsl, jj=jj, ssl=ssl):
                    P.tt(g1, X1B[:, gsl, :], bcast_last(PWr[:, 7 - jj, gsl], 16), ALU.mult, R_, [bg12])
                    P.tt(g2, X2B[:, gsl, :], bcast_last(C2q[:, 7 - jj, gsl], 16), ALU.mult, R_, [bg12])
                    P.tt(Qm[q][:, d, :, ssl], g1, g2, ALU.add, [bg12], [bgen[q]])

                def gp(d=d, gsl=gsl, jj=jj, ssl=ssl):
                    P.tt(g1, X1c[:, gsl, :], bcast_last(C1p[:, jj, gsl], 16), ALU.mult, R_, [bg12], e="gpsimd")
                    P.tt(g2, X2c[:, gsl, :], bcast_last(C2p[:, jj, gsl], 16), ALU.mult, R_, [bg12], e="gpsimd")
                    P.tt(Pm[q][:, d, :, ssl], g1, g2, ALU.add, [bg12], [bgen[q]], e="gpsimd")

                def go(d=d, gsl=gsl, jj=jj, ssl=ssl):
                    P.tt(g1, X1c[:, gsl, :], bcast_last(C1p[:, jj + 8, gsl], 16), ALU.mult, R_, [bg12], e="gpsimd")
                    P.tt(g2, X2c[:, gsl, :], bcast_last(C2p[:, jj + 8, gsl], 16), ALU.mult, R_, [bg12], e="gpsimd")
                    P.tt(Po[q][:, d, :, ssl], g1, g2, ALU.add, [bg12], [bgen[q]], e="gpsimd")
                th += [gq, gp, go]
        return th

    def group_thunks(g):
        gb, gl = divmod(g, 8)
        q = gb % 2
        st_ = g % 2
        th = []
        pU = ps[0].bitcast(BF16)
        for blk in range(8):
            th.append(lambda blk=blk: P.tr(pU[:, blk * 128:(blk + 1) * 128], hcg[:, blk, gl, :], k.ident,
                                           [bhcg, k.bconst], [pb[0]]))
        th.append(lambda: P.cp("scalar", U[st_], pU, [pb[0]], [bU[st_]]))
        pW = ps[1].bitcast(BF16)
        for d in range(2):
            gd = d * 64 + g

            def tw(d=d):
                P.mm(ps[1][:, 0:128], Qm[q][:, d, gl, :], Pm[q][:, d, gl, :], True, True, [bgen[q]], [pb[1]])
                P.tt(TT[st_][d], ps[1][:, 0:128], msk[:, d, :], ALU.mult, [pb[1], bmsk], [bTT[st_][d]])
                P.tr(pW[:, 512:640], Qm[q][:, d, gl, :], k.ident, [bgen[q], k.bconst], [pb[1]])
                P.cp("scalar", WT[st_][d], pW[:, 512:640], [pb[1]], [bWT[st_][d]])
            th.append(tw)
            for kk in range(10):
                def mk(d=d, kk=kk, gd=gd):
                    j_ = tjc[0] % NTJ
                    tjc[0] += 1
                    P.act(tJ[j_], k.jswap, AF.Copy, [k.bconst] + R_, [btJ[j_]], scale=KS2[:, kk, gd:gd + 1])
                    P.stt(Mk[st_][d][:, kk, :], k.identf, KSr[:, kk, gd:gd + 1], tJ[j_], ALU.mult, ALU.add,
                          [k.bconst, btJ[j_]] + R_, [bMk[st_][d]])
                th.append(mk)
        return th

    pending = []

    def pop(n):
        for _ in range(min(n, len(pending))):
            pending.pop(0)()

    pending += batch_thunks(0)
    pending += group_thunks(0)
    for g in range(64):
        gb, gl = divmod(g, 8)
        q = gb % 2
        st_ = g % 2
        pop(len(pending))
        if g + 1 < 64:
            if gl == 7:
                pending += batch_thunks(gb + 1)
            pending += group_thunks(g + 1)
        for d in range(2):
            for hf in range(2):
                P.mm(chain_ps[d][:, hf * 512:(hf + 1) * 512], WT[st_][d], U[st_][:, hf * 512:(hf + 1) * 512],
                     True, True, [bWT[st_][d], bU[st_]], [chain_pb[d][hf]])
        for d in range(2):
            P.cp(cp_eng[d], Ib[d][:, 1:1025], chain_ps[d], chain_pb[d], [bIb[d]])
        pop(4)
        for kk in range(10):
            s_ = 1 << kk
            for d in range(2):
                lo, hi = (s_, 1024) if d == 0 else (0, 1024 - s_)
                for (a_, b_) in ((0, 512), (512, 1024)):
                    l2, h2 = max(lo, a_), min(hi, b_)
                    if l2 >= h2:
                        continue
                    src0 = 1 + l2 - s_ if d == 0 else 1 + l2 + s_
                    P.mm(chain_ps[d][:, l2:h2], Mk[st_][d][:, kk, :], Ib[d][:, src0:src0 + (h2 - l2)],
                         False, True, [bMk[st_][d], bIb[d]], [chain_pb[d][a_ // 512]])
            for d in range(2):
                lo, hi = (s_, 1024) if d == 0 else (0, 1024 - s_)
                P.cp(cp_eng[d], Ib[d][:, 1 + lo:1 + hi], chain_ps[d][:, lo:hi], chain_pb[d], [bIb[d]])
            pop(5)
        for blk in range(8):
            bank = 6 + blk // 4
            osl = slice((blk % 4) * 128, (blk % 4) * 128 + 128)
            csl = slice(blk * 128, (blk + 1) * 128)
            P.mm(ps[bank][:, osl], U[st_][:, csl], TT[st_][0], True, False, [bU[st_], bTT[st_][0]], [pb[bank]])
            P.mm(ps[bank][:, osl], Ib[0][:, blk * 128:blk * 128 + 128], Po[q][:, 0, gl, :], False, False,
                 [bIb[0], bgen[q]], [pb[bank]])
            P.mm(ps[bank][:, osl], U[st_][:, csl], TT[st_][1], False, False, [bU[st_], bTT[st_][1]], [pb[bank]])
            P.mm(ps[bank][:, osl], Ib[1][:, blk * 128 + 2:blk * 128 + 130], Po[q][:, 1, gl, :], False, True,
                 [bIb[1], bgen[q]], [pb[bank]])
        for hb_ in range(2):
            P.cp("vector" if hb_ == 0 else "scalar",
                 ycm[:, hb_ * 4:hb_ * 4 + 4, :, gl * 16:(gl + 1) * 16],
                 ps[6 + hb_].rearrange("p (b i c) -> p b i c", b=4, i=8), [pb[6 + hb_]], [bycm])
        pop(6)
        if gl == 7:
            for blk in range(8):
                P.dma("gpsimd",
                      T["ys5"][blk * 1024:(blk + 1) * 1024, gb * 128:(gb + 1) * 128].rearrange("(c i) f -> c i f", i=8),
                      ycm[:, blk, :, :], bycm, reads=[bycm], writes=[k.b_ys5])
    P.barrier(bufs)
    P.release(bufs)

    P.reset_arena(k.keep)
    bufs = []
    m, bm = load_mods(P, k, T, 1, 0, bufs)
    W = norm_work(P, bufs)
    Wa = P.tile([128, 8, 1024], BF16); bWa = [Buf() for _ in range(8)]
    Wb = P.tile([128, 8, 1024], BF16); bWb = [Buf() for _ in range(8)]
    stg = [P.tile([128, 1024], F32) for _ in range(2)]
    bstg = [Buf("stg%d" % i) for i in range(2)]
    bufs += bstg
    si = [0]
    for kt in range(8):
        s_ = si[0] % 2
        si[0] += 1
        P.dma("sync", stg[s_], T["w_glu_a"][0, kt * 128:(kt + 1) * 128, :], bstg[s_], writes=[bstg[s_]])
        P.tt(Wa[:, kt, :], stg[s_], m[:, 2, :], ALU.mult, [bstg[s_], bm], [bWa[kt]], e="gpsimd")
    load_w_bf16(P, Wb, bWb, T["w_glu_b"][0], 8, 1024, stg, bstg, si)
    dsk = P.tile([128, D], F32); bdsk = Buf("dsk"); bufs.append(bdsk)
    P.dma("sync", dsk, T["s5_d"][0].partition_broadcast(128), bdsk, writes=[bdsk])
    Gd = P.tile([128, D], F32)
    Sd = P.tile([128, D], F32)
    bgs = Buf("GdSd")
    P.tt(Gd, m[:, 1, :], dsk, ALU.mult, [bm, bdsk], [bgs])
    P.tt(Sd, m[:, 0, :], dsk, ALU.mult, [bm, bdsk], [bgs])
    xt = [P.tile([128, D], F32) for _ in range(2)]
    bxt = [Buf("xt%d" % i) for i in range(2)]
    yt = [P.tile([128, D], F32) for _ in range(2)]
    byt = [Buf("yt%d" % i) for i in range(2)]
    bufs += bxt + byt
    zb = [P.tile([128, D], BF16) for _ in range(2)]; bzb = [Buf() for _ in range(2)]
    zT = [P.tile([128, 8, 128], BF16) for _ in range(2)]; bzT = [Buf() for _ in range(2)]
    sgm = [P.tile([128, D], F32) for _ in range(2)]; bsgm = [Buf() for _ in range(2)]
    banks = [(1, (1, 2), (3, 4)), (5, (5, 6), (7, 0))]

    def ldp(c):
        P.dma("sync", xt[c % 2], T["x2"][c * 128:(c + 1) * 128, :], bxt[c % 2], reads=[k.b_x2], writes=[bxt[c % 2]])
        P.dma("sync", yt[c % 2], T["ys5"][c * 128:(c + 1) * 128, :], byt[c % 2], reads=[k.b_ys5], writes=[byt[c % 2]])

    def front(c):
        i = c % 2
        X, bX, Y, bY = xt[i], bxt[i], yt[i], byt[i]
        W.flip()
        rms_rstd(P, X, W["ss"], W["bss"], W["junk"], Buf(), [bX])
        P.stt(W["tn"], X, W["ss"][:, 2:3], Gd, ALU.mult, ALU.mult, [bX, W["bss"], bgs], [W["btn"]])
        P.tt(Y, Y, W["tn"], ALU.add, [bY, W["btn"]], [bY])
        P.tt(Y, Y, Sd, ALU.add, [bY, bgs], [bY])
        P.act(zb[i], Y, AF.Gelu_apprx_tanh, [bY], [bzb[i]])
        transpose8(P, k, zb[i], bzb[i], zT[i], bzT[i], banks[i][0])

    def back(c):
        i = c % 2
        X, bX = xt[i], bxt[i]
        ba, bb = banks[i][1], banks[i][2]
        for hf in range(2):
            for kt in range(8):
                P.mm(ps[ba[hf]], zT[i][:, kt, :], Wa[:, kt, hf * 512:(hf + 1) * 512], kt == 0, kt == 7,
                     [bzT[i], bWa[kt]], [pb[ba[hf]]])
            for kt in range(8):
                P.mm(ps[bb[hf]], zT[i][:, kt, :], Wb[:, kt, hf * 512:(hf + 1) * 512], kt == 0, kt == 7,
                     [bzT[i], bWb[kt]], [pb[bb[hf]]])
        for hf in range(2):
            hsl = slice(hf * 512, (hf + 1) * 512)
            P.act(sgm[i][:, hsl], ps[bb[hf]], AF.Sigmoid, [pb[bb[hf]]], [bsgm[i]])
            P.tt(sgm[i][:, hsl], ps[ba[hf]], sgm[i][:, hsl], ALU.mult, [pb[ba[hf]], bsgm[i]], [bsgm[i]])
            P.tt(X[:, hsl], X[:, hsl], sgm[i][:, hsl], ALU.add, [bX, bsgm[i]], [bX])
        P.dma("gpsimd", T["x3"][c * 128:(c + 1) * 128, :], X, bX, reads=[bX], writes=[k.b_x3])

    ldp(0)
    ldp(1)
    front(0)
    for c in range(NCH):
        if c + 1 < NCH:
            front(c + 1)
        back(c)
        if c + 2 < NCH:
            ldp(c + 2)
    P.barrier(bufs)
    P.release(bufs)


def host_tables():
    L = 128
    nh = 4
    dh = 128
    inv = (10000.0 ** (-np.arange(0, dh, 2, dtype=np.float32) / dh)).astype(np.float32)
    ang = (np.arange(S, dtype=np.float32)[:, None] * inv[None, :]).astype(np.float32)
    rot = np.stack([np.cos(ang), np.sin(ang)], axis=1).astype(np.float32)
    rot_tab = rot.reshape(NCH, 128, 2, 64)
    log_g = np.log1p(-np.exp2(-5.0 - np.arange(nh, dtype=np.float32))).astype(np.float32)
    idx = np.arange(L, dtype=np.float32)
    sc = dh ** -0.5
    dist = np.abs(idx[:, None] - idx[None, :])
    dmat = np.exp(log_g[:, None, None] * dist).astype(np.float32)
    tab = np.zeros((5, 128, 512), np.float32)
    tab[0] = (dmat.transpose(2, 0, 1) * sc).reshape(128, 512)
    wqf = np.exp(log_g[:, None] * (idx + 1.0)[None, :]) * sc
    wqb = np.exp(log_g[:, None] * (L - idx)[None, :]) * sc
    tab[1] = np.broadcast_to(wqf.reshape(1, 512), (128, 512))
    tab[2] = np.broadcast_to(wqb.reshape(1, 512), (128, 512))
    gl = np.exp(log_g * L)
    tab[3] = np.broadcast_to(np.repeat(gl, 128).reshape(1, 512), (128, 512))
    tab[4, :, 0:4] = np.exp(log_g[None, :] * (L - 1.0 - idx)[:, None])
    tab[4, :, 4:8] = np.exp(log_g[None, :] * idx[:, None])
    ii = np.arange(128) // 16
    mask = np.stack([(ii[None, :] >= ii[:, None]), (ii[None, :] <= ii[:, None])]).astype(np.float32)
    return {"rot_tab": rot_tab.astype(np.float32), "ret_tab": tab.astype(np.float32), "s5_mask": mask}


IN_SHAPES = {
    "x": [S, D], "c": [1, D], "norm_g": [2, 2, D], "ada_w": [2, D, 6 * D], "ada_b": [2, 6 * D],
    "w_in": [1, D, 3072], "conv_w": [1, 31, 512], "conv_b": [1, 512], "cln_g": [1, 512], "cln_b": [1, 512],
    "w_out": [1, D, D], "s5_lam_re": [1, 2, 64, 64], "s5_lam_im": [1, 2, 64, 64], "s5_log_step": [1, 2, 64],
    "s5_b_re": [1, 2, 64, 64, 16], "s5_b_im": [1, 2, 64, 64, 16], "s5_c_re": [1, 2, 64, 16, 64],
    "s5_c_im": [1, 2, 64, 16, 64], "s5_d": [1, D], "w_glu_a": [1, D, D], "w_glu_b": [1, D, D],
    "w_fc1": [2, D, DFF], "w_fc2": [2, DFF, D], "norm_f": [D],
    "rot_tab": [NCH, 128, 2, 64], "ret_tab": [5, 128, 512], "s5_mask": [2, 128, 128],
}


def build(phases, debug=(), ext_in=()):
    nc = bass.Bass("TRN2", target_bir_lowering=False)
    T = {}
    for nm, shp in IN_SHAPES.items():
        T[nm] = nc.dram_tensor(nm, shp, F32, kind="ExternalInput").ap()

    def scratch(nm, shp, dt):
        kind = "ExternalOutput" if nm in debug else ("ExternalInput" if nm in ext_in else "Internal")
        T[nm] = nc.dram_tensor(nm, shp, dt, kind=kind).ap()
    scratch("modrows", [2, 6, D], F32)
    scratch("hT0", [128, 8, S + 2 * PADC], BF16)
    scratch("sb_all", [NCH, 128, 512], BF16)
    scratch("x1", [S, D], F32)
    scratch("x2", [S, D], F32)
    scratch("h1", [S, D], BF16)
    scratch("ys5", [S, D], F32)
    scratch("x3", [S, D], F32)
    T["out"] = nc.dram_tensor("out", [S, D], F32, kind="ExternalOutput").ap()
    with ExitStack() as es:
        P = Prog(nc, es)
        P.init_arena(204 * 1024)
        k = K()
        for nm in ["modrows", "hT0", "sb", "x1", "x2", "h1", "ys5", "x3", "out", "xin"]:
            setattr(k, "b_" + nm, Buf(nm))
        setup_consts(P, k)
        if "mod" in phases:
            phase_mod(P, k, T)
        if "l0" in phases:
            phase_l0(P, k, T)
        if "mlp0" in phases:
            phase_mlp(P, k, T, 0, T["x1"], k.b_x1, T["x2"], k.b_x2, False, h1_out=("s5" in phases))
        if "s5" in phases:
            phase_s5(P, k, T, do_pre=("mlp0" not in phases))
        if "mlp1" in phases:
            phase_mlp(P, k, T, 1, T["x3"], k.b_x3, T["out"], k.b_out, True)
        P.barrier([], skip=("sync",))
        P.emit()
        k.nops = {e: len(P.ops[e]) for e in ENGS}
    return nc, k


def make_in_map(inputs, b, tabs):
    m = {}
    for nm in IN_SHAPES:
        if nm in tabs:
            m[nm] = tabs[nm]
        elif nm == "x":
            m[nm] = np.ascontiguousarray(inputs["x"][b])
        elif nm == "c":
            m[nm] = np.ascontiguousarray(inputs["c"][b:b + 1])
        else:
            m[nm] = np.ascontiguousarray(np.asarray(inputs[nm], dtype=np.float32))
    return m


def kernel(**inputs):
    inputs = {kk: np.asarray(v) for kk, v in inputs.items()}
    tabs = host_tables()
    nc, _ = build(["mod", "l0", "mlp0", "s5", "mlp1"])
    in_maps = [make_in_map(inputs, b, tabs) for b in range(8)]
    res = run_bass_kernel_spmd(nc, in_maps, core_ids=list(range(8)))
    return np.stack([np.asarray(r["out"], dtype=np.float32) for r in res.results], axis=0)
```

```python
import math
from contextlib import ExitStack

import numpy as np
import ml_dtypes
import concourse.bass as bass
import concourse.mybir as mybir
from concourse.bass_utils import run_bass_kernel_spmd

F32 = mybir.dt.float32
BF16 = mybir.dt.bfloat16
I32 = mybir.dt.int32
ALU = mybir.AluOpType
AF = mybir.ActivationFunctionType

ENGS = ["tensor", "vector", "scalar", "gpsimd", "sync"]
S = 8192
D = 1024
DFF = 4096
EPS = 1e-6
NCH = 64
PADC = 16


class Buf:
    __slots__ = ("name", "w", "r", "dsem", "dcnt")

    def __init__(self, name=""):
        self.name = name
        self.w = None
        self.r = []
        self.dsem = None
        self.dcnt = 0


class Prog:
    def __init__(self, nc, es):
        self.nc = nc
        self.es = es
        self.ops = {e: [] for e in ENGS}
        self.cnt = {e: 0 for e in ENGS}
        self.sem = {e: es.enter_context(nc.semaphore("se_" + e)) for e in ENGS}
        self.seen = {e: {} for e in ENGS}
        self.dsems = []
        self.free_dsems = []
        self.arena = None
        self.aoff = 0

    def init_arena(self, nbytes):
        self.arena = self.es.enter_context(self.nc.sbuf_tensor("arena", [128, nbytes // 2], BF16))
        self.asize = nbytes
        self.aoff = 0

    def reset_arena(self, keep=0):
        self.aoff = keep
        self.atop = self.asize

    def tile_top(self, shape, dt):
        n = 1
        for s_ in shape[1:]:
            n *= s_
        nb = (n * 4 + 63) // 64 * 64
        self.atop -= nb
        assert self.atop >= self.aoff
        ap = self.arena[0:shape[0], self.atop // 2:(self.atop + n * 4) // 2].bitcast(dt)
        if len(shape) == 3:
            ap = ap.rearrange("p (a b) -> p a b", a=shape[1])
        elif len(shape) == 4:
            ap = ap.rearrange("p (a b c) -> p a b c", a=shape[1], b=shape[2])
        return ap

    def tile(self, shape, dt):
        esz = 4 if dt in (F32, I32) else 2
        n = 1
        for s_ in shape[1:]:
            n *= s_
        nb = (n * esz + 63) // 64 * 64
        assert self.aoff + nb <= getattr(self, "atop", self.asize), ("SBUF arena overflow", self.aoff, nb, self.asize)
        ap = self.arena[0:shape[0], self.aoff // 2:(self.aoff + n * esz) // 2]
        self.aoff += nb
        if esz == 4:
            ap = ap.bitcast(dt)
        if len(shape) == 3:
            ap = ap.rearrange("p (a b) -> p a b", a=shape[1])
        elif len(shape) == 4:
            ap = ap.rearrange("p (a b c) -> p a b c", a=shape[1], b=shape[2])
        return ap

    def _dsem(self, b):
        if b.dsem is None:
            if self.free_dsems:
                b.dsem, b.dcnt = self.free_dsems.pop()
            else:
                s_ = self.es.enter_context(self.nc.semaphore("sd%d" % len(self.dsems)))
                self.dsems.append(s_)
                b.dsem, b.dcnt = s_, 0
        return b.dsem

    def release(self, bufs):
        for b in bufs:
            if b.dsem is not None:
                self.free_dsems.append((b.dsem, b.dcnt))
                b.dsem = None

    def _waits(self, e, reads, writes):
        evs = []
        for b in reads:
            if b.w is not None:
                evs.append(b.w)
        for b in writes:
            if b.w is not None:
                evs.append(b.w)
            evs.extend(b.r)
        out = {}
        for (s_, v) in evs:
            if e == "tensor" and s_ is self.sem["tensor"]:
                continue
            if self.seen[e].get(s_.name, -1) >= v:
                continue
            if out.get(s_.name, (None, -1))[1] < v:
                out[s_.name] = (s_, v)
        for (s_, v) in out.values():
            self.seen[e][s_.name] = v
        return list(out.values())

    def op(self, e, fn, reads=(), writes=()):
        waits = self._waits(e, reads, writes)
        self.cnt[e] += 1
        ev = (self.sem[e], self.cnt[e])
        for b in reads:
            b.r.append(ev)
        for b in writes:
            b.w = ev
            b.r = []
        self.ops[e].append((waits, fn, ev[0], 1))
        return ev

    def dma(self, e, out, in_, sbuf_buf, reads=(), writes=(), **kw):
        waits = self._waits(e, reads, writes)
        s_ = self._dsem(sbuf_buf)
        sbuf_buf.dcnt += 16
        ev = (s_, sbuf_buf.dcnt)
        for b in reads:
            b.r.append(ev)
        for b in writes:
            b.w = ev
            b.r = []
        self.ops[e].append((waits, (lambda eng: eng.dma_start(out=out, in_=in_, **kw)), s_, 16))
        return ev

    def barrier(self, all_bufs=(), skip=()):
        evs = [(self.sem[f], self.cnt[f]) for f in ENGS if self.cnt[f] > 0]
        live = {}
        for b in all_bufs:
            if b.dsem is not None:
                live[b.dsem.name] = (b.dsem, b.dcnt)
        for (s_, c_) in self.free_dsems:
            live.setdefault(s_.name, (s_, c_))
        evs += [v for v in live.values() if v[1] > 0]
        for e in ENGS:
            if e in skip:
                continue
            w = []
            for (s_, v) in evs:
                if self.seen[e].get(s_.name, -1) >= v:
                    continue
                self.seen[e][s_.name] = v
                w.append((s_, v))
            if w:
                self.ops[e].append((w, None, None, 0))

    def emit(self):
        with self.nc.Block() as block:
            def mk(e):
                def body(eng):
                    for (waits, fn, s_, inc) in self.ops[e]:
                        for (ws, wv) in waits:
                            eng.wait_ge(ws, wv)
                        if fn is not None:
                            fn(eng).then_inc(s_, inc)
                return body
            block.tensor(mk("tensor"))
            block.vector(mk("vector"))
            block.scalar(mk("scalar"))
            block.gpsimd(mk("gpsimd"))
            block.sync(mk("sync"))

    def act(self, out, in_, func, r, w, **kw):
        return self.op("scalar", lambda a: a.activation(out=out, in_=in_, func=func, **kw), r, w)

    def tt(self, out, in0, in1, op, r, w, e="vector"):
        return self.op(e, lambda v: v.tensor_tensor(out=out, in0=in0, in1=in1, op=op), r, w)

    def ts(self, out, in0, s1, s2, op0, op1, r, w, e="vector"):
        if s2 is None:
            return self.op(e, lambda v: v.tensor_scalar(out=out, in0=in0, scalar1=s1, scalar2=None,
                                                        op0=op0), r, w)
        return self.op(e, lambda v: v.tensor_scalar(out=out, in0=in0, scalar1=s1, scalar2=s2,
                                                    op0=op0, op1=op1), r, w)

    def stt(self, out, in0, sc, in1, op0, op1, r, w):
        return self.op("vector", lambda v: v.scalar_tensor_tensor(out=out, in0=in0, scalar=sc, in1=in1,
                                                                  op0=op0, op1=op1), r, w)

    def cp(self, e, out, in_, r, w):
        if e == "scalar":
            return self.act(out, in_, AF.Copy, r, w)
        return self.op(e, lambda v: v.tensor_copy(out=out, in_=in_), r, w)

    def mm(self, out, lhsT, rhs, start, stop, r, w):
        return self.op("tensor", lambda t: t.matmul(out, lhsT=lhsT, rhs=rhs, start=start, stop=stop), r, w)

    def tr(self, out, in_, ident, r, w):
        return self.op("tensor", lambda t: t.transpose(out=out, in_=in_, identity=ident), r, w)

    def memset(self, e, ap, val, w):
        return self.op(e, lambda g: g.memset(ap, val), (), w)

    def recip(self, out, in_, r, w):
        return self.op("vector", lambda v: v.reciprocal(out=out, in_=in_), r, w)


class K:
    pass


def bcast_mid(ap2, n):
    return ap2.unsqueeze(1).to_broadcast([ap2.shape[0], n, ap2.shape[1]])


def bcast_last(ap2, n):
    return ap2.unsqueeze(2).to_broadcast([ap2.shape[0], ap2.shape[1], n])


def setup_consts(P, k):
    k.identf = P.tile([128, 128], F32)
    k.ident = P.tile([128, 128], BF16)
    k.jswap = P.tile([128, 128], F32)
    k.bconst = Buf("const")
    b = k.bconst
    P.memset("gpsimd", k.identf, 1.0, [b])
    P.op("gpsimd", lambda g: g.affine_select(out=k.identf, in_=k.identf, pattern=[[-1, 128]],
                                             compare_op=ALU.is_equal, fill=0.0, base=0,
                                             channel_multiplier=1), [b], [b])
    P.cp("gpsimd", k.ident, k.identf, [b], [b])
    P.memset("gpsimd", k.jswap, 0.0, [b])
    P.cp("gpsimd", k.jswap[0:64, 64:128], k.identf[0:64, 0:64], [b], [b])
    P.cp("gpsimd", k.jswap[64:128, 0:64], k.identf[64:128, 64:128], [b], [b])
    k.psbig = P.es.enter_context(P.nc.psum_tensor("psbig", [128, 4096], F32))
    k.ps = [k.psbig[:, i * 512:(i + 1) * 512] for i in range(8)]
    k.pb = [Buf("psb%d" % i) for i in range(8)]
    k.keep = P.aoff


def rms_rstd(P, xin, ss, bss, junk, bjunk, rx, col=0, n=D):
    P.act(junk, xin, AF.Square, rx, [bjunk, bss], accum_out=ss[:, col:col + 1])
    P.act(ss[:, col + 1:col + 2], ss[:, col:col + 1], AF.Sqrt, [bss], [bss], scale=1.0 / n, bias=EPS)
    P.recip(ss[:, col + 2:col + 3], ss[:, col + 1:col + 2], [bss], [bss])


_EPS_AP = {}


def k_eps(P):
    return _EPS_AP["ap"]


def load_w_bf16(P, dst, bdst_list, w, kts, ncols, stg, bstg, si, col0=0, scale_cols=None):
    cw = min(ncols, stg[0].shape[1])
    for kt in range(kts):
        for c0 in range(0, ncols, cw):
            s_ = si[0] % 2
            si[0] += 1
            P.dma("sync", stg[s_][:, 0:cw], w[kt * 128:(kt + 1) * 128, col0 + c0:col0 + c0 + cw], bstg[s_],
                  writes=[bstg[s_]])
            P.cp(("gpsimd", "vector", "scalar")[si[0] % 3], dst[:, kt, c0:c0 + cw], stg[s_][:, 0:cw], [bstg[s_]],
                 [bdst_list[kt]])


def phase_mod(P, k, T):
    P.reset_arena(k.keep)
    bufs = []
    cT = P.tile([128, 8], F32)
    bc = Buf("cT"); bufs.append(bc)
    P.dma("sync", cT, T["c"][0, :].rearrange("(kt p) -> p kt", p=128), bc, writes=[bc],
          allow_slow_non_contiguous=True)
    P.act(cT, cT, AF.Silu, [bc], [bc])
    adab = P.tile([1, 2 * 6144], F32)
    ng = P.tile([1, 4 * 1024], F32)
    bsm = Buf("small"); bufs.append(bsm)
    P.dma("sync", adab, T["ada_b"].rearrange("l n -> (l n)").unsqueeze(0), bsm, writes=[bsm])
    P.dma("sync", ng, T["norm_g"].rearrange("l t n -> (l t n)").unsqueeze(0), bsm, writes=[bsm])
    MR = P.tile([1, 2 * 6144], F32)
    bmr = Buf("MR"); bufs.append(bmr)
    NB = 2
    aw = [P.tile([128, 8, 512], F32) for _ in range(NB)]
    baw = [Buf("aw%d" % i) for i in range(NB)]
    bufs += baw
    it = 0
    for l in range(2):
        for cb in range(12):
            s_ = it % NB
            P.dma("sync", aw[s_], T["ada_w"][l, :, cb * 512:(cb + 1) * 512].rearrange("(kt p) n -> p kt n", p=128),
                  baw[s_], writes=[baw[s_]])
            bank = it % 2
            for kt in range(8):
                P.mm(k.ps[bank][0:1, :], cT[:, kt:kt + 1], aw[s_][:, kt, :], kt == 0, kt == 7,
                     [bc, baw[s_]], [k.pb[bank]])
            o = l * 6144 + cb * 512
            P.tt(MR[:, o:o + 512], k.ps[bank][0:1, :], adab[:, o:o + 512], ALU.add, [k.pb[bank], bsm], [bmr])
            it += 1
    for l in range(2):
        for t in range(2):
            o = l * 6144 + (1 + 3 * t) * 1024
            P.stt(MR[:, o:o + 1024], MR[:, o:o + 1024], 1.0, ng[:, (l * 2 + t) * 1024:(l * 2 + t + 1) * 1024],
                  ALU.add, ALU.mult, [bmr, bsm], [bmr])
    P.dma("sync", T["modrows"].rearrange("l t n -> (l t n)").unsqueeze(0), MR, bmr, reads=[bmr],
          writes=[k.b_modrows])
    P.barrier(bufs)
    P.release(bufs)


def load_mods(P, k, T, l, which, bufs):
    m = P.tile([128, 3, 1024], F32)
    bm = Buf("mods"); bufs.append(bm)
    for i in range(3):
        P.dma("sync", m[:, i, :], T["modrows"][l, which * 3 + i, :].partition_broadcast(128), bm,
              reads=[k.b_modrows], writes=[bm])
    return m, bm


class NormW(dict):
    def __init__(self, sets):
        super().__init__()
        self.sets = sets
        self.i = 0

    def __getitem__(self, key):
        return self.sets[self.i][key]

    def flip(self):
        self.i ^= 1


def norm_mod_tile(P, k, X, bX, m, bm, W):
    W.flip()
    rms_rstd(P, X, W["ss"], W["bss"], W["junk"], Buf(), [bX])
    P.stt(W["tn"], X, W["ss"][:, 2:3], m[:, 1, :], ALU.mult, ALU.mult, [bX, W["bss"], bm], [W["btn"]])
    P.tt(W["hb"], W["tn"], m[:, 0, :], ALU.add, [W["btn"], bm], [W["bhb"]])


def norm_work(P, bufs, with_f32=False):
    sets = []
    junk = P.tile([128, D], BF16)
    for _ in range(2):
        Wd = {}
        Wd["ss"] = P.tile([128, 8], F32); Wd["bss"] = Buf("ss")
        Wd["junk"] = junk; Wd["bjunk"] = Buf("junk")
        Wd["tn"] = P.tile([128, D], F32); Wd["btn"] = Buf("tn")
        Wd["hb"] = P.tile([128, D], BF16); Wd["bhb"] = Buf("hb")
        sets.append(Wd)
    return NormW(sets)


def transpose8(P, k, src, bsrc, dst3, bdst, bank, n=8, evac="scalar"):
    pT = k.ps[bank].bitcast(BF16)
    for kt in range(n):
        P.tr(pT[:, kt * 128:(kt + 1) * 128], src[:, kt * 128:(kt + 1) * 128], k.ident,
             [bsrc, k.bconst], [k.pb[bank]])
    P.cp(evac, dst3, pT[:, 0:n * 128].rearrange("p (k t) -> p k t", k=n), [k.pb[bank]], [bdst])


def phase_mlp(P, k, T, l, x_in, b_in, x_out, b_out, final, h1_out=False):
    P.reset_arena(k.keep)
    bufs = []
    TB = 256
    NS = 2
    W1 = P.tile([128, 8, DFF], BF16)
    W2 = P.tile([128, 32, D], BF16)
    bW1 = [Buf() for _ in range(8)]
    bW2 = [Buf() for _ in range(32)]
    stg = [P.tile([128, 1024], F32) for _ in range(2)]
    bstg = [Buf("stg%d" % i) for i in range(2)]
    bufs += bstg
    si = [0]
    m, bm = load_mods(P, k, T, l, 1, bufs)
    fg = None
    if final:
        fg = P.tile([128, D], F32)
        bfg = Buf("fg"); bufs.append(bfg)
        P.dma("sync", fg, T["norm_f"].partition_broadcast(128), bfg, writes=[bfg])
    load_w_bf16(P, W1, bW1, T["w_fc1"][l], 8, DFF, stg, bstg, si)
    w2v = T["w_fc2"][l]
    for kt in range(32):
        s_ = si[0] % 2
        si[0] += 1
        P.dma("sync", stg[s_], w2v[kt * 128:(kt + 1) * 128, :], bstg[s_], writes=[bstg[s_]])
        P.tt(W2[:, kt, :], stg[s_], m[:, 2, :], ALU.mult, [bstg[s_], bm], [bW2[kt]],
             e=("gpsimd", "vector")[kt % 2])
    NX = 2
    xt = [P.tile([128, NS, D], F32) for _ in range(NX)]
    bxt = [Buf("xt%d" % i) for i in range(NX)]
    bufs += bxt
    W = norm_work(P, bufs)
    if h1_out:
        bufs += [W.sets[0]["bhb"], W.sets[1]["bhb"]]
        m1 = P.tile([128, 2, 1024], F32)
        bm1 = Buf("m1"); bufs.append(bm1)
        for i_ in range(2):
            P.dma("sync", m1[:, i_, :], T["modrows"][1, i_, :].partition_broadcast(128), bm1,
                  reads=[k.b_modrows], writes=[bm1])
    hT = [P.tile([128, 8, TB], BF16) for _ in range(2)]
    bhT = [Buf("hT0"), Buf("hT1")]
    NA = 4
    actT = [P.tile([128, TB], BF16) for _ in range(NA)]
    bact = [Buf() for _ in range(NA)]
    rl = [P.tile([128, TB], F32) for _ in range(2)]
    brl = [Buf() for _ in range(2)]
    ps, pb = k.ps, k.pb
    nblk = S // TB

    def load_x(blk):
        xb = blk % NX
        P.dma("sync", xt[xb], x_in[blk * TB:(blk + 1) * TB, :].rearrange("(s p) f -> p s f", p=128),
              bxt[xb], reads=[b_in], writes=[bxt[xb]])

    def front(blk):
        X, bX = xt[blk % NX], bxt[blk % NX]
        for s_ in range(NS):
            norm_mod_tile(P, k, X[:, s_, :], bX, m, bm, W)
            transpose8(P, k, W["hb"], W["bhb"], hT[blk % 2][:, :, s_ * 128:(s_ + 1) * 128], bhT[blk % 2], 0)

    def fc1(blk, j):
        bank = 1 + (j % 2)
        for kt in range(8):
            P.mm(ps[bank][:, 0:TB], W1[:, kt, j * 128:(j + 1) * 128], hT[blk % 2][:, kt, :], kt == 0, kt == 7,
                 [bW1[kt], bhT[blk % 2]], [pb[bank]])
        a = j % NA
        r = j % 2
        P.act(rl[r], ps[bank][:, 0:TB], AF.Relu, [pb[bank]], [brl[r]])
        P.tt(actT[a], rl[r], rl[r], ALU.mult, [brl[r]], [bact[a]])

    def fc2(blk, j):
        a = j % NA
        for s_ in range(NS):
            for h in range(2):
                bank = 4 + s_ * 2 + h
                P.mm(ps[bank], actT[a][:, s_ * 128:(s_ + 1) * 128], W2[:, j, h * 512:(h + 1) * 512],
                     j == 0, j == 31, [bact[a], bW2[j]], [pb[bank]])

    tails = []

    def tail(blk):
        X, bX = xt[blk % NX], bxt[blk % NX]
        if final:
            for s_ in range(NS):
                W.flip()
                rms_rstd(P, X[:, s_, :], W["ss"], W["bss"], W["junk"], Buf(), [bX], col=4)
                P.stt(X[:, s_, :], X[:, s_, :], W["ss"][:, 6:7], fg, ALU.mult, ALU.mult,
                      [bX, W["bss"], bfg], [bX])
        P.dma("gpsimd", x_out[blk * TB:(blk + 1) * TB, :].rearrange("(s p) f -> p s f", p=128), X, bX,
              reads=[bX], writes=[b_out])
        if h1_out:
            for s_ in range(NS):
                W.flip()
                rms_rstd(P, X[:, s_, :], W["ss"], W["bss"], W["junk"], Buf(), [bX], col=4)
                P.stt(W["tn"], X[:, s_, :], W["ss"][:, 6:7], m1[:, 1, :], ALU.mult, ALU.mult,
                      [bX, W["bss"], bm1], [W["btn"]])
                P.tt(W["hb"], W["tn"], m1[:, 0, :], ALU.add, [W["btn"], bm1], [W["bhb"]])
                r0 = blk * TB + s_ * 128
                P.dma("gpsimd", T["h1"][r0:r0 + 128, :], W["hb"], W["bhb"], reads=[W["bhb"]], writes=[k.b_h1])
        if blk + 2 < nblk:
            load_x(blk + 2)

    load_x(0)
    load_x(1)
    front(0)
    for blk in range(nblk):
        X, bX = xt[blk % NX], bxt[blk % NX]
        fc1(blk, 0)
        for j in range(32):
            if j + 1 < 32:
                fc1(blk, j + 1)
            fc2(blk, j)
            if j == 4 and tails:
                tail(tails.pop(0))
            if j == 14 and blk + 1 < nblk:
                front(blk + 1)
        for s_ in range(NS):
            for h in range(2):
                bank = 4 + s_ * 2 + h
                P.tt(X[:, s_, h * 512:(h + 1) * 512], ps[bank], X[:, s_, h * 512:(h + 1) * 512], ALU.add,
                     [pb[bank], bX], [bX])
        tails.append(blk)
    while tails:
        tail(tails.pop(0))
    P.barrier(bufs)
    P.release(bufs)


def phase_l0(P, k, T):
    P.reset_arena(k.keep)
    bufs = []
    ps, pb = k.ps, k.pb
    Win = P.tile([128, 8, 3072], BF16)
    bWin = [Buf() for _ in range(8)]
    Wout = P.tile([128, 8, 1024], BF16)
    bWout = [Buf() for _ in range(8)]
    stg = [P.tile([128, 1536], F32) for _ in range(2)]
    bstg = [Buf("stg%d" % i) for i in range(2)]
    bufs += bstg
    si = [0]
    m, bm = load_mods(P, k, T, 0, 0, bufs)
    for kt in range(8):
        for c0 in (0, 1536):
            s_ = si[0] % 2
            si[0] += 1
            P.dma("sync", stg[s_][:, 0:1536], T["w_in"][0, kt * 128:(kt + 1) * 128, c0:c0 + 1536], bstg[s_],
                  writes=[bstg[s_]])
            P.cp(("gpsimd", "vector", "scalar")[si[0] % 3], Win[:, kt, c0:c0 + 1536], stg[s_][:, 0:1536],
                 [bstg[s_]], [bWin[kt]])
    load_w_bf16(P, Wout, bWout, T["w_out"][0], 8, 1024, stg, bstg, si)
    rt = P.tile([128, 5, 512], F32)
    brt = Buf("rt"); bufs.append(brt)
    P.dma("sync", rt, T["ret_tab"].rearrange("t p n -> p t n"), brt, writes=[brt])
    DMT, WQF, WQB, GL = rt[:, 0, :], rt[:, 1, :], rt[:, 2, :], rt[:, 3, :]
    wkf, wkb = rt[:, 4, 0:4], rt[:, 4, 4:8]
    cw = P.tile([128, 4, 31], F32)
    cpar = P.tile([128, 3, 4], F32)
    bcp = Buf("convp"); bufs.append(bcp)
    for j in range(4):
        P.dma("sync", cw[:, j, :], T["conv_w"][0][:, j * 128:(j + 1) * 128].rearrange("w c -> c w"), bcp,
              writes=[bcp], allow_slow_non_contiguous=True)
    for i, nm in enumerate(["conv_b", "cln_g", "cln_b"]):
        P.dma("sync", cpar[:, i, :], T[nm][0].rearrange("(j c) -> c j", c=128), bcp, writes=[bcp],
              allow_slow_non_contiguous=True)
    DG = P.tile([128, 4 * 31, 128], BF16)
    bDG = Buf("DG")
    for j in range(4):
        for w in range(31):
            if (j * 31 + w) % 2 == 0:
                P.ts(DG[:, j * 31 + w, :], k.identf, cw[:, j, w:w + 1], None, ALU.mult, None, [bcp, k.bconst], [bDG])
            else:
                P.act(DG[:, j * 31 + w, :], k.identf, AF.Copy, [bcp, k.bconst], [bDG], scale=cw[:, j, w:w + 1])
    onesf = P.tile([128, 128], F32)
    bones = Buf("ones")
    P.memset("gpsimd", onesf, 1.0 / 512.0, [bones])

    W = norm_work(P, bufs)
    cs = [P.tile([128, 2, 64], F32) for _ in range(2)]
    bcs = [Buf("cs%d" % i) for i in range(2)]
    bufs += bcs
    tA = P.tile([128, 8, 64], F32); btA = Buf()
    tB = P.tile([128, 8, 64], F32); btB = Buf()
    qk = P.tile([128, 2, 512], BF16); bqk = Buf("qk")
    vbf = P.tile([128, 2, 512], BF16); bvbf = Buf("vbf")
    markA = P.aoff
    xt = [P.tile([128, D], F32) for _ in range(2)]
    bxt = [Buf("xt%d" % i) for i in range(2)]
    bufs += bxt
    hTc = [P.tile([128, 8, 128], BF16) for _ in range(2)]
    bhTc = [Buf("hTc%d" % i) for i in range(2)]
    bufs += bhTc
    R = P.tile([128, 512], F32); bR = Buf("R")
    Rb = [P.tile([128, 512], BF16) for _ in range(2)]
    bRb = [Buf("Rb%d" % i) for i in range(2)]
    bufs += bRb
    zero_bf = P.tile([128, 8, PADC], BF16); bz = Buf("zero"); bufs.append(bz)
    P.memset("gpsimd", zero_bf, 0.0, [bz])
    hT0 = T["hT0"]
    P.dma("sync", hT0[:, :, 0:PADC], zero_bf, bz, reads=[bz], writes=[k.b_hT0])
    P.dma("sync", hT0[:, :, PADC + S:PADC + S + PADC], zero_bf, bz, reads=[bz], writes=[k.b_hT0])

    def rotary(src_ps, dst, nh, bsrc, bdst, cst, bcst):
        sv = src_ps.rearrange("p (h t d) -> p h t d", h=nh, t=2)
        dv = dst.rearrange("p (h t d) -> p h t d", h=nh, t=2)
        cosb = bcast_mid(cst[:, 0, :], nh)
        sinb = bcast_mid(cst[:, 1, :], nh)
        a_, b_ = tA[:, 0:nh, :], tB[:, 0:nh, :]
        P.tt(a_, sv[:, :, 0, :], cosb, ALU.mult, [bsrc, bcst], [btA])
        P.tt(b_, sv[:, :, 1, :], sinb, ALU.mult, [bsrc, bcst], [btB])
        P.tt(dv[:, :, 0, :], a_, b_, ALU.subtract, [btA, btB], [bdst])
        P.tt(a_, sv[:, :, 0, :], sinb, ALU.mult, [bsrc, bcst], [btA])
        P.tt(b_, sv[:, :, 1, :], cosb, ALU.mult, [bsrc, bcst], [btB])
        P.tt(dv[:, :, 1, :], a_, b_, ALU.add, [btA, btB], [bdst])

    P.memset("vector", R, 0.0, [bR])
    P.memset("vector", Rb[0], 0.0, [bRb[0]])
    P.memset("vector", Rb[1], 0.0, [bRb[1]])
    order = list(range(NCH - 1, -1, -1))

    def loadA(i):
        c = order[i]
        P.dma("sync", xt[i % 2], T["x"][c * 128:(c + 1) * 128, :], bxt[i % 2], writes=[bxt[i % 2]])
        P.dma("sync", cs[i % 2], T["rot_tab"][c], bcs[i % 2], writes=[bcs[i % 2]])
    def frontA(i):
        c = order[i]
        X, bX = xt[i % 2], bxt[i % 2]
        H, bH = hTc[i % 2], bhTc[i % 2]
        norm_mod_tile(P, k, X, bX, m, bm, W)
        transpose8(P, k, W["hb"], W["bhb"], H, bH, 0)
        P.dma("gpsimd", hT0[:, :, PADC + c * 128:PADC + (c + 1) * 128], H, bH, reads=[bH], writes=[k.b_hT0])

    def backA(i):
        c = order[i]
        H, bH = hTc[i % 2], bhTc[i % 2]
        kb_, vb_ = (1, 2) if i % 2 == 0 else (4, 5)
        for kt in range(8):
            P.mm(ps[kb_], H[:, kt, :], Win[:, kt, 1536:2048], kt == 0, kt == 7, [bH, bWin[kt]], [pb[kb_]])
        for kt in range(8):
            P.mm(ps[vb_], H[:, kt, :], Win[:, kt, 2048:2560], kt == 0, kt == 7, [bH, bWin[kt]], [pb[vb_]])
        rotary(ps[kb_], qk[:, 1, :], 4, pb[kb_], bqk, cs[i % 2], bcs[i % 2])
        P.tt(vbf[:, 1, :].rearrange("p (h e) -> p h e", h=4), ps[vb_].rearrange("p (h e) -> p h e", h=4),
             bcast_last(wkb, 128), ALU.mult, [pb[vb_], brt], [bvbf])
        for h in range(4):
            P.mm(ps[3][:, h * 128:(h + 1) * 128], qk[:, 1, h * 128:(h + 1) * 128], vbf[:, 1, h * 128:(h + 1) * 128],
                 True, True, [bqk, bvbf], [pb[3]])
        rb, brb = Rb[i % 2], bRb[i % 2]
        P.cp("scalar", rb, R, [bR], [brb])
        P.dma("gpsimd", T["sb_all"][c], rb, brb, reads=[brb], writes=[k.b_sb])
        P.tt(R, R, GL, ALU.mult, [bR, brt], [bR])
        P.tt(R, R, ps[3], ALU.add, [bR, pb[3]], [bR])

    loadA(0)
    loadA(1)
    frontA(0)
    for i in range(NCH):
        if i + 1 < NCH:
            frontA(i + 1)
        backA(i)
        if i + 2 < NCH:
            loadA(i + 2)

    P.barrier(bufs)
    P.aoff = markA
    hTh = [P.tile([128, 8, 160], BF16) for _ in range(2)]
    bhTh = [Buf("hTh%d" % i) for i in range(2)]
    bufs += bhTh
    sbt = [P.tile([128, 512], BF16) for _ in range(2)]
    bsbt = [Buf("sbt%d" % i) for i in range(2)]
    bufs += bsbt
    sig = P.tile([128, 4, 158], F32); bsig = Buf()
    abf = P.tile([128, 4, 158], BF16); babf = Buf()
    conv = P.tile([128, 4, 128], F32); bconv = Buf()
    csq = P.tile([128, 4, 128], F32); bcsq = Buf()
    st = P.tile([128, 4, 128], F32); bst = Buf()
    aT = P.tile([128, 8, 128], BF16); baT = Buf("aT")
    sg = P.tile([128, 512], F32); bsg = Buf()
    qT = P.tile([128, 4, 512], BF16); bqT = Buf("qT")
    sT = P.tile([128, 512], BF16); bsT = Buf()
    rr = P.tile([128, 512], BF16); brr = Buf()
    hs = P.tile([128, 16], F32); bhs = Buf()
    Sf = P.tile([128, 512], F32); bSf = Buf("Sf")
    Sfb = P.tile([128, 512], BF16); bSfb = Buf("Sfb")
    xo = [P.tile([128, D], F32) for _ in range(2)]
    bxo = [Buf("xo%d" % i) for i in range(2)]
    bufs += bxo
    tmpo = P.tile([128, 512], F32); btmpo = Buf()
    P.memset("vector", Sf, 0.0, [bSf])
    P.memset("vector", Sfb, 0.0, [bSfb])

    def loadB(c):
        i = c % 2
        P.dma("sync", hTh[i][:, :, 0:158], hT0[:, :, PADC + c * 128 - 15:PADC + c * 128 + 143], bhTh[i],
              reads=[k.b_hT0], writes=[bhTh[i]])
        P.dma("sync", sbt[i], T["sb_all"][c], bsbt[i], reads=[k.b_sb], writes=[bsbt[i]])
        P.dma("sync", cs[i], T["rot_tab"][c], bcs[i], writes=[bcs[i]])
        P.dma("sync", xo[i], T["x"][c * 128:(c + 1) * 128, :], bxo[i], writes=[bxo[i]])
    def stA1(c):
        H, bH = hTh[c % 2], bhTh[c % 2]
        for half in range(2):
            for jj in range(2):
                j = half * 2 + jj
                for kt in range(8):
                    P.mm(ps[half][:, jj * 160:jj * 160 + 158], Win[:, kt, j * 128:(j + 1) * 128], H[:, kt, 0:158],
                         kt == 0, kt == 7, [bWin[kt], bH], [pb[half]])
                for kt in range(8):
                    P.mm(ps[2 + half][:, jj * 160:jj * 160 + 158], Win[:, kt, 512 + j * 128:512 + (j + 1) * 128],
                         H[:, kt, 0:158], kt == 0, kt == 7, [bWin[kt], bH], [pb[2 + half]])

    def stA2(c):
        for half in range(2):
            gv = ps[2 + half][:, 0:320].rearrange("p (j t) -> p j t", j=2)[:, :, 0:158]
            vv = ps[half][:, 0:320].rearrange("p (j t) -> p j t", j=2)[:, :, 0:158]
            P.act(sig[:, half * 2:half * 2 + 2, :], gv, AF.Sigmoid, [pb[2 + half]], [bsig])
            P.tt(abf[:, half * 2:half * 2 + 2, :], vv, sig[:, half * 2:half * 2 + 2, :], ALU.mult,
                 [pb[half], bsig], [babf])

    def stA3(c):
        for j in range(4):
            for w in range(31):
                P.mm(ps[0][:, j * 128:(j + 1) * 128], DG[:, j * 31 + w, :], abf[:, j, w:w + 128], w == 0, w == 30,
                     [bDG, babf], [pb[0]])

    def stA4(c):
        for j in range(4):
            P.act(conv[:, j, :], ps[0][:, j * 128:(j + 1) * 128], AF.Identity, [pb[0], bcp], [bconv],
                  bias=cpar[:, 0, j:j + 1])
        P.act(csq, conv, AF.Square, [bconv], [bcsq])

    def stA5(c):
        for j in range(4):
            P.mm(ps[1][:, 0:128], onesf, conv[:, j, :], j == 0, j == 3, [bones, bconv], [pb[1]])
        for j in range(4):
            P.mm(ps[1][:, 128:256], onesf, csq[:, j, :], j == 0, j == 3, [bones, bcsq], [pb[1]])

    def stA6(c):
        P.cp("vector", st[:, 0, :], ps[1][:, 0:128], [pb[1]], [bst])
        P.tt(st[:, 1, :], st[:, 0, :], st[:, 0, :], ALU.mult, [bst], [bst])
        P.tt(st[:, 1, :], ps[1][:, 128:256], st[:, 1, :], ALU.subtract, [pb[1], bst], [bst])
        P.act(st[:, 2, :], st[:, 1, :], AF.Sqrt, [bst], [bst], bias=EPS)
        P.recip(st[:, 3, :], st[:, 2, :], [bst], [bst])
        P.tt(conv, conv, bcast_mid(st[:, 0, :], 4), ALU.subtract, [bconv, bst], [bconv])
        P.tt(conv, conv, bcast_mid(st[:, 3, :], 4), ALU.mult, [bconv, bst], [bconv])

    def stA7(c):
        for j in range(4):
            P.act(aT[:, j, :], conv[:, j, :], AF.Silu, [bconv, bcp], [baT], scale=cpar[:, 1, j:j + 1],
                  bias=cpar[:, 2, j:j + 1])

    def stB1(c):
        H, bH = hTh[c % 2], bhTh[c % 2]
        for blk_, col0 in enumerate((1024, 1536, 2048, 2560)):
            bank = 4 + blk_
            for kt in range(8):
                P.mm(ps[bank], H[:, kt, 15:143], Win[:, kt, col0:col0 + 512], kt == 0, kt == 7, [bH, bWin[kt]],
                     [pb[bank]])

    def stB2(c):
        i = c % 2
        rotary(ps[4], qk[:, 0, :], 4, pb[4], bqk, cs[i], bcs[i])
        rotary(ps[5], qk[:, 1, :], 4, pb[5], bqk, cs[i], bcs[i])
        P.cp("scalar", vbf[:, 0, :], ps[6], [pb[6]], [bvbf])
        P.tt(vbf[:, 1, :].rearrange("p (h e) -> p h e", h=4), ps[6].rearrange("p (h e) -> p h e", h=4),
             bcast_last(wkf, 128), ALU.mult, [pb[6], brt], [bvbf])
        P.act(sg, ps[7], AF.Silu, [pb[7]], [bsg])

    def stB3(c):
        pTq = ps[4].bitcast(BF16)
        pTk = ps[5].bitcast(BF16)
        for h in range(4):
            P.tr(pTq[:, h * 128:(h + 1) * 128], qk[:, 0, h * 128:(h + 1) * 128], k.ident, [bqk, k.bconst], [pb[4]])
        for h in range(4):
            P.tr(pTk[:, h * 128:(h + 1) * 128], qk[:, 1, h * 128:(h + 1) * 128], k.ident, [bqk, k.bconst], [pb[5]])

    def stB4(c):
        pTq = ps[4].bitcast(BF16)
        pTk = ps[5].bitcast(BF16)
        P.cp("scalar", qT[:, 0, :], pTq[:, 0:512], [pb[4]], [bqT])
        P.tt(qT[:, 1, :], pTq[:, 0:512], WQF, ALU.mult, [pb[4], brt], [bqT])
        P.tt(qT[:, 2, :], pTq[:, 0:512], WQB, ALU.mult, [pb[4], brt], [bqT])
        P.cp("scalar", qT[:, 3, :], pTk[:, 0:512], [pb[5]], [bqT])

    def stB5(c):
        for h in range(4):
            P.mm(ps[2][:, h * 128:(h + 1) * 128], qT[:, 3, h * 128:(h + 1) * 128], qT[:, 0, h * 128:(h + 1) * 128],
                 True, True, [bqT], [pb[2]])

    def stB6(c):
        P.tt(sT, ps[2], DMT, ALU.mult, [pb[2], brt], [bsT])

    def stB7(c):
        i = c % 2
        for h in range(4):
            hs_ = slice(h * 128, (h + 1) * 128)
            P.mm(ps[7][:, hs_], sT[:, hs_], vbf[:, 0, hs_], True, False, [bsT, bvbf], [pb[7]])
            P.mm(ps[7][:, hs_], qT[:, 1, hs_], Sfb[:, hs_], False, False, [bqT, bSfb], [pb[7]])
            P.mm(ps[7][:, hs_], qT[:, 2, hs_], sbt[i][:, hs_], False, True, [bqT, bsbt[i]], [pb[7]])
        for h in range(4):
            hs_ = slice(h * 128, (h + 1) * 128)
            P.mm(ps[6][:, hs_], qk[:, 1, hs_], vbf[:, 1, hs_], True, True, [bqk, bvbf], [pb[6]])

    def stB8(c):
        P.tt(Sf, Sf, GL, ALU.mult, [bSf, brt], [bSf])
        P.tt(Sf, Sf, ps[6], ALU.add, [bSf, pb[6]], [bSf])
        P.cp("scalar", Sfb, Sf, [bSf], [bSfb])
        for h in range(4):
            P.act(W["junk"][:, 0:128], ps[7][:, h * 128:(h + 1) * 128], AF.Square, [pb[7]], [Buf(), bhs],
                  accum_out=hs[:, h:h + 1])
        P.act(hs[:, 4:8], hs[:, 0:4], AF.Sqrt, [bhs], [bhs], scale=1.0 / 128.0, bias=EPS)
        P.recip(hs[:, 8:12], hs[:, 4:8], [bhs], [bhs])
        for h in range(4):
            hs_ = slice(h * 128, (h + 1) * 128)
            P.stt(rr[:, hs_], ps[7][:, hs_], hs[:, 8 + h:9 + h], sg[:, hs_], ALU.mult, ALU.mult,
                  [pb[7], bhs, bsg], [brr])

    def stB9(c):
        transpose8(P, k, rr, brr, aT[:, 4:8, :], baT, 6, n=4)

    def stC(c):
        i = c % 2
        for hf in range(2):
            for kt in range(8):
                P.mm(ps[4 + hf], aT[:, kt, :], Wout[:, kt, hf * 512:(hf + 1) * 512], kt == 0, kt == 7,
                     [baT, bWout[kt]], [pb[4 + hf]])
        for hf in range(2):
            P.tt(tmpo, ps[4 + hf], m[:, 2, hf * 512:(hf + 1) * 512], ALU.mult, [pb[4 + hf], bm], [btmpo])
            P.tt(xo[i][:, hf * 512:(hf + 1) * 512], tmpo, xo[i][:, hf * 512:(hf + 1) * 512], ALU.add,
                 [btmpo, bxo[i]], [bxo[i]])
        P.dma("gpsimd", T["x1"][c * 128:(c + 1) * 128, :], xo[i], bxo[i], reads=[bxo[i]], writes=[k.b_x1])

    loadB(0)
    loadB(1)
    stA1(0)
    for c in range(NCH):
        for stg_ in (stB1, stA2, stB2, stA3, stB3, stB4, stA4, stB5, stA5, stB6, stA6, stB7):
            stg_(c)
        if c + 1 < NCH:
            stA1(c + 1)
        for stg_ in (stB8, stA7, stB9, stC):
            stg_(c)
        if c + 2 < NCH:
            loadB(c + 2)
    P.barrier(bufs)
    P.release(bufs)


def phase_s5(P, k, T, do_pre=True):
    ps, pb = k.ps, k.pb
    if do_pre:
        P.reset_arena(k.keep)
        bufs = []
        m, bm = load_mods(P, k, T, 1, 0, bufs)
        W = norm_work(P, bufs)
        xt = [P.tile([128, D], F32) for _ in range(2)]
        bxt = [Buf("xt%d" % i) for i in range(2)]
        bufs += bxt
        hbs = [P.tile([128, D], BF16) for _ in range(2)]
        bhbs = [Buf("hbs%d" % i) for i in range(2)]
        bufs += bhbs

        def ldx(c):
            P.dma("sync", xt[c % 2], T["x2"][c * 128:(c + 1) * 128, :], bxt[c % 2], reads=[k.b_x2], writes=[bxt[c % 2]])
        ldx(0)
        for c in range(NCH):
            if c + 1 < NCH:
                ldx(c + 1)
            X, bX = xt[c % 2], bxt[c % 2]
            W.flip()
            rms_rstd(P, X, W["ss"], W["bss"], W["junk"], Buf(), [bX])
            P.stt(W["tn"], X, W["ss"][:, 2:3], m[:, 1, :], ALU.mult, ALU.mult, [bX, W["bss"], bm], [W["btn"]])
            P.tt(hbs[c % 2], W["tn"], m[:, 0, :], ALU.add, [W["btn"], bm], [bhbs[c % 2]])
            P.dma("gpsimd", T["h1"][c * 128:(c + 1) * 128, :], hbs[c % 2], bhbs[c % 2], reads=[bhbs[c % 2]],
                  writes=[k.b_h1])
        P.barrier(bufs)
        P.release(bufs)

    P.reset_arena(k.keep)
    bufs = []
    NG = 128
    bp = Buf("s5prep")
    bld = [Buf("s5ld%d" % i) for i in range(6)]
    bufs += bld
    sg = P.tile([128, 2], F32)
    P.memset("gpsimd", sg[0:64, 0:1], -1.0, [bp])
    P.memset("gpsimd", sg[64:128, 0:1], 1.0, [bp])
    P.memset("gpsimd", sg[0:64, 1:2], 1.0, [bp])
    P.memset("gpsimd", sg[64:128, 1:2], -1.0, [bp])
    lr = P.tile([128, NG], F32)
    li = P.tile([128, NG], F32)
    dtv = P.tile([128, NG], F32)
    for hlf in range(2):
        P.dma("sync", lr[hlf * 64:(hlf + 1) * 64, :], T["s5_lam_re"][0].rearrange("d g p -> p (d g)"), bld[0],
              writes=[bld[0]], allow_slow_non_contiguous=True)
        P.dma("sync", li[hlf * 64:(hlf + 1) * 64, :], T["s5_lam_im"][0].rearrange("d g p -> p (d g)"), bld[1],
              writes=[bld[1]], allow_slow_non_contiguous=True)
    P.dma("sync", dtv, T["s5_log_step"][0].rearrange("d g -> (d g)").partition_broadcast(128), bld[2],
          writes=[bld[2]])
    X1b = P.tile_top([128, NG, 16], F32)
    X2b = P.tile_top([128, NG, 16], F32)
    X1c = P.tile([128, NG, 16], F32)
    X2c = P.tile([128, NG, 16], F32)
    bre = T["s5_b_re"][0].rearrange("d g p c -> p (d g) c")
    bim = T["s5_b_im"][0].rearrange("d g p c -> p (d g) c")
    P.dma("sync", X1b[0:64], bre, bld[3], writes=[bld[3]], allow_slow_non_contiguous=True)
    P.dma("sync", X1b[64:128], bim, bld[3], writes=[bld[3]], allow_slow_non_contiguous=True)
    P.dma("sync", X2b[0:64], bim, bld[3], writes=[bld[3]], allow_slow_non_contiguous=True)
    P.dma("sync", X2b[64:128], bre, bld[3], writes=[bld[3]], allow_slow_non_contiguous=True)
    CRI = P.tile_top([128, 16, 2, 64], F32)
    cre = T["s5_c_re"][0].rearrange("d g c p -> (d g c) p").rearrange("(t r) p -> r t p", r=128)
    cim = T["s5_c_im"][0].rearrange("d g c p -> (d g c) p").rearrange("(t r) p -> r t p", r=128)
    P.dma("sync", CRI[:, :, 0, :], cre, bld[4], writes=[bld[4]])
    P.dma("sync", CRI[:, :, 1, :], cim, bld[4], writes=[bld[4]])
    CIR = P.tile_top([128, 16, 2, 64], F32)
    P.dma("sync", CIR[:, :, 0, :], cim, bld[5], writes=[bld[5]])
    P.dma("sync", CIR[:, :, 1, :], cre, bld[5], writes=[bld[5]])
    for t in range(16):
        P.tr(ps[6][:, 0:128], CRI[:, t, :, :], k.identf, [bld[4], k.bconst], [pb[6]])
        P.cp("vector", X1c[:, t * 8:(t + 1) * 8, :], ps[6][:, 0:128].rearrange("p (g c) -> p g c", g=8),
             [pb[6]], [bp])
        P.tr(ps[7][:, 0:128], CIR[:, t, :, :], k.identf, [bld[5], k.bconst], [pb[7]])
        P.cp("vector", X2c[:, t * 8:(t + 1) * 8, :], ps[7][:, 0:128].rearrange("p (g c) -> p g c", g=8),
             [pb[7]], [bp])
    ldall = list(bld)
    t1 = P.tile([128, NG], F32)
    t2 = P.tile([128, NG], F32)
    t3 = P.tile([128, NG], F32)
    ti = P.tile([128, NG], I32)
    cosv = P.tile([128, NG], F32)
    sinv = P.tile([128, NG], F32)
    mag = P.tile([128, NG], F32)
    R_ = [bp] + ldall
    Wp = [bp]
    P.ts(lr, lr, -1e-4, None, ALU.min, None, R_, Wp)
    P.act(dtv, dtv, AF.Exp, R_, Wp)
    P.tt(t1, lr, dtv, ALU.mult, R_, Wp)
    P.act(mag, t1, AF.Exp, R_, Wp)
    P.tt(t1, li, dtv, ALU.mult, R_, Wp)
    P.ts(t1, t1, 1.0 / (2.0 * math.pi), None, ALU.mult, None, R_, Wp)

    def sin_turns(dst, off):
        P.ts(t2, t1, off, None, ALU.add, None, R_, Wp)
        P.cp("vector", ti, t2, R_, Wp)
        P.cp("vector", t3, ti, R_, Wp)
        P.tt(t2, t2, t3, ALU.subtract, R_, Wp)
        P.ts(t3, t2, 0.5, None, ALU.is_gt, None, R_, Wp)
        P.tt(t2, t2, t3, ALU.subtract, R_, Wp)
        P.ts(t3, t2, -0.5, None, ALU.is_lt, None, R_, Wp)
        P.tt(t2, t2, t3, ALU.add, R_, Wp)
        P.ts(t2, t2, 0.4999995, -0.4999995, ALU.min, ALU.max, R_, Wp)
        P.act(dst, t2, AF.Sin, R_, Wp, scale=2.0 * math.pi)
    sin_turns(sinv, 0.0)
    sin_turns(cosv, 0.25)
    NP_ = 23
    PWr = P.tile([128, NP_, NG], F32)
    PWi = P.tile_top([128, NP_, NG], F32)
    ar, ai = PWr[:, 8, :], PWi[:, 8, :]
    P.tt(ar, mag, cosv, ALU.mult, R_, Wp)
    P.tt(ai, mag, sinv, ALU.mult, R_, Wp)
    P.memset("vector", PWr[:, 7, :], 1.0, Wp)
    P.memset("vector", PWi[:, 7, :], 0.0, Wp)
    fr = P.tile([128, NG], F32)
    fi = P.tile([128, NG], F32)
    P.tt(t1, lr, lr, ALU.mult, R_, Wp)
    P.tt(t2, li, li, ALU.mult, R_, Wp)
    P.tt(t1, t1, t2, ALU.add, R_, Wp)
    P.recip(t1, t1, R_, Wp)
    P.ts(t2, ar, -1.0, None, ALU.add, None, R_, Wp)
    P.tt(t3, t2, lr, ALU.mult, R_, Wp)
    P.tt(fr, ai, li, ALU.mult, R_, Wp)
    P.tt(fr, fr, t3, ALU.add, R_, Wp)
    P.tt(fr, fr, t1, ALU.mult, R_, Wp)
    P.tt(t3, ai, lr, ALU.mult, R_, Wp)
    P.tt(fi, t2, li, ALU.mult, R_, Wp)
    P.tt(fi, t3, fi, ALU.subtract, R_, Wp)
    P.tt(fi, fi, t1, ALU.mult, R_, Wp)
    P.tt(t1, mag, mag, ALU.mult, R_, Wp)
    P.recip(t1, t1, R_, Wp)
    P.tt(PWr[:, 6, :], ar, t1, ALU.mult, R_, Wp)
    P.tt(t2, ai, t1, ALU.mult, R_, Wp)
    P.ts(PWi[:, 6, :], t2, -1.0, None, ALU.mult, None, R_, Wp)

    def cmul(or_, oi_, xr, xi, yr, yi):
        P.tt(t1, xr, yr, ALU.mult, R_, Wp)
        P.tt(t2, xi, yi, ALU.mult, R_, Wp)
        P.tt(t3, xr, yi, ALU.mult, R_, Wp)
        P.tt(cosv, xi, yr, ALU.mult, R_, Wp)
        P.tt(or_, t1, t2, ALU.subtract, R_, Wp)
        P.tt(oi_, t3, cosv, ALU.add, R_, Wp)
    for n in range(2, 16):
        cmul(PWr[:, 7 + n, :], PWi[:, 7 + n, :], PWr[:, 6 + n, :], PWi[:, 6 + n, :], ar, ai)
    for n in range(2, 8):
        cmul(PWr[:, 7 - n, :], PWi[:, 7 - n, :], PWr[:, 8 - n, :], PWi[:, 8 - n, :], PWr[:, 6, :], PWi[:, 6, :])
    KSr = P.tile([128, 10, NG], F32)
    KSi = P.tile([128, 10, NG], F32)
    KS2 = P.tile([128, 10, NG], F32)
    P.cp("vector", KSr[:, 0, :], PWr[:, 15, :], R_, Wp)
    P.cp("vector", KSi[:, 0, :], PWi[:, 15, :], R_, Wp)
    for kk in range(1, 10):
        cmul(KSr[:, kk, :], KSi[:, kk, :], KSr[:, kk - 1, :], KSi[:, kk - 1, :], KSr[:, kk - 1, :], KSi[:, kk - 1, :])
    P.ts(KS2, KSi, sg[:, 1:2], None, ALU.mult, None, R_, Wp)
    C2q = P.tile([128, 8, NG], F32)
    C1p = P.tile([128, 16, NG], F32)
    C2p = P.tile([128, 16, NG], F32)
    P.ts(C2q, PWi[:, 0:8, :], sg[:, 0:1], None, ALU.mult, None, R_, Wp)
    P.ts(C1p, PWr[:, 7:23, :], sg[:, 1:2], None, ALU.mult, None, R_, Wp)
    P.ts(C2p, PWi[:, 7:23, :], -1.0, None, ALU.mult, None, R_, Wp)
    X1B = P.tile([128, NG, 16], F32)
    X2B = P.tile([128, NG, 16], F32)
    tb1 = P.tile_top([128, NG, 16], F32)
    P.ts(t1, fi, sg[:, 0:1], None, ALU.mult, None, R_, Wp)
    P.ts(t2, fi, sg[:, 1:2], None, ALU.mult, None, R_, Wp)
    P.tt(X1B, X1b, bcast_last(fr, 16), ALU.mult, R_, Wp)
    P.tt(tb1, X2b, bcast_last(t1, 16), ALU.mult, R_, Wp)
    P.tt(X1B, X1B, tb1, ALU.add, R_, Wp)
    P.tt(X2B, X2b, bcast_last(fr, 16), ALU.mult, R_, Wp)
    P.tt(tb1, X1b, bcast_last(t2, 16), ALU.mult, R_, Wp)
    P.tt(X2B, X2B, tb1, ALU.add, R_, Wp)
    msk = P.tile([128, 2, 128], F32)
    bmsk = Buf("msk"); bufs.append(bmsk)
    P.dma("sync", msk, T["s5_mask"].rearrange("d p n -> p d n"), bmsk, writes=[bmsk])

    P.barrier(bufs)
    P.atop = P.asize
    Qm = [P.tile([128, 2, 8, 128], BF16) for _ in range(2)]
    Pm = [P.tile([128, 2, 8, 128], BF16) for _ in range(2)]
    Po = [P.tile([128, 2, 8, 128], BF16) for _ in range(2)]
    bgen = [Buf("gen0"), Buf("gen1")]
    g1 = P.tile([128, 8, 16], F32); g2 = P.tile([128, 8, 16], F32)
    bg12 = Buf("g12")
    hcm = P.tile([128, 8, 8, 128], BF16)
    bhcm = Buf("hcm"); bufs.append(bhcm)
    hcg = P.tile([128, 8, 8, 128], BF16)
    bhcg = Buf("hcg")
    ycm = P.tile([128, 8, 8, 128], F32)
    bycm = Buf("ycm"); bufs.append(bycm)
    U = [P.tile([128, 1024], BF16) for _ in range(2)]; bU = [Buf("U0"), Buf("U1")]
    TT = [[P.tile([128, 128], BF16) for _ in range(2)] for _ in range(2)]
    bTT = [[Buf() for _ in range(2)] for _ in range(2)]
    WT = [[P.tile([128, 128], BF16) for _ in range(2)] for _ in range(2)]
    bWT = [[Buf() for _ in range(2)] for _ in range(2)]
    Mk = [[P.tile([128, 10, 128], BF16) for _ in range(2)] for _ in range(2)]
    bMk = [[Buf() for _ in range(2)] for _ in range(2)]
    NTJ = 4
    tJ = [P.tile([128, 128], F32) for _ in range(NTJ)]; btJ = [Buf() for _ in range(NTJ)]
    Ib = [P.tile([128, 1026], BF16) for _ in range(2)]; bIb = [Buf("Ib0"), Buf("Ib1")]
    for d in range(2):
        P.memset("gpsimd", Ib[d], 0.0, [bIb[d]])
    chain_ps = [k.psbig[:, (2 + 2 * d) * 512:(4 + 2 * d) * 512] for d in range(2)]
    chain_pb = [[pb[2 + 2 * d], pb[3 + 2 * d]] for d in range(2)]
    cp_eng = ["scalar", "vector"]
    tjc = [0]

    def batch_thunks(gb):
        q = gb % 2
        th = []
        gen_eng = "vector" if gb == 0 else "gpsimd"

        def ld():
            for blk in range(8):
                P.dma("sync", hcm[:, blk, :, :],
                      T["h1"][blk * 1024:(blk + 1) * 1024, gb * 128:(gb + 1) * 128].rearrange("(c i) f -> c i f", i=8),
                      bhcm, reads=[k.b_h1], writes=[bhcm])
        th.append(ld)
        for blk in range(8):
            th.append(lambda blk=blk: P.cp("gpsimd", hcg[:, blk, :, :].rearrange("p g (i c) -> p i g c", i=8),
                                           hcm[:, blk, :, :].rearrange("p i (g c) -> p i g c", g=8), [bhcm], [bhcg]))
        for d in range(2):
            gsl = slice(d * 64 + gb * 8, d * 64 + gb * 8 + 8)
            for jj in range(8):
                slot = jj if d == 0 else 7 - jj
                ssl = slice(slot * 16, slot * 16 + 16)

                def gq(d=d, gsl=gsl, jj=jj, ssl=ssl):
                    P.tt(g1, X1B[:, gsl, :], bcast_last(PWr[:, 7 - jj, gsl], 16), ALU.mult, R_, [bg12])
                    P.tt(g2, X2B[:, gsl, :], bcast_last(C2q[:, 7 - jj, gsl], 16), ALU.mult, R_, [bg12])
                    P.tt(Qm[q][:, d, :, ssl], g1, g2, ALU.add, [bg12], [bgen[q]])

                def gp(d=d, gsl=gsl, jj=jj, ssl=ssl):
                    P.tt(g1, X1c[:, gsl, :], bcast_last(C1p[:, jj, gsl], 16), ALU.mult, R_, [bg12], e=gen_eng)
                    P.tt(g2, X2c[:, gsl, :], bcast_last(C2p[:, jj, gsl], 16), ALU.mult, R_, [bg12], e=gen_eng)
                    P.tt(Pm[q][:, d, :, ssl], g1, g2, ALU.add, [bg12], [bgen[q]], e=gen_eng)

                def go(d=d, gsl=gsl, jj=jj, ssl=ssl):
                    P.tt(g1, X1c[:, gsl, :], bcast_last(C1p[:, jj + 8, gsl], 16), ALU.mult, R_, [bg12], e=gen_eng)
                    P.tt(g2, X2c[:, gsl, :], bcast_last(C2p[:, jj + 8, gsl], 16), ALU.mult, R_, [bg12], e=gen_eng)
                    P.tt(Po[q][:, d, :, ssl], g1, g2, ALU.add, [bg12], [bgen[q]], e=gen_eng)
                th += [gq, gp, go]
        return th

    def group_thunks(g):
        gb, gl = divmod(g, 8)
        q = gb % 2
        st_ = g % 2
        th = []
        pU = ps[0].bitcast(BF16)
        for blk in range(8):
            th.append(lambda blk=blk: P.tr(pU[:, blk * 128:(blk + 1) * 128], hcg[:, blk, gl, :], k.ident,
                                           [bhcg, k.bconst], [pb[0]]))
        th.append(lambda: P.cp("scalar", U[st_], pU, [pb[0]], [bU[st_]]))
        pW = ps[1].bitcast(BF16)
        for d in range(2):
            gd = d * 64 + g

            def tw(d=d):
                P.mm(ps[1][:, 0:128], Qm[q][:, d, gl, :], Pm[q][:, d, gl, :], True, True, [bgen[q]], [pb[1]])
                P.tt(TT[st_][d], ps[1][:, 0:128], msk[:, d, :], ALU.mult, [pb[1], bmsk], [bTT[st_][d]])
                P.tr(pW[:, 512:640], Qm[q][:, d, gl, :], k.ident, [bgen[q], k.bconst], [pb[1]])
                P.cp("scalar", WT[st_][d], pW[:, 512:640], [pb[1]], [bWT[st_][d]])
            th.append(tw)
            for kk in range(10):
                def mk(d=d, kk=kk, gd=gd):
                    j_ = tjc[0] % NTJ
                    tjc[0] += 1
                    P.act(tJ[j_], k.jswap, AF.Copy, [k.bconst] + R_, [btJ[j_]], scale=KS2[:, kk, gd:gd + 1])
                    P.stt(Mk[st_][d][:, kk, :], k.identf, KSr[:, kk, gd:gd + 1], tJ[j_], ALU.mult, ALU.add,
                          [k.bconst, btJ[j_]] + R_, [bMk[st_][d]])
                th.append(mk)
        return th

    pending = []

    def pop(n):
        for _ in range(min(n, len(pending))):
            pending.pop(0)()

    pending += batch_thunks(0)
    pending += group_thunks(0)
    for g in range(64):
        gb, gl = divmod(g, 8)
        q = gb % 2
        st_ = g % 2
        pop(len(pending))
        if g + 1 < 64:
            if gl == 7:
                pending += batch_thunks(gb + 1)
            pending += group_thunks(g + 1)
        for d in range(2):
            for hf in range(2):
                P.mm(chain_ps[d][:, hf * 512:(hf + 1) * 512], WT[st_][d], U[st_][:, hf * 512:(hf + 1) * 512],
                     True, True, [bWT[st_][d], bU[st_]], [chain_pb[d][hf]])
        for d in range(2):
            P.cp(cp_eng[d], Ib[d][:, 1:1025], chain_ps[d], chain_pb[d], [bIb[d]])
        pop(4)
        for kk in range(10):
            s_ = 1 << kk
            for d in range(2):
                lo, hi = (s_, 1024) if d == 0 else (0, 1024 - s_)
                for (a_, b_) in ((0, 512), (512, 1024)):
                    l2, h2 = max(lo, a_), min(hi, b_)
                    if l2 >= h2:
                        continue
                    src0 = 1 + l2 - s_ if d == 0 else 1 + l2 + s_
                    P.mm(chain_ps[d][:, l2:h2], Mk[st_][d][:, kk, :], Ib[d][:, src0:src0 + (h2 - l2)],
                         False, True, [bMk[st_][d], bIb[d]], [chain_pb[d][a_ // 512]])
            for d in range(2):
                lo, hi = (s_, 1024) if d == 0 else (0, 1024 - s_)
                P.cp(cp_eng[d], Ib[d][:, 1 + lo:1 + hi], chain_ps[d][:, lo:hi], chain_pb[d], [bIb[d]])
            pop(5)
        for blk in range(8):
            bank = 6 + blk // 4
            osl = slice((blk % 4) * 128, (blk % 4) * 128 + 128)
            csl = slice(blk * 128, (blk + 1) * 128)
            P.mm(ps[bank][:, osl], U[st_][:, csl], TT[st_][0], True, False, [bU[st_], bTT[st_][0]], [pb[bank]])
            P.mm(ps[bank][:, osl], Ib[0][:, blk * 128:blk * 128 + 128], Po[q][:, 0, gl, :], False, False,
                 [bIb[0], bgen[q]], [pb[bank]])
            P.mm(ps[bank][:, osl], U[st_][:, csl], TT[st_][1], False, False, [bU[st_], bTT[st_][1]], [pb[bank]])
            P.mm(ps[bank][:, osl], Ib[1][:, blk * 128 + 2:blk * 128 + 130], Po[q][:, 1, gl, :], False, True,
                 [bIb[1], bgen[q]], [pb[bank]])
        for hb_ in range(2):
            P.cp("vector" if hb_ == 0 else "scalar",
                 ycm[:, hb_ * 4:hb_ * 4 + 4, :, gl * 16:(gl + 1) * 16],
                 ps[6 + hb_].rearrange("p (b i c) -> p b i c", b=4, i=8), [pb[6 + hb_]], [bycm])
        pop(6)
        if gl == 7:
            for blk in range(8):
                P.dma("gpsimd",
                      T["ys5"][blk * 1024:(blk + 1) * 1024, gb * 128:(gb + 1) * 128].rearrange("(c i) f -> c i f", i=8),
                      ycm[:, blk, :, :], bycm, reads=[bycm], writes=[k.b_ys5])
    P.barrier(bufs)
    P.release(bufs)

    P.reset_arena(k.keep)
    bufs = []
    m, bm = load_mods(P, k, T, 1, 0, bufs)
    W = norm_work(P, bufs)
    Wa = P.tile([128, 8, 1024], BF16); bWa = [Buf() for _ in range(8)]
    Wb = P.tile([128, 8, 1024], BF16); bWb = [Buf() for _ in range(8)]
    stg = [P.tile([128, 1024], F32) for _ in range(2)]
    bstg = [Buf("stg%d" % i) for i in range(2)]
    bufs += bstg
    si = [0]
    for kt in range(8):
        s_ = si[0] % 2
        si[0] += 1
        P.dma("sync", stg[s_], T["w_glu_a"][0, kt * 128:(kt + 1) * 128, :], bstg[s_], writes=[bstg[s_]])
        P.tt(Wa[:, kt, :], stg[s_], m[:, 2, :], ALU.mult, [bstg[s_], bm], [bWa[kt]], e="gpsimd")
    load_w_bf16(P, Wb, bWb, T["w_glu_b"][0], 8, 1024, stg, bstg, si)
    dsk = P.tile([128, D], F32); bdsk = Buf("dsk"); bufs.append(bdsk)
    P.dma("sync", dsk, T["s5_d"][0].partition_broadcast(128), bdsk, writes=[bdsk])
    Gd = P.tile([128, D], F32)
    Sd = P.tile([128, D], F32)
    bgs = Buf("GdSd")
    P.tt(Gd, m[:, 1, :], dsk, ALU.mult, [bm, bdsk], [bgs])
    P.tt(Sd, m[:, 0, :], dsk, ALU.mult, [bm, bdsk], [bgs])
    NXP = 4
    xt = [P.tile([128, D], F32) for _ in range(NXP)]
    bxt = [Buf("xt%d" % i) for i in range(NXP)]
    yt = [P.tile([128, D], F32) for _ in range(NXP)]
    byt = [Buf("yt%d" % i) for i in range(NXP)]
    bufs += bxt + byt
    zb = [P.tile([128, D], BF16) for _ in range(2)]; bzb = [Buf() for _ in range(2)]
    zT = [P.tile([128, 8, 128], BF16) for _ in range(2)]; bzT = [Buf() for _ in range(2)]
    sgm = [P.tile([128, D], F32) for _ in range(2)]; bsgm = [Buf() for _ in range(2)]
    banks = [(1, (1, 2), (3, 4)), (5, (5, 6), (7, 0))]

    def ldp(c):
        P.dma("sync", xt[c % NXP], T["x2"][c * 128:(c + 1) * 128, :], bxt[c % NXP], reads=[k.b_x2],
              writes=[bxt[c % NXP]])
        P.dma("sync", yt[c % NXP], T["ys5"][c * 128:(c + 1) * 128, :], byt[c % NXP], reads=[k.b_ys5],
              writes=[byt[c % NXP]])

    def front(c):
        i = c % 2
        X, bX, Y, bY = xt[c % NXP], bxt[c % NXP], yt[c % NXP], byt[c % NXP]
        W.flip()
        rms_rstd(P, X, W["ss"], W["bss"], W["junk"], Buf(), [bX])
        P.stt(W["tn"], X, W["ss"][:, 2:3], Gd, ALU.mult, ALU.mult, [bX, W["bss"], bgs], [W["btn"]])
        P.tt(Y, Y, W["tn"], ALU.add, [bY, W["btn"]], [bY])
        P.tt(Y, Y, Sd, ALU.add, [bY, bgs], [bY])
        P.act(zb[i], Y, AF.Gelu_apprx_tanh, [bY], [bzb[i]])
        transpose8(P, k, zb[i], bzb[i], zT[i], bzT[i], banks[i][0])

    def back(c):
        i = c % 2
        X, bX = xt[c % NXP], bxt[c % NXP]
        ba, bb = banks[i][1], banks[i][2]
        for hf in range(2):
            for kt in range(8):
                P.mm(ps[ba[hf]], zT[i][:, kt, :], Wa[:, kt, hf * 512:(hf + 1) * 512], kt == 0, kt == 7,
                     [bzT[i], bWa[kt]], [pb[ba[hf]]])
            for kt in range(8):
                P.mm(ps[bb[hf]], zT[i][:, kt, :], Wb[:, kt, hf * 512:(hf + 1) * 512], kt == 0, kt == 7,
                     [bzT[i], bWb[kt]], [pb[bb[hf]]])
        for hf in range(2):
            hsl = slice(hf * 512, (hf + 1) * 512)
            P.act(sgm[i][:, hsl], ps[bb[hf]], AF.Sigmoid, [pb[bb[hf]]], [bsgm[i]])
            P.tt(sgm[i][:, hsl], ps[ba[hf]], sgm[i][:, hsl], ALU.mult, [pb[ba[hf]], bsgm[i]], [bsgm[i]])
            P.tt(X[:, hsl], X[:, hsl], sgm[i][:, hsl], ALU.add, [bX, bsgm[i]], [bX])
        P.dma("gpsimd", T["x3"][c * 128:(c + 1) * 128, :], X, bX, reads=[bX], writes=[k.b_x3])

    ldp(0)
    ldp(1)
    ldp(2)
    front(0)
    for c in range(NCH):
        if c + 1 < NCH:
            front(c + 1)
        back(c)
        if c + 3 < NCH:
            ldp(c + 3)
    P.barrier(bufs)
    P.release(bufs)


def host_tables():
    L = 128
    nh = 4
    dh = 128
    inv = (10000.0 ** (-np.arange(0, dh, 2, dtype=np.float32) / dh)).astype(np.float32)
    ang = (np.arange(S, dtype=np.float32)[:, None] * inv[None, :]).astype(np.float32)
    rot = np.stack([np.cos(ang), np.sin(ang)], axis=1).astype(np.float32)
    rot_tab = rot.reshape(NCH, 128, 2, 64)
    log_g = np.log1p(-np.exp2(-5.0 - np.arange(nh, dtype=np.float32))).astype(np.float32)
    idx = np.arange(L, dtype=np.float32)
    sc = dh ** -0.5
    dist = np.abs(idx[:, None] - idx[None, :])
    dmat = np.exp(log_g[:, None, None] * dist).astype(np.float32)
    tab = np.zeros((5, 128, 512), np.float32)
    tab[0] = (dmat.transpose(2, 0, 1) * sc).reshape(128, 512)
    wqf = np.exp(log_g[:, None] * (idx + 1.0)[None, :]) * sc
    wqb = np.exp(log_g[:, None] * (L - idx)[None, :]) * sc
    tab[1] = np.broadcast_to(wqf.reshape(1, 512), (128, 512))
    tab[2] = np.broadcast_to(wqb.reshape(1, 512), (128, 512))
    gl = np.exp(log_g * L)
    tab[3] = np.broadcast_to(np.repeat(gl, 128).reshape(1, 512), (128, 512))
    tab[4, :, 0:4] = np.exp(log_g[None, :] * (L - 1.0 - idx)[:, None])
    tab[4, :, 4:8] = np.exp(log_g[None, :] * idx[:, None])
    ii = np.arange(128) // 16
    mask = np.stack([(ii[None, :] >= ii[:, None]), (ii[None, :] <= ii[:, None])]).astype(np.float32)
    return {"rot_tab": rot_tab.astype(np.float32), "ret_tab": tab.astype(np.float32), "s5_mask": mask}


IN_SHAPES = {
    "x": [S, D], "c": [1, D], "norm_g": [2, 2, D], "ada_w": [2, D, 6 * D], "ada_b": [2, 6 * D],
    "w_in": [1, D, 3072], "conv_w": [1, 31, 512], "conv_b": [1, 512], "cln_g": [1, 512], "cln_b": [1, 512],
    "w_out": [1, D, D], "s5_lam_re": [1, 2, 64, 64], "s5_lam_im": [1, 2, 64, 64], "s5_log_step": [1, 2, 64],
    "s5_b_re": [1, 2, 64, 64, 16], "s5_b_im": [1, 2, 64, 64, 16], "s5_c_re": [1, 2, 64, 16, 64],
    "s5_c_im": [1, 2, 64, 16, 64], "s5_d": [1, D], "w_glu_a": [1, D, D], "w_glu_b": [1, D, D],
    "w_fc1": [2, D, DFF], "w_fc2": [2, DFF, D], "norm_f": [D],
    "rot_tab": [NCH, 128, 2, 64], "ret_tab": [5, 128, 512], "s5_mask": [2, 128, 128],
}


def build(phases, debug=(), ext_in=()):
    nc = bass.Bass("TRN2", target_bir_lowering=False)
    T = {}
    for nm, shp in IN_SHAPES.items():
        T[nm] = nc.dram_tensor(nm, shp, F32, kind="ExternalInput").ap()

    def scratch(nm, shp, dt):
        kind = "ExternalOutput" if nm in debug else ("ExternalInput" if nm in ext_in else "Internal")
        T[nm] = nc.dram_tensor(nm, shp, dt, kind=kind).ap()
    scratch("modrows", [2, 6, D], F32)
    scratch("hT0", [128, 8, S + 2 * PADC], BF16)
    scratch("sb_all", [NCH, 128, 512], BF16)
    scratch("x1", [S, D], F32)
    scratch("x2", [S, D], F32)
    scratch("h1", [S, D], BF16)
    scratch("ys5", [S, D], F32)
    scratch("x3", [S, D], F32)
    T["out"] = nc.dram_tensor("out", [S, D], F32, kind="ExternalOutput").ap()
    with ExitStack() as es:
        P = Prog(nc, es)
        P.init_arena(204 * 1024)
        k = K()
        for nm in ["modrows", "hT0", "sb", "x1", "x2", "h1", "ys5", "x3", "out", "xin"]:
            setattr(k, "b_" + nm, Buf(nm))
        setup_consts(P, k)
        if "mod" in phases:
            phase_mod(P, k, T)
        if "l0" in phases:
            phase_l0(P, k, T)
        if "mlp0" in phases:
            phase_mlp(P, k, T, 0, T["x1"], k.b_x1, T["x2"], k.b_x2, False, h1_out=("s5" in phases))
        if "s5" in phases:
            phase_s5(P, k, T, do_pre=("mlp0" not in phases))
        if "mlp1" in phases:
            phase_mlp(P, k, T, 1, T["x3"], k.b_x3, T["out"], k.b_out, True)
        P.barrier([], skip=("sync",))
        P.emit()
        k.nops = {e: len(P.ops[e]) for e in ENGS}
    return nc, k


def make_in_map(inputs, b, tabs):
    m = {}
    for nm in IN_SHAPES:
        if nm in tabs:
            m[nm] = tabs[nm]
        elif nm == "x":
            m[nm] = np.ascontiguousarray(inputs["x"][b])
        elif nm == "c":
            m[nm] = np.ascontiguousarray(inputs["c"][b:b + 1])
        else:
            m[nm] = np.ascontiguousarray(np.asarray(inputs[nm], dtype=np.float32))
    return m


def kernel(**inputs):
    inputs = {kk: np.asarray(v) for kk, v in inputs.items()}
    tabs = host_tables()
    nc, _ = build(["mod", "l0", "mlp0", "s5", "mlp1"])
    in_maps = [make_in_map(inputs, b, tabs) for b in range(8)]
    res = run_bass_kernel_spmd(nc, in_maps, core_ids=list(range(8)))
    return np.stack([np.asarray(r["out"], dtype=np.float32) for r in res.results], axis=0)
```

```python
import math
from contextlib import ExitStack

import numpy as np
import ml_dtypes
import concourse.bass as bass
import concourse.mybir as mybir
from concourse.bass_utils import run_bass_kernel_spmd

F32 = mybir.dt.float32
BF16 = mybir.dt.bfloat16
I32 = mybir.dt.int32
ALU = mybir.AluOpType
AF = mybir.ActivationFunctionType

ENGS = ["tensor", "vector", "scalar", "gpsimd", "sync"]
S = 8192
D = 1024
DFF = 4096
EPS = 1e-6
NCH = 64
PADC = 16


class Buf:
    __slots__ = ("name", "w", "r", "dsem", "dcnt", "ssem", "scnt")

    def __init__(self, name=""):
        self.name = name
        self.w = None
        self.r = []
        self.dsem = None
        self.dcnt = 0
        self.ssem = None
        self.scnt = 0


class Prog:
    def __init__(self, nc, es):
        self.nc = nc
        self.es = es
        self.ops = {e: [] for e in ENGS}
        self.cnt = {e: 0 for e in ENGS}
        self.sem = {e: es.enter_context(nc.semaphore("se_" + e)) for e in ENGS}
        self.seen = {e: {} for e in ENGS}
        self.dsems = []
        self.ssems = []
        self.free_dsems = []
        self.arena = None
        self.aoff = 0

    def init_arena(self, nbytes):
        self.arena = self.es.enter_context(self.nc.sbuf_tensor("arena", [128, nbytes // 2], BF16))
        self.asize = nbytes
        self.aoff = 0

    def reset_arena(self, keep=0):
        self.aoff = keep
        self.atop = self.asize

    def tile_top(self, shape, dt):
        n = 1
        for s_ in shape[1:]:
            n *= s_
        nb = (n * 4 + 63) // 64 * 64
        self.atop -= nb
        assert self.atop >= self.aoff
        ap = self.arena[0:shape[0], self.atop // 2:(self.atop + n * 4) // 2].bitcast(dt)
        if len(shape) == 3:
            ap = ap.rearrange("p (a b) -> p a b", a=shape[1])
        elif len(shape) == 4:
            ap = ap.rearrange("p (a b c) -> p a b c", a=shape[1], b=shape[2])
        return ap

    def tile(self, shape, dt):
        esz = 4 if dt in (F32, I32) else 2
        n = 1
        for s_ in shape[1:]:
            n *= s_
        nb = (n * esz + 63) // 64 * 64
        assert self.aoff + nb <= getattr(self, "atop", self.asize), ("SBUF arena overflow", self.aoff, nb, self.asize)
        ap = self.arena[0:shape[0], self.aoff // 2:(self.aoff + n * esz) // 2]
        self.aoff += nb
        if esz == 4:
            ap = ap.bitcast(dt)
        if len(shape) == 3:
            ap = ap.rearrange("p (a b) -> p a b", a=shape[1])
        elif len(shape) == 4:
            ap = ap.rearrange("p (a b c) -> p a b c", a=shape[1], b=shape[2])
        return ap

    def _dsem(self, b):
        if b.dsem is None:
            if self.free_dsems:
                b.dsem, b.dcnt = self.free_dsems.pop()
            else:
                s_ = self.es.enter_context(self.nc.semaphore("sd%d" % len(self.dsems)))
                self.dsems.append(s_)
                b.dsem, b.dcnt = s_, 0
        return b.dsem

    def release(self, bufs):
        for b in bufs:
            if b.dsem is not None:
                self.free_dsems.append((b.dsem, b.dcnt))
                b.dsem = None

    def _waits(self, e, reads, writes):
        evs = []
        for b in reads:
            if b.w is not None:
                evs.append(b.w)
        for b in writes:
            if b.w is not None:
                evs.append(b.w)
            evs.extend(b.r)
        out = {}
        for (s_, v) in evs:
            if e == "tensor" and s_ is self.sem["tensor"]:
                continue
            if self.seen[e].get(s_.name, -1) >= v:
                continue
            if out.get(s_.name, (None, -1))[1] < v:
                out[s_.name] = (s_, v)
        for (s_, v) in out.values():
            self.seen[e][s_.name] = v
        return list(out.values())

    def op(self, e, fn, reads=(), writes=()):
        waits = self._waits(e, reads, writes)
        self.cnt[e] += 1
        ev = (self.sem[e], self.cnt[e])
        for b in reads:
            b.r.append(ev)
        for b in writes:
            b.w = ev
            b.r = []
        self.ops[e].append((waits, fn, ev[0], 1))
        return ev

    def dma(self, e, out, in_, sbuf_buf, reads=(), writes=(), **kw):
        waits = self._waits(e, reads, writes)
        if e == "gpsimd":
            if sbuf_buf.ssem is None:
                sbuf_buf.ssem = self.es.enter_context(self.nc.semaphore("ss%d" % len(self.ssems)))
                self.ssems.append(sbuf_buf)
                sbuf_buf.scnt = 0
            s_ = sbuf_buf.ssem
            sbuf_buf.scnt += 16
            ev = (s_, sbuf_buf.scnt)
        else:
            s_ = self._dsem(sbuf_buf)
            sbuf_buf.dcnt += 16
            ev = (s_, sbuf_buf.dcnt)
        for b in reads:
            b.r.append(ev)
        for b in writes:
            b.w = ev
            b.r = []
        self.ops[e].append((waits, (lambda eng: eng.dma_start(out=out, in_=in_, **kw)), s_, 16))
        return ev

    def barrier(self, all_bufs=(), skip=()):
        evs = [(self.sem[f], self.cnt[f]) for f in ENGS if self.cnt[f] > 0]
        live = {}
        for b in all_bufs:
            if b.dsem is not None:
                live[b.dsem.name] = (b.dsem, b.dcnt)
        for (s_, c_) in self.free_dsems:
            live.setdefault(s_.name, (s_, c_))
        for b in self.ssems:
            live[b.ssem.name] = (b.ssem, b.scnt)
        evs += [v for v in live.values() if v[1] > 0]
        for e in ENGS:
            if e in skip:
                continue
            w = []
            for (s_, v) in evs:
                if self.seen[e].get(s_.name, -1) >= v:
                    continue
                self.seen[e][s_.name] = v
                w.append((s_, v))
            if w:
                self.ops[e].append((w, None, None, 0))

    def emit(self):
        with self.nc.Block() as block:
            def mk(e):
                def body(eng):
                    for (waits, fn, s_, inc) in self.ops[e]:
                        for (ws, wv) in waits:
                            eng.wait_ge(ws, wv)
                        if fn is not None:
                            fn(eng).then_inc(s_, inc)
                return body
            block.tensor(mk("tensor"))
            block.vector(mk("vector"))
            block.scalar(mk("scalar"))
            block.gpsimd(mk("gpsimd"))
            block.sync(mk("sync"))

    def act(self, out, in_, func, r, w, **kw):
        return self.op("scalar", lambda a: a.activation(out=out, in_=in_, func=func, **kw), r, w)

    def tt(self, out, in0, in1, op, r, w, e="vector"):
        return self.op(e, lambda v: v.tensor_tensor(out=out, in0=in0, in1=in1, op=op), r, w)

    def ts(self, out, in0, s1, s2, op0, op1, r, w, e="vector"):
        if s2 is None:
            return self.op(e, lambda v: v.tensor_scalar(out=out, in0=in0, scalar1=s1, scalar2=None,
                                                        op0=op0), r, w)
        return self.op(e, lambda v: v.tensor_scalar(out=out, in0=in0, scalar1=s1, scalar2=s2,
                                                    op0=op0, op1=op1), r, w)

    def stt(self, out, in0, sc, in1, op0, op1, r, w):
        return self.op("vector", lambda v: v.scalar_tensor_tensor(out=out, in0=in0, scalar=sc, in1=in1,
                                                                  op0=op0, op1=op1), r, w)

    def cp(self, e, out, in_, r, w):
        if e == "scalar":
            return self.act(out, in_, AF.Copy, r, w)
        return self.op(e, lambda v: v.tensor_copy(out=out, in_=in_), r, w)

    def mm(self, out, lhsT, rhs, start, stop, r, w):
        return self.op("tensor", lambda t: t.matmul(out, lhsT=lhsT, rhs=rhs, start=start, stop=stop), r, w)

    def tr(self, out, in_, ident, r, w):
        return self.op("tensor", lambda t: t.transpose(out=out, in_=in_, identity=ident), r, w)

    def memset(self, e, ap, val, w):
        return self.op(e, lambda g: g.memset(ap, val), (), w)

    def recip(self, out, in_, r, w):
        return self.op("vector", lambda v: v.reciprocal(out=out, in_=in_), r, w)


class K:
    pass


def bcast_mid(ap2, n):
    return ap2.unsqueeze(1).to_broadcast([ap2.shape[0], n, ap2.shape[1]])


def bcast_last(ap2, n):
    return ap2.unsqueeze(2).to_broadcast([ap2.shape[0], ap2.shape[1], n])


def setup_consts(P, k):
    k.identf = P.tile([128, 128], F32)
    k.ident = P.tile([128, 128], BF16)
    k.jswap = P.tile([128, 128], F32)
    k.bconst = Buf("const")
    b = k.bconst
    P.memset("gpsimd", k.identf, 1.0, [b])
    P.op("gpsimd", lambda g: g.affine_select(out=k.identf, in_=k.identf, pattern=[[-1, 128]],
                                             compare_op=ALU.is_equal, fill=0.0, base=0,
                                             channel_multiplier=1), [b], [b])
    P.cp("gpsimd", k.ident, k.identf, [b], [b])
    P.memset("gpsimd", k.jswap, 0.0, [b])
    P.cp("gpsimd", k.jswap[0:64, 64:128], k.identf[0:64, 0:64], [b], [b])
    P.cp("gpsimd", k.jswap[64:128, 0:64], k.identf[64:128, 64:128], [b], [b])
    k.psbig = P.es.enter_context(P.nc.psum_tensor("psbig", [128, 4096], F32))
    k.ps = [k.psbig[:, i * 512:(i + 1) * 512] for i in range(8)]
    k.pb = [Buf("psb%d" % i) for i in range(8)]
    k.keep = P.aoff


def rms_rstd(P, xin, ss, bss, junk, bjunk, rx, col=0, n=D):
    P.act(junk, xin, AF.Square, rx, [bjunk, bss], accum_out=ss[:, col:col + 1])
    P.act(ss[:, col + 1:col + 2], ss[:, col:col + 1], AF.Sqrt, [bss], [bss], scale=1.0 / n, bias=EPS)
    P.recip(ss[:, col + 2:col + 3], ss[:, col + 1:col + 2], [bss], [bss])


_EPS_AP = {}


def k_eps(P):
    return _EPS_AP["ap"]


def load_w_bf16(P, dst, bdst_list, w, kts, ncols, stg, bstg, si, col0=0, scale_cols=None):
    cw = min(ncols, stg[0].shape[1])
    for kt in range(kts):
        for c0 in range(0, ncols, cw):
            s_ = si[0] % 2
            si[0] += 1
            P.dma("sync", stg[s_][:, 0:cw], w[kt * 128:(kt + 1) * 128, col0 + c0:col0 + c0 + cw], bstg[s_],
                  writes=[bstg[s_]])
            P.cp(("gpsimd", "vector", "scalar")[si[0] % 3], dst[:, kt, c0:c0 + cw], stg[s_][:, 0:cw], [bstg[s_]],
                 [bdst_list[kt]])


def phase_mod(P, k, T):
    P.reset_arena(k.keep)
    bufs = []
    cT = P.tile([128, 8], F32)
    bc = Buf("cT"); bufs.append(bc)
    P.dma("sync", cT, T["c"][0, :].rearrange("(kt p) -> p kt", p=128), bc, writes=[bc],
          allow_slow_non_contiguous=True)
    P.act(cT, cT, AF.Silu, [bc], [bc])
    adab = P.tile([1, 2 * 6144], F32)
    ng = P.tile([1, 4 * 1024], F32)
    bsm = Buf("small"); bufs.append(bsm)
    P.dma("sync", adab, T["ada_b"].rearrange("l n -> (l n)").unsqueeze(0), bsm, writes=[bsm])
    P.dma("sync", ng, T["norm_g"].rearrange("l t n -> (l t n)").unsqueeze(0), bsm, writes=[bsm])
    MR = P.tile([1, 2 * 6144], F32)
    bmr = Buf("MR"); bufs.append(bmr)
    NB = 2
    aw = [P.tile([128, 8, 512], F32) for _ in range(NB)]
    baw = [Buf("aw%d" % i) for i in range(NB)]
    bufs += baw
    it = 0
    for l in range(2):
        for cb in range(12):
            s_ = it % NB
            P.dma("sync", aw[s_], T["ada_w"][l, :, cb * 512:(cb + 1) * 512].rearrange("(kt p) n -> p kt n", p=128),
                  baw[s_], writes=[baw[s_]])
            bank = it % 2
            for kt in range(8):
                P.mm(k.ps[bank][0:1, :], cT[:, kt:kt + 1], aw[s_][:, kt, :], kt == 0, kt == 7,
                     [bc, baw[s_]], [k.pb[bank]])
            o = l * 6144 + cb * 512
            P.tt(MR[:, o:o + 512], k.ps[bank][0:1, :], adab[:, o:o + 512], ALU.add, [k.pb[bank], bsm], [bmr])
            it += 1
    for l in range(2):
        for t in range(2):
            o = l * 6144 + (1 + 3 * t) * 1024
            P.stt(MR[:, o:o + 1024], MR[:, o:o + 1024], 1.0, ng[:, (l * 2 + t) * 1024:(l * 2 + t + 1) * 1024],
                  ALU.add, ALU.mult, [bmr, bsm], [bmr])
    P.dma("sync", T["modrows"].rearrange("l t n -> (l t n)").unsqueeze(0), MR, bmr, reads=[bmr],
          writes=[k.b_modrows])
    P.barrier(bufs)
    P.release(bufs)


def load_mods(P, k, T, l, which, bufs):
    m = P.tile([128, 3, 1024], F32)
    bm = Buf("mods"); bufs.append(bm)
    for i in range(3):
        P.dma("sync", m[:, i, :], T["modrows"][l, which * 3 + i, :].partition_broadcast(128), bm,
              reads=[k.b_modrows], writes=[bm])
    return m, bm


class NormW(dict):
    def __init__(self, sets):
        super().__init__()
        self.sets = sets
        self.i = 0

    def __getitem__(self, key):
        return self.sets[self.i][key]

    def flip(self):
        self.i ^= 1


def norm_mod_tile(P, k, X, bX, m, bm, W):
    W.flip()
    rms_rstd(P, X, W["ss"], W["bss"], W["junk"], Buf(), [bX])
    P.stt(W["tn"], X, W["ss"][:, 2:3], m[:, 1, :], ALU.mult, ALU.mult, [bX, W["bss"], bm], [W["btn"]])
    P.tt(W["hb"], W["tn"], m[:, 0, :], ALU.add, [W["btn"], bm], [W["bhb"]])


def norm_work(P, bufs, with_f32=False):
    sets = []
    junk = P.tile([128, D], BF16)
    for _ in range(2):
        Wd = {}
        Wd["ss"] = P.tile([128, 8], F32); Wd["bss"] = Buf("ss")
        Wd["junk"] = junk; Wd["bjunk"] = Buf("junk")
        Wd["tn"] = P.tile([128, D], F32); Wd["btn"] = Buf("tn")
        Wd["hb"] = P.tile([128, D], BF16); Wd["bhb"] = Buf("hb")
        sets.append(Wd)
    return NormW(sets)


def transpose8(P, k, src, bsrc, dst3, bdst, bank, n=8, evac="scalar"):
    pT = k.ps[bank].bitcast(BF16)
    for kt in range(n):
        P.tr(pT[:, kt * 128:(kt + 1) * 128], src[:, kt * 128:(kt + 1) * 128], k.ident,
             [bsrc, k.bconst], [k.pb[bank]])
    P.cp(evac, dst3, pT[:, 0:n * 128].rearrange("p (k t) -> p k t", k=n), [k.pb[bank]], [bdst])


def phase_mlp(P, k, T, l, x_in, b_in, x_out, b_out, final, h1_out=False):
    P.reset_arena(k.keep)
    bufs = []
    TB = 256
    NS = 2
    W1 = P.tile([128, 8, DFF], BF16)
    W2 = P.tile([128, 32, D], BF16)
    bW1 = [Buf() for _ in range(8)]
    bW2 = [Buf() for _ in range(32)]
    stg = [P.tile([128, 1024], F32) for _ in range(2)]
    bstg = [Buf("stg%d" % i) for i in range(2)]
    bufs += bstg
    si = [0]
    m, bm = load_mods(P, k, T, l, 1, bufs)
    fg = None
    if final:
        fg = P.tile([128, D], F32)
        bfg = Buf("fg"); bufs.append(bfg)
        P.dma("sync", fg, T["norm_f"].partition_broadcast(128), bfg, writes=[bfg])
    load_w_bf16(P, W1, bW1, T["w_fc1"][l], 8, DFF, stg, bstg, si)
    w2v = T["w_fc2"][l]
    for kt in range(32):
        s_ = si[0] % 2
        si[0] += 1
        P.dma("sync", stg[s_], w2v[kt * 128:(kt + 1) * 128, :], bstg[s_], writes=[bstg[s_]])
        P.tt(W2[:, kt, :], stg[s_], m[:, 2, :], ALU.mult, [bstg[s_], bm], [bW2[kt]],
             e=("gpsimd", "vector")[kt % 2])
    NX = 2
    xt = [P.tile([128, NS, D], F32) for _ in range(NX)]
    bxt = [Buf("xt%d" % i) for i in range(NX)]
    bufs += bxt
    W = norm_work(P, bufs)
    if h1_out:
        bufs += [W.sets[0]["bhb"], W.sets[1]["bhb"]]
        m1 = P.tile([128, 2, 1024], F32)
        bm1 = Buf("m1"); bufs.append(bm1)
        for i_ in range(2):
            P.dma("sync", m1[:, i_, :], T["modrows"][1, i_, :].partition_broadcast(128), bm1,
                  reads=[k.b_modrows], writes=[bm1])
    hT = [P.tile([128, 8, TB], BF16) for _ in range(2)]
    bhT = [Buf("hT0"), Buf("hT1")]
    NA = 4
    actT = [P.tile([128, TB], BF16) for _ in range(NA)]
    bact = [Buf() for _ in range(NA)]
    rl = [P.tile([128, TB], F32) for _ in range(2)]
    brl = [Buf() for _ in range(2)]
    ps, pb = k.ps, k.pb
    nblk = S // TB

    def load_x(blk):
        xb = blk % NX
        P.dma("sync", xt[xb], x_in[blk * TB:(blk + 1) * TB, :].rearrange("(s p) f -> p s f", p=128),
              bxt[xb], reads=[b_in], writes=[bxt[xb]])

    def front(blk):
        X, bX = xt[blk % NX], bxt[blk % NX]
        for s_ in range(NS):
            norm_mod_tile(P, k, X[:, s_, :], bX, m, bm, W)
            transpose8(P, k, W["hb"], W["bhb"], hT[blk % 2][:, :, s_ * 128:(s_ + 1) * 128], bhT[blk % 2], 0)

    def fc1(blk, j):
        bank = 1 + (j % 2)
        for kt in range(8):
            P.mm(ps[bank][:, 0:TB], W1[:, kt, j * 128:(j + 1) * 128], hT[blk % 2][:, kt, :], kt == 0, kt == 7,
                 [bW1[kt], bhT[blk % 2]], [pb[bank]])
        a = j % NA
        r = j % 2
        P.act(rl[r], ps[bank][:, 0:TB], AF.Relu, [pb[bank]], [brl[r]])
        P.tt(actT[a], rl[r], rl[r], ALU.mult, [brl[r]], [bact[a]])

    def fc2(blk, j):
        a = j % NA
        for s_ in range(NS):
            for h in range(2):
                bank = 4 + s_ * 2 + h
                P.mm(ps[bank], actT[a][:, s_ * 128:(s_ + 1) * 128], W2[:, j, h * 512:(h + 1) * 512],
                     j == 0, j == 31, [bact[a], bW2[j]], [pb[bank]])

    tails = []

    def tail(blk):
        X, bX = xt[blk % NX], bxt[blk % NX]
        if final:
            for s_ in range(NS):
                W.flip()
                rms_rstd(P, X[:, s_, :], W["ss"], W["bss"], W["junk"], Buf(), [bX], col=4)
                P.stt(X[:, s_, :], X[:, s_, :], W["ss"][:, 6:7], fg, ALU.mult, ALU.mult,
                      [bX, W["bss"], bfg], [bX])
        P.dma("gpsimd", x_out[blk * TB:(blk + 1) * TB, :].rearrange("(s p) f -> p s f", p=128), X, bX,
              reads=[bX], writes=[b_out])
        if h1_out:
            for s_ in range(NS):
                W.flip()
                rms_rstd(P, X[:, s_, :], W["ss"], W["bss"], W["junk"], Buf(), [bX], col=4)
                P.stt(W["tn"], X[:, s_, :], W["ss"][:, 6:7], m1[:, 1, :], ALU.mult, ALU.mult,
                      [bX, W["bss"], bm1], [W["btn"]])
                P.tt(W["hb"], W["tn"], m1[:, 0, :], ALU.add, [W["btn"], bm1], [W["bhb"]])
                r0 = blk * TB + s_ * 128
                P.dma("gpsimd", T["h1"][r0:r0 + 128, :], W["hb"], W["bhb"], reads=[W["bhb"]], writes=[k.b_h1])
        if blk + 2 < nblk:
            load_x(blk + 2)

    load_x(0)
    load_x(1)
    front(0)
    for blk in range(nblk):
        X, bX = xt[blk % NX], bxt[blk % NX]
        fc1(blk, 0)
        for j in range(32):
            if j + 1 < 32:
                fc1(blk, j + 1)
            fc2(blk, j)
            if j == 4 and tails:
                tail(tails.pop(0))
            if j == 14 and blk + 1 < nblk:
                front(blk + 1)
        for s_ in range(NS):
            for h in range(2):
                bank = 4 + s_ * 2 + h
                P.tt(X[:, s_, h * 512:(h + 1) * 512], ps[bank], X[:, s_, h * 512:(h + 1) * 512], ALU.add,
                     [pb[bank], bX], [bX])
        tails.append(blk)
    while tails:
        tail(tails.pop(0))
    P.barrier(bufs)
    P.release(bufs)


def phase_l0(P, k, T):
    P.reset_arena(k.keep)
    bufs = []
    ps, pb = k.ps, k.pb
    Win = P.tile([128, 8, 3072], BF16)
    bWin = [Buf() for _ in range(8)]
    Wout = P.tile([128, 8, 1024], BF16)
    bWout = [Buf() for _ in range(8)]
    stg = [P.tile([128, 1536], F32) for _ in range(2)]
    bstg = [Buf("stg%d" % i) for i in range(2)]
    bufs += bstg
    si = [0]
    m, bm = load_mods(P, k, T, 0, 0, bufs)
    for kt in range(8):
        for c0 in (0, 1536):
            s_ = si[0] % 2
            si[0] += 1
            P.dma("sync", stg[s_][:, 0:1536], T["w_in"][0, kt * 128:(kt + 1) * 128, c0:c0 + 1536], bstg[s_],
                  writes=[bstg[s_]])
            P.cp(("gpsimd", "vector", "scalar")[si[0] % 3], Win[:, kt, c0:c0 + 1536], stg[s_][:, 0:1536],
                 [bstg[s_]], [bWin[kt]])
    load_w_bf16(P, Wout, bWout, T["w_out"][0], 8, 1024, stg, bstg, si)
    rt = P.tile([128, 5, 512], F32)
    brt = Buf("rt"); bufs.append(brt)
    P.dma("sync", rt, T["ret_tab"].rearrange("t p n -> p t n"), brt, writes=[brt])
    DMT, WQF, WQB, GL = rt[:, 0, :], rt[:, 1, :], rt[:, 2, :], rt[:, 3, :]
    wkf, wkb = rt[:, 4, 0:4], rt[:, 4, 4:8]
    cw = P.tile([128, 4, 31], F32)
    cpar = P.tile([128, 3, 4], F32)
    bcp = Buf("convp"); bufs.append(bcp)
    for j in range(4):
        P.dma("sync", cw[:, j, :], T["conv_w"][0][:, j * 128:(j + 1) * 128].rearrange("w c -> c w"), bcp,
              writes=[bcp], allow_slow_non_contiguous=True)
    for i, nm in enumerate(["conv_b", "cln_g", "cln_b"]):
        P.dma("sync", cpar[:, i, :], T[nm][0].rearrange("(j c) -> c j", c=128), bcp, writes=[bcp],
              allow_slow_non_contiguous=True)
    DG = P.tile([128, 4 * 31, 128], BF16)
    bDG = Buf("DG")
    for j in range(4):
        for w in range(31):
            if (j * 31 + w) % 2 == 0:
                P.ts(DG[:, j * 31 + w, :], k.identf, cw[:, j, w:w + 1], None, ALU.mult, None, [bcp, k.bconst], [bDG])
            else:
                P.act(DG[:, j * 31 + w, :], k.identf, AF.Copy, [bcp, k.bconst], [bDG], scale=cw[:, j, w:w + 1])
    onesf = P.tile([128, 128], F32)
    bones = Buf("ones")
    P.memset("gpsimd", onesf, 1.0 / 512.0, [bones])

    W = norm_work(P, bufs)
    cs = [P.tile([128, 2, 64], F32) for _ in range(2)]
    bcs = [Buf("cs%d" % i) for i in range(2)]
    bufs += bcs
    tA = P.tile([128, 8, 64], F32); btA = Buf()
    tB = P.tile([128, 8, 64], F32); btB = Buf()
    qk = P.tile([128, 2, 512], BF16); bqk = Buf("qk")
    vbf = P.tile([128, 2, 512], BF16); bvbf = Buf("vbf")
    markA = P.aoff
    xt = [P.tile([128, D], F32) for _ in range(2)]
    bxt = [Buf("xt%d" % i) for i in range(2)]
    bufs += bxt
    hTc = [P.tile([128, 8, 128], BF16) for _ in range(2)]
    bhTc = [Buf("hTc%d" % i) for i in range(2)]
    bufs += bhTc
    R = P.tile([128, 512], F32); bR = Buf("R")
    Rb = [P.tile([128, 512], BF16) for _ in range(2)]
    bRb = [Buf("Rb%d" % i) for i in range(2)]
    bufs += bRb
    zero_bf = P.tile([128, 8, PADC], BF16); bz = Buf("zero"); bufs.append(bz)
    P.memset("gpsimd", zero_bf, 0.0, [bz])
    hT0 = T["hT0"]
    P.dma("sync", hT0[:, :, 0:PADC], zero_bf, bz, reads=[bz], writes=[k.b_hT0])
    P.dma("sync", hT0[:, :, PADC + S:PADC + S + PADC], zero_bf, bz, reads=[bz], writes=[k.b_hT0])

    def rotary(src_ps, dst, nh, bsrc, bdst, cst, bcst):
        sv = src_ps.rearrange("p (h t d) -> p h t d", h=nh, t=2)
        dv = dst.rearrange("p (h t d) -> p h t d", h=nh, t=2)
        cosb = bcast_mid(cst[:, 0, :], nh)
        sinb = bcast_mid(cst[:, 1, :], nh)
        a_, b_ = tA[:, 0:nh, :], tB[:, 0:nh, :]
        P.tt(a_, sv[:, :, 0, :], cosb, ALU.mult, [bsrc, bcst], [btA])
        P.tt(b_, sv[:, :, 1, :], sinb, ALU.mult, [bsrc, bcst], [btB])
        P.tt(dv[:, :, 0, :], a_, b_, ALU.subtract, [btA, btB], [bdst])
        P.tt(a_, sv[:, :, 0, :], sinb, ALU.mult, [bsrc, bcst], [btA])
        P.tt(b_, sv[:, :, 1, :], cosb, ALU.mult, [bsrc, bcst], [btB])
        P.tt(dv[:, :, 1, :], a_, b_, ALU.add, [btA, btB], [bdst])

    P.memset("vector", R, 0.0, [bR])
    P.memset("vector", Rb[0], 0.0, [bRb[0]])
    P.memset("vector", Rb[1], 0.0, [bRb[1]])
    order = list(range(NCH - 1, -1, -1))

    def loadA(i):
        c = order[i]
        P.dma("sync", xt[i % 2], T["x"][c * 128:(c + 1) * 128, :], bxt[i % 2], writes=[bxt[i % 2]])
        P.dma("sync", cs[i % 2], T["rot_tab"][c], bcs[i % 2], writes=[bcs[i % 2]])
    def frontA(i):
        c = order[i]
        X, bX = xt[i % 2], bxt[i % 2]
        H, bH = hTc[i % 2], bhTc[i % 2]
        norm_mod_tile(P, k, X, bX, m, bm, W)
        transpose8(P, k, W["hb"], W["bhb"], H, bH, 0)
        P.dma("gpsimd", hT0[:, :, PADC + c * 128:PADC + (c + 1) * 128], H, bH, reads=[bH], writes=[k.b_hT0])

    def backA(i):
        c = order[i]
        H, bH = hTc[i % 2], bhTc[i % 2]
        kb_, vb_ = (1, 2) if i % 2 == 0 else (4, 5)
        for kt in range(8):
            P.mm(ps[kb_], H[:, kt, :], Win[:, kt, 1536:2048], kt == 0, kt == 7, [bH, bWin[kt]], [pb[kb_]])
        for kt in range(8):
            P.mm(ps[vb_], H[:, kt, :], Win[:, kt, 2048:2560], kt == 0, kt == 7, [bH, bWin[kt]], [pb[vb_]])
        rotary(ps[kb_], qk[:, 1, :], 4, pb[kb_], bqk, cs[i % 2], bcs[i % 2])
        P.tt(vbf[:, 1, :].rearrange("p (h e) -> p h e", h=4), ps[vb_].rearrange("p (h e) -> p h e", h=4),
             bcast_last(wkb, 128), ALU.mult, [pb[vb_], brt], [bvbf])
        for h in range(4):
            P.mm(ps[3][:, h * 128:(h + 1) * 128], qk[:, 1, h * 128:(h + 1) * 128], vbf[:, 1, h * 128:(h + 1) * 128],
                 True, True, [bqk, bvbf], [pb[3]])
        rb, brb = Rb[i % 2], bRb[i % 2]
        P.cp("scalar", rb, R, [bR], [brb])
        P.dma("gpsimd", T["sb_all"][c], rb, brb, reads=[brb], writes=[k.b_sb])
        P.tt(R, R, GL, ALU.mult, [bR, brt], [bR])
        P.tt(R, R, ps[3], ALU.add, [bR, pb[3]], [bR])

    loadA(0)
    loadA(1)
    frontA(0)
    for i in range(NCH):
        if i + 1 < NCH:
            frontA(i + 1)
        backA(i)
        if i + 2 < NCH:
            loadA(i + 2)

    P.barrier(bufs)
    P.aoff = markA
    hTh = [P.tile([128, 8, 160], BF16) for _ in range(2)]
    bhTh = [Buf("hTh%d" % i) for i in range(2)]
    bufs += bhTh
    sbt = [P.tile([128, 512], BF16) for _ in range(2)]
    bsbt = [Buf("sbt%d" % i) for i in range(2)]
    bufs += bsbt
    sig = P.tile([128, 4, 158], F32); bsig = Buf()
    abf = P.tile([128, 4, 158], BF16); babf = Buf()
    conv = P.tile([128, 4, 128], F32); bconv = Buf()
    csq = P.tile([128, 4, 128], F32); bcsq = Buf()
    st = P.tile([128, 4, 128], F32); bst = Buf()
    aT = P.tile([128, 8, 128], BF16); baT = Buf("aT")
    sg = P.tile([128, 512], F32); bsg = Buf()
    qT = P.tile([128, 4, 512], BF16); bqT = Buf("qT")
    sT = P.tile([128, 512], BF16); bsT = Buf()
    rr = P.tile([128, 512], BF16); brr = Buf()
    hs = P.tile([128, 16], F32); bhs = Buf()
    Sf = P.tile([128, 512], F32); bSf = Buf("Sf")
    Sfb = P.tile([128, 512], BF16); bSfb = Buf("Sfb")
    xo = [P.tile([128, D], F32) for _ in range(2)]
    bxo = [Buf("xo%d" % i) for i in range(2)]
    bufs += bxo
    tmpo = P.tile([128, 512], F32); btmpo = Buf()
    P.memset("vector", Sf, 0.0, [bSf])
    P.memset("vector", Sfb, 0.0, [bSfb])

    def loadB(c):
        i = c % 2
        P.dma("sync", hTh[i][:, :, 0:158], hT0[:, :, PADC + c * 128 - 15:PADC + c * 128 + 143], bhTh[i],
              reads=[k.b_hT0], writes=[bhTh[i]])
        P.dma("sync", sbt[i], T["sb_all"][c], bsbt[i], reads=[k.b_sb], writes=[bsbt[i]])
        P.dma("sync", cs[i], T["rot_tab"][c], bcs[i], writes=[bcs[i]])
        P.dma("sync", xo[i], T["x"][c * 128:(c + 1) * 128, :], bxo[i], writes=[bxo[i]])
    def stA1(c):
        H, bH = hTh[c % 2], bhTh[c % 2]
        for half in range(2):
            for jj in range(2):
                j = half * 2 + jj
                for kt in range(8):
                    P.mm(ps[half][:, jj * 160:jj * 160 + 158], Win[:, kt, j * 128:(j + 1) * 128], H[:, kt, 0:158],
                         kt == 0, kt == 7, [bWin[kt], bH], [pb[half]])
                for kt in range(8):
                    P.mm(ps[2 + half][:, jj * 160:jj * 160 + 158], Win[:, kt, 512 + j * 128:512 + (j + 1) * 128],
                         H[:, kt, 0:158], kt == 0, kt == 7, [bWin[kt], bH], [pb[2 + half]])

    def stA2(c):
        for half in range(2):
            gv = ps[2 + half][:, 0:320].rearrange("p (j t) -> p j t", j=2)[:, :, 0:158]
            vv = ps[half][:, 0:320].rearrange("p (j t) -> p j t", j=2)[:, :, 0:158]
            P.act(sig[:, half * 2:half * 2 + 2, :], gv, AF.Sigmoid, [pb[2 + half]], [bsig])
            P.tt(abf[:, half * 2:half * 2 + 2, :], vv, sig[:, half * 2:half * 2 + 2, :], ALU.mult,
                 [pb[half], bsig], [babf])

    def stA3(c):
        for j in range(4):
            for w in range(31):
                P.mm(ps[0][:, j * 128:(j + 1) * 128], DG[:, j * 31 + w, :], abf[:, j, w:w + 128], w == 0, w == 30,
                     [bDG, babf], [pb[0]])

    def stA4(c):
        for j in range(4):
            P.act(conv[:, j, :], ps[0][:, j * 128:(j + 1) * 128], AF.Identity, [pb[0], bcp], [bconv],
                  bias=cpar[:, 0, j:j + 1])
        P.act(csq, conv, AF.Square, [bconv], [bcsq])

    def stA5(c):
        for j in range(4):
            P.mm(ps[1][:, 0:128], onesf, conv[:, j, :], j == 0, j == 3, [bones, bconv], [pb[1]])
        for j in range(4):
            P.mm(ps[1][:, 128:256], onesf, csq[:, j, :], j == 0, j == 3, [bones, bcsq], [pb[1]])

    def stA6(c):
        P.cp("vector", st[:, 0, :], ps[1][:, 0:128], [pb[1]], [bst])
        P.tt(st[:, 1, :], st[:, 0, :], st[:, 0, :], ALU.mult, [bst], [bst])
        P.tt(st[:, 1, :], ps[1][:, 128:256], st[:, 1, :], ALU.subtract, [pb[1], bst], [bst])
        P.act(st[:, 2, :], st[:, 1, :], AF.Sqrt, [bst], [bst], bias=EPS)
        P.recip(st[:, 3, :], st[:, 2, :], [bst], [bst])
        P.tt(conv, conv, bcast_mid(st[:, 0, :], 4), ALU.subtract, [bconv, bst], [bconv])
        P.tt(conv, conv, bcast_mid(st[:, 3, :], 4), ALU.mult, [bconv, bst], [bconv])

    def stA7(c):
        for j in range(4):
            P.act(aT[:, j, :], conv[:, j, :], AF.Silu, [bconv, bcp], [baT], scale=cpar[:, 1, j:j + 1],
                  bias=cpar[:, 2, j:j + 1])

    def stB1(c):
        H, bH = hTh[c % 2], bhTh[c % 2]
        for blk_, col0 in enumerate((1024, 1536, 2048, 2560)):
            bank = 4 + blk_
            for kt in range(8):
                P.mm(ps[bank], H[:, kt, 15:143], Win[:, kt, col0:col0 + 512], kt == 0, kt == 7, [bH, bWin[kt]],
                     [pb[bank]])

    def stB2(c):
        i = c % 2
        rotary(ps[4], qk[:, 0, :], 4, pb[4], bqk, cs[i], bcs[i])
        rotary(ps[5], qk[:, 1, :], 4, pb[5], bqk, cs[i], bcs[i])
        P.cp("scalar", vbf[:, 0, :], ps[6], [pb[6]], [bvbf])
        P.tt(vbf[:, 1, :].rearrange("p (h e) -> p h e", h=4), ps[6].rearrange("p (h e) -> p h e", h=4),
             bcast_last(wkf, 128), ALU.mult, [pb[6], brt], [bvbf])
        P.act(sg, ps[7], AF.Silu, [pb[7]], [bsg])

    def stB3(c):
        pTq = ps[4].bitcast(BF16)
        pTk = ps[5].bitcast(BF16)
        for h in range(4):
            P.tr(pTq[:, h * 128:(h + 1) * 128], qk[:, 0, h * 128:(h + 1) * 128], k.ident, [bqk, k.bconst], [pb[4]])
        for h in range(4):
            P.tr(pTk[:, h * 128:(h + 1) * 128], qk[:, 1, h * 128:(h + 1) * 128], k.ident, [bqk, k.bconst], [pb[5]])

    def stB4(c):
        pTq = ps[4].bitcast(BF16)
        pTk = ps[5].bitcast(BF16)
        P.cp("scalar", qT[:, 0, :], pTq[:, 0:512], [pb[4]], [bqT])
        P.tt(qT[:, 1, :], pTq[:, 0:512], WQF, ALU.mult, [pb[4], brt], [bqT])
        P.tt(qT[:, 2, :], pTq[:, 0:512], WQB, ALU.mult, [pb[4], brt], [bqT])
        P.cp("scalar", qT[:, 3, :], pTk[:, 0:512], [pb[5]], [bqT])

    def stB5(c):
        for h in range(4):
            P.mm(ps[2][:, h * 128:(h + 1) * 128], qT[:, 3, h * 128:(h + 1) * 128], qT[:, 0, h * 128:(h + 1) * 128],
                 True, True, [bqT], [pb[2]])

    def stB6(c):
        P.tt(sT, ps[2], DMT, ALU.mult, [pb[2], brt], [bsT])

    def stB7(c):
        i = c % 2
        for h in range(4):
            hs_ = slice(h * 128, (h + 1) * 128)
            P.mm(ps[7][:, hs_], sT[:, hs_], vbf[:, 0, hs_], True, False, [bsT, bvbf], [pb[7]])
            P.mm(ps[7][:, hs_], qT[:, 1, hs_], Sfb[:, hs_], False, False, [bqT, bSfb], [pb[7]])
            P.mm(ps[7][:, hs_], qT[:, 2, hs_], sbt[i][:, hs_], False, True, [bqT, bsbt[i]], [pb[7]])
        for h in range(4):
            hs_ = slice(h * 128, (h + 1) * 128)
            P.mm(ps[6][:, hs_], qk[:, 1, hs_], vbf[:, 1, hs_], True, True, [bqk, bvbf], [pb[6]])

    def stB8(c):
        P.tt(Sf, Sf, GL, ALU.mult, [bSf, brt], [bSf])
        P.tt(Sf, Sf, ps[6], ALU.add, [bSf, pb[6]], [bSf])
        P.cp("scalar", Sfb, Sf, [bSf], [bSfb])
        for h in range(4):
            P.act(W["junk"][:, 0:128], ps[7][:, h * 128:(h + 1) * 128], AF.Square, [pb[7]], [Buf(), bhs],
                  accum_out=hs[:, h:h + 1])
        P.act(hs[:, 4:8], hs[:, 0:4], AF.Sqrt, [bhs], [bhs], scale=1.0 / 128.0, bias=EPS)
        P.recip(hs[:, 8:12], hs[:, 4:8], [bhs], [bhs])
        for h in range(4):
            hs_ = slice(h * 128, (h + 1) * 128)
            P.stt(rr[:, hs_], ps[7][:, hs_], hs[:, 8 + h:9 + h], sg[:, hs_], ALU.mult, ALU.mult,
                  [pb[7], bhs, bsg], [brr])

    def stB9(c):
        transpose8(P, k, rr, brr, aT[:, 4:8, :], baT, 6, n=4)

    def stC(c):
        i = c % 2
        for hf in range(2):
            for kt in range(8):
                P.mm(ps[4 + hf], aT[:, kt, :], Wout[:, kt, hf * 512:(hf + 1) * 512], kt == 0, kt == 7,
                     [baT, bWout[kt]], [pb[4 + hf]])
        for hf in range(2):
            P.tt(tmpo, ps[4 + hf], m[:, 2, hf * 512:(hf + 1) * 512], ALU.mult, [pb[4 + hf], bm], [btmpo])
            P.tt(xo[i][:, hf * 512:(hf + 1) * 512], tmpo, xo[i][:, hf * 512:(hf + 1) * 512], ALU.add,
                 [btmpo, bxo[i]], [bxo[i]])
        P.dma("gpsimd", T["x1"][c * 128:(c + 1) * 128, :], xo[i], bxo[i], reads=[bxo[i]], writes=[k.b_x1])

    loadB(0)
    loadB(1)
    stA1(0)
    for c in range(NCH):
        for stg_ in (stB1, stA2, stB2, stA3, stB3, stB4, stA4, stB5, stA5, stB6, stA6, stB7):
            stg_(c)
        if c + 1 < NCH:
            stA1(c + 1)
        for stg_ in (stB8, stA7, stB9, stC):
            stg_(c)
        if c + 2 < NCH:
            loadB(c + 2)
    P.barrier(bufs)
    P.release(bufs)


def phase_s5(P, k, T, do_pre=True):
    ps, pb = k.ps, k.pb
    if do_pre:
        P.reset_arena(k.keep)
        bufs = []
        m, bm = load_mods(P, k, T, 1, 0, bufs)
        W = norm_work(P, bufs)
        xt = [P.tile([128, D], F32) for _ in range(2)]
        bxt = [Buf("xt%d" % i) for i in range(2)]
        bufs += bxt
        hbs = [P.tile([128, D], BF16) for _ in range(2)]
        bhbs = [Buf("hbs%d" % i) for i in range(2)]
        bufs += bhbs

        def ldx(c):
            P.dma("sync", xt[c % 2], T["x2"][c * 128:(c + 1) * 128, :], bxt[c % 2], reads=[k.b_x2], writes=[bxt[c % 2]])
        ldx(0)
        for c in range(NCH):
            if c + 1 < NCH:
                ldx(c + 1)
            X, bX = xt[c % 2], bxt[c % 2]
            W.flip()
            rms_rstd(P, X, W["ss"], W["bss"], W["junk"], Buf(), [bX])
            P.stt(W["tn"], X, W["ss"][:, 2:3], m[:, 1, :], ALU.mult, ALU.mult, [bX, W["bss"], bm], [W["btn"]])
            P.tt(hbs[c % 2], W["tn"], m[:, 0, :], ALU.add, [W["btn"], bm], [bhbs[c % 2]])
            P.dma("gpsimd", T["h1"][c * 128:(c + 1) * 128, :], hbs[c % 2], bhbs[c % 2], reads=[bhbs[c % 2]],
                  writes=[k.b_h1])
        P.barrier(bufs)
        P.release(bufs)

    P.reset_arena(k.keep)
    bufs = []
    NG = 128
    bp = Buf("s5prep")
    bld = [Buf("s5ld%d" % i) for i in range(6)]
    bufs += bld
    sg = P.tile([128, 2], F32)
    P.memset("gpsimd", sg[0:64, 0:1], -1.0, [bp])
    P.memset("gpsimd", sg[64:128, 0:1], 1.0, [bp])
    P.memset("gpsimd", sg[0:64, 1:2], 1.0, [bp])
    P.memset("gpsimd", sg[64:128, 1:2], -1.0, [bp])
    lr = P.tile([128, NG], F32)
    li = P.tile([128, NG], F32)
    dtv = P.tile([128, NG], F32)
    for hlf in range(2):
        P.dma("sync", lr[hlf * 64:(hlf + 1) * 64, :], T["s5_lam_re"][0].rearrange("d g p -> p (d g)"), bld[0],
              writes=[bld[0]], allow_slow_non_contiguous=True)
        P.dma("sync", li[hlf * 64:(hlf + 1) * 64, :], T["s5_lam_im"][0].rearrange("d g p -> p (d g)"), bld[1],
              writes=[bld[1]], allow_slow_non_contiguous=True)
    P.dma("sync", dtv, T["s5_log_step"][0].rearrange("d g -> (d g)").partition_broadcast(128), bld[2],
          writes=[bld[2]])
    X1b = P.tile_top([128, NG, 16], F32)
    X2b = P.tile_top([128, NG, 16], F32)
    X1c = P.tile([128, NG, 16], F32)
    X2c = P.tile([128, NG, 16], F32)
    bre = T["s5_b_re"][0].rearrange("d g p c -> p (d g) c")
    bim = T["s5_b_im"][0].rearrange("d g p c -> p (d g) c")
    P.dma("sync", X1b[0:64], bre, bld[3], writes=[bld[3]], allow_slow_non_contiguous=True)
    P.dma("sync", X1b[64:128], bim, bld[3], writes=[bld[3]], allow_slow_non_contiguous=True)
    P.dma("sync", X2b[0:64], bim, bld[3], writes=[bld[3]], allow_slow_non_contiguous=True)
    P.dma("sync", X2b[64:128], bre, bld[3], writes=[bld[3]], allow_slow_non_contiguous=True)
    CRI = P.tile_top([128, 16, 2, 64], F32)
    cre = T["s5_c_re"][0].rearrange("d g c p -> (d g c) p").rearrange("(t r) p -> r t p", r=128)
    cim = T["s5_c_im"][0].rearrange("d g c p -> (d g c) p").rearrange("(t r) p -> r t p", r=128)
    P.dma("sync", CRI[:, :, 0, :], cre, bld[4], writes=[bld[4]])
    P.dma("sync", CRI[:, :, 1, :], cim, bld[4], writes=[bld[4]])
    CIR = P.tile_top([128, 16, 2, 64], F32)
    P.dma("sync", CIR[:, :, 0, :], cim, bld[5], writes=[bld[5]])
    P.dma("sync", CIR[:, :, 1, :], cre, bld[5], writes=[bld[5]])
    for t in range(16):
        P.tr(ps[6][:, 0:128], CRI[:, t, :, :], k.identf, [bld[4], k.bconst], [pb[6]])
        P.cp("vector", X1c[:, t * 8:(t + 1) * 8, :], ps[6][:, 0:128].rearrange("p (g c) -> p g c", g=8),
             [pb[6]], [bp])
        P.tr(ps[7][:, 0:128], CIR[:, t, :, :], k.identf, [bld[5], k.bconst], [pb[7]])
        P.cp("vector", X2c[:, t * 8:(t + 1) * 8, :], ps[7][:, 0:128].rearrange("p (g c) -> p g c", g=8),
             [pb[7]], [bp])
    ldall = list(bld)
    t1 = P.tile([128, NG], F32)
    t2 = P.tile([128, NG], F32)
    t3 = P.tile([128, NG], F32)
    ti = P.tile([128, NG], I32)
    cosv = P.tile([128, NG], F32)
    sinv = P.tile([128, NG], F32)
    mag = P.tile([128, NG], F32)
    R_ = [bp] + ldall
    Wp = [bp]
    P.ts(lr, lr, -1e-4, None, ALU.min, None, R_, Wp)
    P.act(dtv, dtv, AF.Exp, R_, Wp)
    P.tt(t1, lr, dtv, ALU.mult, R_, Wp)
    P.act(mag, t1, AF.Exp, R_, Wp)
    P.tt(t1, li, dtv, ALU.mult, R_, Wp)
    P.ts(t1, t1, 1.0 / (2.0 * math.pi), None, ALU.mult, None, R_, Wp)

    def sin_turns(dst, off):
        P.ts(t2, t1, off, None, ALU.add, None, R_, Wp)
        P.cp("vector", ti, t2, R_, Wp)
        P.cp("vector", t3, ti, R_, Wp)
        P.tt(t2, t2, t3, ALU.subtract, R_, Wp)
        P.ts(t3, t2, 0.5, None, ALU.is_gt, None, R_, Wp)
        P.tt(t2, t2, t3, ALU.subtract, R_, Wp)
        P.ts(t3, t2, -0.5, None, ALU.is_lt, None, R_, Wp)
        P.tt(t2, t2, t3, ALU.add, R_, Wp)
        P.ts(t2, t2, 0.4999995, -0.4999995, ALU.min, ALU.max, R_, Wp)
        P.act(dst, t2, AF.Sin, R_, Wp, scale=2.0 * math.pi)
    sin_turns(sinv, 0.0)
    sin_turns(cosv, 0.25)
    NP_ = 23
    PWr = P.tile([128, NP_, NG], F32)
    PWi = P.tile_top([128, NP_, NG], F32)
    ar, ai = PWr[:, 8, :], PWi[:, 8, :]
    P.tt(ar, mag, cosv, ALU.mult, R_, Wp)
    P.tt(ai, mag, sinv, ALU.mult, R_, Wp)
    P.memset("vector", PWr[:, 7, :], 1.0, Wp)
    P.memset("vector", PWi[:, 7, :], 0.0, Wp)
    fr = P.tile([128, NG], F32)
    fi = P.tile([128, NG], F32)
    P.tt(t1, lr, lr, ALU.mult, R_, Wp)
    P.tt(t2, li, li, ALU.mult, R_, Wp)
    P.tt(t1, t1, t2, ALU.add, R_, Wp)
    P.recip(t1, t1, R_, Wp)
    P.ts(t2, ar, -1.0, None, ALU.add, None, R_, Wp)
    P.tt(t3, t2, lr, ALU.mult, R_, Wp)
    P.tt(fr, ai, li, ALU.mult, R_, Wp)
    P.tt(fr, fr, t3, ALU.add, R_, Wp)
    P.tt(fr, fr, t1, ALU.mult, R_, Wp)
    P.tt(t3, ai, lr, ALU.mult, R_, Wp)
    P.tt(fi, t2, li, ALU.mult, R_, Wp)
    P.tt(fi, t3, fi, ALU.subtract, R_, Wp)
    P.tt(fi, fi, t1, ALU.mult, R_, Wp)
    P.tt(t1, mag, mag, ALU.mult, R_, Wp)
    P.recip(t1, t1, R_, Wp)
    P.tt(PWr[:, 6, :], ar, t1, ALU.mult, R_, Wp)
    P.tt(t2, ai, t1, ALU.mult, R_, Wp)
    P.ts(PWi[:, 6, :], t2, -1.0, None, ALU.mult, None, R_, Wp)

    def cmul(or_, oi_, xr, xi, yr, yi):
        P.tt(t1, xr, yr, ALU.mult, R_, Wp)
        P.tt(t2, xi, yi, ALU.mult, R_, Wp)
        P.tt(t3, xr, yi, ALU.mult, R_, Wp)
        P.tt(cosv, xi, yr, ALU.mult, R_, Wp)
        P.tt(or_, t1, t2, ALU.subtract, R_, Wp)
        P.tt(oi_, t3, cosv, ALU.add, R_, Wp)
    for n in range(2, 16):
        cmul(PWr[:, 7 + n, :], PWi[:, 7 + n, :], PWr[:, 6 + n, :], PWi[:, 6 + n, :], ar, ai)
    for n in range(2, 8):
        cmul(PWr[:, 7 - n, :], PWi[:, 7 - n, :], PWr[:, 8 - n, :], PWi[:, 8 - n, :], PWr[:, 6, :], PWi[:, 6, :])
    KSr = P.tile([128, 10, NG], F32)
    KSi = P.tile([128, 10, NG], F32)
    KS2 = P.tile([128, 10, NG], F32)
    P.cp("vector", KSr[:, 0, :], PWr[:, 15, :], R_, Wp)
    P.cp("vector", KSi[:, 0, :], PWi[:, 15, :], R_, Wp)
    for kk in range(1, 10):
        cmul(KSr[:, kk, :], KSi[:, kk, :], KSr[:, kk - 1, :], KSi[:, kk - 1, :], KSr[:, kk - 1, :], KSi[:, kk - 1, :])
    P.ts(KS2, KSi, sg[:, 1:2], None, ALU.mult, None, R_, Wp)
    C2q = P.tile([128, 8, NG], F32)
    C1p = P.tile([128, 16, NG], F32)
    C2p = P.tile([128, 16, NG], F32)
    P.ts(C2q, PWi[:, 0:8, :], sg[:, 0:1], None, ALU.mult, None, R_, Wp)
    P.ts(C1p, PWr[:, 7:23, :], sg[:, 1:2], None, ALU.mult, None, R_, Wp)
    P.ts(C2p, PWi[:, 7:23, :], -1.0, None, ALU.mult, None, R_, Wp)
    X1B = P.tile([128, NG, 16], F32)
    X2B = P.tile([128, NG, 16], F32)
    tb1 = P.tile_top([128, NG, 16], F32)
    P.ts(t1, fi, sg[:, 0:1], None, ALU.mult, None, R_, Wp)
    P.ts(t2, fi, sg[:, 1:2], None, ALU.mult, None, R_, Wp)
    P.tt(X1B, X1b, bcast_last(fr, 16), ALU.mult, R_, Wp)
    P.tt(tb1, X2b, bcast_last(t1, 16), ALU.mult, R_, Wp)
    P.tt(X1B, X1B, tb1, ALU.add, R_, Wp)
    P.tt(X2B, X2b, bcast_last(fr, 16), ALU.mult, R_, Wp)
    P.tt(tb1, X1b, bcast_last(t2, 16), ALU.mult, R_, Wp)
    P.tt(X2B, X2B, tb1, ALU.add, R_, Wp)
    msk = P.tile([128, 2, 128], F32)
    bmsk = Buf("msk"); bufs.append(bmsk)
    P.dma("sync", msk, T["s5_mask"].rearrange("d p n -> p d n"), bmsk, writes=[bmsk])

    P.barrier(bufs)
    P.atop = P.asize
    Qm = [P.tile([128, 2, 8, 128], BF16) for _ in range(2)]
    Pm = [P.tile([128, 2, 8, 128], BF16) for _ in range(2)]
    Po = [P.tile([128, 2, 8, 128], BF16) for _ in range(2)]
    bgen = [Buf("gen0"), Buf("gen1")]
    g1 = P.tile([128, 8, 16], F32); g2 = P.tile([128, 8, 16], F32)
    bg12 = Buf("g12")
    hcm = P.tile([128, 8, 8, 128], BF16)
    bhcm = Buf("hcm"); bufs.append(bhcm)
    hcg = P.tile([128, 8, 8, 128], BF16)
    bhcg = Buf("hcg")
    ycm = P.tile([128, 8, 8, 128], F32)
    bycm = Buf("ycm"); bufs.append(bycm)
    U = [P.tile([128, 1024], BF16) for _ in range(2)]; bU = [Buf("U0"), Buf("U1")]
    TT = [[P.tile([128, 128], BF16) for _ in range(2)] for _ in range(2)]
    bTT = [[Buf() for _ in range(2)] for _ in range(2)]
    WT = [[P.tile([128, 128], BF16) for _ in range(2)] for _ in range(2)]
    bWT = [[Buf() for _ in range(2)] for _ in range(2)]
    Mk = [[P.tile([128, 10, 128], BF16) for _ in range(2)] for _ in range(2)]
    bMk = [[Buf() for _ in range(2)] for _ in range(2)]
    NTJ = 4
    tJ = [P.tile([128, 128], F32) for _ in range(NTJ)]; btJ = [Buf() for _ in range(NTJ)]
    Ib = [P.tile([128, 1026], BF16) for _ in range(2)]; bIb = [Buf("Ib0"), Buf("Ib1")]
    for d in range(2):
        P.memset("gpsimd", Ib[d], 0.0, [bIb[d]])
    chain_ps = [k.psbig[:, (2 + 2 * d) * 512:(4 + 2 * d) * 512] for d in range(2)]
    chain_pb = [[pb[2 + 2 * d], pb[3 + 2 * d]] for d in range(2)]
    cp_eng = ["scalar", "vector"]
    tjc = [0]

    def batch_thunks(gb):
        q = gb % 2
        th = []
        gen_eng = "vector" if gb == 0 else "gpsimd"

        def ld():
            for blk in range(8):
                P.dma("sync", hcm[:, blk, :, :],
                      T["h1"][blk * 1024:(blk + 1) * 1024, gb * 128:(gb + 1) * 128].rearrange("(c i) f -> c i f", i=8),
                      bhcm, reads=[k.b_h1], writes=[bhcm])
        th.append(ld)
        for blk in range(8):
            th.append(lambda blk=blk: P.cp("gpsimd", hcg[:, blk, :, :].rearrange("p g (i c) -> p i g c", i=8),
                                           hcm[:, blk, :, :].rearrange("p i (g c) -> p i g c", g=8), [bhcm], [bhcg]))
        for d in range(2):
            gsl = slice(d * 64 + gb * 8, d * 64 + gb * 8 + 8)
            for jj in range(8):
                slot = jj if d == 0 else 7 - jj
                ssl = slice(slot * 16, slot * 16 + 16)

                def gq(d=d, gsl=gsl, jj=jj, ssl=ssl):
                    P.tt(g1, X1B[:, gsl, :], bcast_last(PWr[:, 7 - jj, gsl], 16), ALU.mult, R_, [bg12])
                    P.tt(g2, X2B[:, gsl, :], bcast_last(C2q[:, 7 - jj, gsl], 16), ALU.mult, R_, [bg12])
                    P.tt(Qm[q][:, d, :, ssl], g1, g2, ALU.add, [bg12], [bgen[q]])

                def gp(d=d, gsl=gsl, jj=jj, ssl=ssl):
                    P.tt(g1, X1c[:, gsl, :], bcast_last(C1p[:, jj, gsl], 16), ALU.mult, R_, [bg12], e=gen_eng)
                    P.tt(g2, X2c[:, gsl, :], bcast_last(C2p[:, jj, gsl], 16), ALU.mult, R_, [bg12], e=gen_eng)
                    P.tt(Pm[q][:, d, :, ssl], g1, g2, ALU.add, [bg12], [bgen[q]], e=gen_eng)

                def go(d=d, gsl=gsl, jj=jj, ssl=ssl):
                    P.tt(g1, X1c[:, gsl, :], bcast_last(C1p[:, jj + 8, gsl], 16), ALU.mult, R_, [bg12], e=gen_eng)
                    P.tt(g2, X2c[:, gsl, :], bcast_last(C2p[:, jj + 8, gsl], 16), ALU.mult, R_, [bg12], e=gen_eng)
                    P.tt(Po[q][:, d, :, ssl], g1, g2, ALU.add, [bg12], [bgen[q]], e=gen_eng)
                th += [gq, gp, go]
        return th

    def group_thunks(g):
        gb, gl = divmod(g, 8)
        q = gb % 2
        st_ = g % 2
        th = []
        pU = ps[0].bitcast(BF16)
        for blk in range(8):
            th.append(lambda blk=blk: P.tr(pU[:, blk * 128:(blk + 1) * 128], hcg[:, blk, gl, :], k.ident,
                                           [bhcg, k.bconst], [pb[0]]))
        th.append(lambda: P.cp("scalar", U[st_], pU, [pb[0]], [bU[st_]]))
        pW = ps[1].bitcast(BF16)
        for d in range(2):
            gd = d * 64 + g

            def tw(d=d):
                P.mm(ps[1][:, 0:128], Qm[q][:, d, gl, :], Pm[q][:, d, gl, :], True, True, [bgen[q]], [pb[1]])
                P.tt(TT[st_][d], ps[1][:, 0:128], msk[:, d, :], ALU.mult, [pb[1], bmsk], [bTT[st_][d]])
                P.tr(pW[:, 512:640], Qm[q][:, d, gl, :], k.ident, [bgen[q], k.bconst], [pb[1]])
                P.cp("scalar", WT[st_][d], pW[:, 512:640], [pb[1]], [bWT[st_][d]])
            th.append(tw)
            for kk in range(10):
                def mk(d=d, kk=kk, gd=gd):
                    j_ = tjc[0] % NTJ
                    tjc[0] += 1
                    P.act(tJ[j_], k.jswap, AF.Copy, [k.bconst] + R_, [btJ[j_]], scale=KS2[:, kk, gd:gd + 1])
                    P.stt(Mk[st_][d][:, kk, :], k.identf, KSr[:, kk, gd:gd + 1], tJ[j_], ALU.mult, ALU.add,
                          [k.bconst, btJ[j_]] + R_, [bMk[st_][d]])
                th.append(mk)
        return th

    pending = []

    def pop(n):
        for _ in range(min(n, len(pending))):
            pending.pop(0)()

    pending += batch_thunks(0)
    pending += group_thunks(0)
    for g in range(64):
        gb, gl = divmod(g, 8)
        q = gb % 2
        st_ = g % 2
        pop(len(pending))
        if g + 1 < 64:
            if gl == 7:
                pending += batch_thunks(gb + 1)
            pending += group_thunks(g + 1)
        for d in range(2):
            for hf in range(2):
                P.mm(chain_ps[d][:, hf * 512:(hf + 1) * 512], WT[st_][d], U[st_][:, hf * 512:(hf + 1) * 512],
                     True, True, [bWT[st_][d], bU[st_]], [chain_pb[d][hf]])
        for d in range(2):
            P.cp(cp_eng[d], Ib[d][:, 1:1025], chain_ps[d], chain_pb[d], [bIb[d]])
        pop(4)
        for kk in range(10):
            s_ = 1 << kk
            for d in range(2):
                lo, hi = (s_, 1024) if d == 0 else (0, 1024 - s_)
                for (a_, b_) in ((0, 512), (512, 1024)):
                    l2, h2 = max(lo, a_), min(hi, b_)
                    if l2 >= h2:
                        continue
                    src0 = 1 + l2 - s_ if d == 0 else 1 + l2 + s_
                    P.mm(chain_ps[d][:, l2:h2], Mk[st_][d][:, kk, :], Ib[d][:, src0:src0 + (h2 - l2)],
                         False, True, [bMk[st_][d], bIb[d]], [chain_pb[d][a_ // 512]])
            for d in range(2):
                lo, hi = (s_, 1024) if d == 0 else (0, 1024 - s_)
                P.cp(cp_eng[d], Ib[d][:, 1 + lo:1 + hi], chain_ps[d][:, lo:hi], chain_pb[d], [bIb[d]])
            pop(5)
        for blk in range(8):
            bank = 6 + blk // 4
            osl = slice((blk % 4) * 128, (blk % 4) * 128 + 128)
            csl = slice(blk * 128, (blk + 1) * 128)
            P.mm(ps[bank][:, osl], U[st_][:, csl], TT[st_][0], True, False, [bU[st_], bTT[st_][0]], [pb[bank]])
            P.mm(ps[bank][:, osl], Ib[0][:, blk * 128:blk * 128 + 128], Po[q][:, 0, gl, :], False, False,
                 [bIb[0], bgen[q]], [pb[bank]])
            P.mm(ps[bank][:, osl], U[st_][:, csl], TT[st_][1], False, False, [bU[st_], bTT[st_][1]], [pb[bank]])
            P.mm(ps[bank][:, osl], Ib[1][:, blk * 128 + 2:blk * 128 + 130], Po[q][:, 1, gl, :], False, True,
                 [bIb[1], bgen[q]], [pb[bank]])
        for hb_ in range(2):
            P.cp("vector" if hb_ == 0 else "scalar",
                 ycm[:, hb_ * 4:hb_ * 4 + 4, :, gl * 16:(gl + 1) * 16],
                 ps[6 + hb_].rearrange("p (b i c) -> p b i c", b=4, i=8), [pb[6 + hb_]], [bycm])
        pop(6)
        if gl == 7:
            for blk in range(8):
                P.dma("gpsimd",
                      T["ys5"][blk * 1024:(blk + 1) * 1024, gb * 128:(gb + 1) * 128].rearrange("(c i) f -> c i f", i=8),
                      ycm[:, blk, :, :], bycm, reads=[bycm], writes=[k.b_ys5])
    P.barrier(bufs)
    P.release(bufs)

    P.reset_arena(k.keep)
    bufs = []
    m, bm = load_mods(P, k, T, 1, 0, bufs)
    W = norm_work(P, bufs)
    Wa = P.tile([128, 8, 1024], BF16); bWa = [Buf() for _ in range(8)]
    Wb = P.tile([128, 8, 1024], BF16); bWb = [Buf() for _ in range(8)]
    stg = [P.tile([128, 1024], F32) for _ in range(2)]
    bstg = [Buf("stg%d" % i) for i in range(2)]
    bufs += bstg
    si = [0]
    for kt in range(8):
        s_ = si[0] % 2
        si[0] += 1
        P.dma("sync", stg[s_], T["w_glu_a"][0, kt * 128:(kt + 1) * 128, :], bstg[s_], writes=[bstg[s_]])
        P.tt(Wa[:, kt, :], stg[s_], m[:, 2, :], ALU.mult, [bstg[s_], bm], [bWa[kt]], e="gpsimd")
    load_w_bf16(P, Wb, bWb, T["w_glu_b"][0], 8, 1024, stg, bstg, si)
    dsk = P.tile([128, D], F32); bdsk = Buf("dsk"); bufs.append(bdsk)
    P.dma("sync", dsk, T["s5_d"][0].partition_broadcast(128), bdsk, writes=[bdsk])
    Gd = P.tile([128, D], F32)
    Sd = P.tile([128, D], F32)
    bgs = Buf("GdSd")
    P.tt(Gd, m[:, 1, :], dsk, ALU.mult, [bm, bdsk], [bgs])
    P.tt(Sd, m[:, 0, :], dsk, ALU.mult, [bm, bdsk], [bgs])
    NXP = 4
    xt = [P.tile([128, D], F32) for _ in range(NXP)]
    bxt = [Buf("xt%d" % i) for i in range(NXP)]
    yt = [P.tile([128, D], F32) for _ in range(NXP)]
    byt = [Buf("yt%d" % i) for i in range(NXP)]
    bufs += bxt + byt
    zb = [P.tile([128, D], BF16) for _ in range(2)]; bzb = [Buf() for _ in range(2)]
    zT = [P.tile([128, 8, 128], BF16) for _ in range(2)]; bzT = [Buf() for _ in range(2)]
    sgm = [P.tile([128, D], F32) for _ in range(2)]; bsgm = [Buf() for _ in range(2)]
    banks = [(1, (1, 2), (3, 4)), (5, (5, 6), (7, 0))]

    def ldp(c):
        P.dma("sync", xt[c % NXP], T["x2"][c * 128:(c + 1) * 128, :], bxt[c % NXP], reads=[k.b_x2],
              writes=[bxt[c % NXP]])
        P.dma("sync", yt[c % NXP], T["ys5"][c * 128:(c + 1) * 128, :], byt[c % NXP], reads=[k.b_ys5],
              writes=[byt[c % NXP]])

    def front(c):
        i = c % 2
        X, bX, Y, bY = xt[c % NXP], bxt[c % NXP], yt[c % NXP], byt[c % NXP]
        W.flip()
        rms_rstd(P, X, W["ss"], W["bss"], W["junk"], Buf(), [bX])
        P.stt(W["tn"], X, W["ss"][:, 2:3], Gd, ALU.mult, ALU.mult, [bX, W["bss"], bgs], [W["btn"]])
        P.tt(Y, Y, W["tn"], ALU.add, [bY, W["btn"]], [bY])
        P.tt(Y, Y, Sd, ALU.add, [bY, bgs], [bY])
        P.act(zb[i], Y, AF.Gelu_apprx_tanh, [bY], [bzb[i]])
        transpose8(P, k, zb[i], bzb[i], zT[i], bzT[i], banks[i][0])

    def back(c):
        i = c % 2
        X, bX = xt[c % NXP], bxt[c % NXP]
        ba, bb = banks[i][1], banks[i][2]
        for hf in range(2):
            for kt in range(8):
                P.mm(ps[ba[hf]], zT[i][:, kt, :], Wa[:, kt, hf * 512:(hf + 1) * 512], kt == 0, kt == 7,
                     [bzT[i], bWa[kt]], [pb[ba[hf]]])
            for kt in range(8):
                P.mm(ps[bb[hf]], zT[i][:, kt, :], Wb[:, kt, hf * 512:(hf + 1) * 512], kt == 0, kt == 7,
                     [bzT[i], bWb[kt]], [pb[bb[hf]]])
        for hf in range(2):
            hsl = slice(hf * 512, (hf + 1) * 512)
            P.act(sgm[i][:, hsl], ps[bb[hf]], AF.Sigmoid, [pb[bb[hf]]], [bsgm[i]])
            P.tt(sgm[i][:, hsl], ps[ba[hf]], sgm[i][:, hsl], ALU.mult, [pb[ba[hf]], bsgm[i]], [bsgm[i]])
            P.tt(X[:, hsl], X[:, hsl], sgm[i][:, hsl], ALU.add, [bX, bsgm[i]], [bX])
        P.dma("gpsimd", T["x3"][c * 128:(c + 1) * 128, :], X, bX, reads=[bX], writes=[k.b_x3])

    ldp(0)
    ldp(1)
    ldp(2)
    front(0)
    for c in range(NCH):
        if c + 1 < NCH:
            front(c + 1)
        back(c)
        if c + 3 < NCH:
            ldp(c + 3)
    P.barrier(bufs)
    P.release(bufs)


def host_tables():
    L = 128
    nh = 4
    dh = 128
    inv = (10000.0 ** (-np.arange(0, dh, 2, dtype=np.float32) / dh)).astype(np.float32)
    ang = (np.arange(S, dtype=np.float32)[:, None] * inv[None, :]).astype(np.float32)
    rot = np.stack([np.cos(ang), np.sin(ang)], axis=1).astype(np.float32)
    rot_tab = rot.reshape(NCH, 128, 2, 64)
    log_g = np.log1p(-np.exp2(-5.0 - np.arange(nh, dtype=np.float32))).astype(np.float32)
    idx = np.arange(L, dtype=np.float32)
    sc = dh ** -0.5
    dist = np.abs(idx[:, None] - idx[None, :])
    dmat = np.exp(log_g[:, None, None] * dist).astype(np.float32)
    tab = np.zeros((5, 128, 512), np.float32)
    tab[0] = (dmat.transpose(2, 0, 1) * sc).reshape(128, 512)
    wqf = np.exp(log_g[:, None] * (idx + 1.0)[None, :]) * sc
    wqb = np.exp(log_g[:, None] * (L - idx)[None, :]) * sc
    tab[1] = np.broadcast_to(wqf.reshape(1, 512), (128, 512))
    tab[2] = np.broadcast_to(wqb.reshape(1, 512), (128, 512))
    gl = np.exp(log_g * L)
    tab[3] = np.broadcast_to(np.repeat(gl, 128).reshape(1, 512), (128, 512))
    tab[4, :, 0:4] = np.exp(log_g[None, :] * (L - 1.0 - idx)[:, None])
    tab[4, :, 4:8] = np.exp(log_g[None, :] * idx[:, None])
    ii = np.arange(128) // 16
    mask = np.stack([(ii[None, :] >= ii[:, None]), (ii[None, :] <= ii[:, None])]).astype(np.float32)
    return {"rot_tab": rot_tab.astype(np.float32), "ret_tab": tab.astype(np.float32), "s5_mask": mask}


IN_SHAPES = {
    "x": [S, D], "c": [1, D], "norm_g": [2, 2, D], "ada_w": [2, D, 6 * D], "ada_b": [2, 6 * D],
    "w_in": [1, D, 3072], "conv_w": [1, 31, 512], "conv_b": [1, 512], "cln_g": [1, 512], "cln_b": [1, 512],
    "w_out": [1, D, D], "s5_lam_re": [1, 2, 64, 64], "s5_lam_im": [1, 2, 64, 64], "s5_log_step": [1, 2, 64],
    "s5_b_re": [1, 2, 64, 64, 16], "s5_b_im": [1, 2, 64, 64, 16], "s5_c_re": [1, 2, 64, 16, 64],
    "s5_c_im": [1, 2, 64, 16, 64], "s5_d": [1, D], "w_glu_a": [1, D, D], "w_glu_b": [1, D, D],
    "w_fc1": [2, D, DFF], "w_fc2": [2, DFF, D], "norm_f": [D],
    "rot_tab": [NCH, 128, 2, 64], "ret_tab": [5, 128, 512], "s5_mask": [2, 128, 128],
}


def build(phases, debug=(), ext_in=()):
    nc = bass.Bass("TRN2", target_bir_lowering=False)
    T = {}
    for nm, shp in IN_SHAPES.items():
        T[nm] = nc.dram_tensor(nm, shp, F32, kind="ExternalInput").ap()

    def scratch(nm, shp, dt):
        kind = "ExternalOutput" if nm in debug else ("ExternalInput" if nm in ext_in else "Internal")
        T[nm] = nc.dram_tensor(nm, shp, dt, kind=kind).ap()
    scratch("modrows", [2, 6, D], F32)
    scratch("hT0", [128, 8, S + 2 * PADC], BF16)
    scratch("sb_all", [NCH, 128, 512], BF16)
    scratch("x1", [S, D], F32)
    scratch("x2", [S, D], F32)
    scratch("h1", [S, D], BF16)
    scratch("ys5", [S, D], F32)
    scratch("x3", [S, D], F32)
    T["out"] = nc.dram_tensor("out", [S, D], F32, kind="ExternalOutput").ap()
    with ExitStack() as es:
        P = Prog(nc, es)
        P.init_arena(204 * 1024)
        k = K()
        for nm in ["modrows", "hT0", "sb", "x1", "x2", "h1", "ys5", "x3", "out", "xin"]:
            setattr(k, "b_" + nm, Buf(nm))
        setup_consts(P, k)
        if "mod" in phases:
            phase_mod(P, k, T)
        if "l0" in phases:
            phase_l0(P, k, T)
        if "mlp0" in phases:
            phase_mlp(P, k, T, 0, T["x1"], k.b_x1, T["x2"], k.b_x2, False, h1_out=("s5" in phases))
        if "s5" in phases:
            phase_s5(P, k, T, do_pre=("mlp0" not in phases))
        if "mlp1" in phases:
            phase_mlp(P, k, T, 1, T["x3"], k.b_x3, T["out"], k.b_out, True)
        P.barrier([], skip=("sync",))
        P.emit()
        k.nops = {e: len(P.ops[e]) for e in ENGS}
    return nc, k


def make_in_map(inputs, b, tabs):
    m = {}
    for nm in IN_SHAPES:
        if nm in tabs:
            m[nm] = tabs[nm]
        elif nm == "x":
            m[nm] = np.ascontiguousarray(inputs["x"][b])
        elif nm == "c":
            m[nm] = np.ascontiguousarray(inputs["c"][b:b + 1])
        else:
            m[nm] = np.ascontiguousarray(np.asarray(inputs[nm], dtype=np.float32))
    return m


def kernel(**inputs):
    inputs = {kk: np.asarray(v) for kk, v in inputs.items()}
    tabs = host_tables()
    nc, _ = build(["mod", "l0", "mlp0", "s5", "mlp1"])
    in_maps = [make_in_map(inputs, b, tabs) for b in range(8)]
    res = run_bass_kernel_spmd(nc, in_maps, core_ids=list(range(8)))
    return np.stack([np.asarray(r["out"], dtype=np.float32) for r in res.results], axis=0)
```
